# Optimizing a Trainium2 kernel written in Bass

```python
import jax, jax.numpy as jnp
from jax import lax
import numpy as np

D_MODEL = 1024
BATCH = 8
SEQ = 2048
DEPTH = 1

MLA_HEADS = 8
MLA_NOPE = 64
MLA_ROPE = 32
MLA_V = 64
MLA_Q_LORA = 512
MLA_KV_LORA = 256
DIL_HEADS = 8
DIL_HEAD_DIM = 64
DIL_PATTERNS = ((128, 1), (512, 4), (2048, 16))
DIL_WIDTH = DIL_HEADS * DIL_HEAD_DIM
MIX_WIDTH = MLA_HEADS * MLA_V + DIL_WIDTH
IN_COLS = MLA_Q_LORA + MLA_KV_LORA + MLA_ROPE + 3 * DIL_WIDTH
SPLITS = (MLA_Q_LORA,
          MLA_Q_LORA + MLA_KV_LORA,
          MLA_Q_LORA + MLA_KV_LORA + MLA_ROPE,
          MLA_Q_LORA + MLA_KV_LORA + MLA_ROPE + DIL_WIDTH,
          MLA_Q_LORA + MLA_KV_LORA + MLA_ROPE + 2 * DIL_WIDTH)
D_FF = 2816
CONV_WIDTH = 3
ROPE_THETA = 10000.0
EPS = 1e-6
Q_BLOCK = 128
NEG_INF = -1e30

kernel_name = "hybrid_mla_dilated_convffn_adaln"


def rms_norm(x, g):
    xf = x.astype(jnp.float32)
    y = xf * lax.rsqrt(jnp.mean(xf * xf, axis=-1, keepdims=True) + EPS)
    return (y * g.astype(jnp.float32)).astype(x.dtype)


def rope(x, positions):
    d = x.shape[-1]
    half = d // 2
    inv_freq = jnp.power(ROPE_THETA, -2.0 * jnp.arange(half, dtype=jnp.float32) / d)
    ang = positions.astype(jnp.float32)[:, :, None, None] * inv_freq
    cos, sin = jnp.cos(ang), jnp.sin(ang)
    xf = x.astype(jnp.float32)
    x1, x2 = xf[..., :half], xf[..., half:]
    return jnp.concatenate([x1 * cos - x2 * sin, x1 * sin + x2 * cos], axis=-1).astype(x.dtype)


def causal_dense_attention(q, k, v, scale):
    B, S, H, Dk = q.shape
    Dv = v.shape[-1]
    nb = S // Q_BLOCK
    qb = q.reshape(B, nb, Q_BLOCK, H, Dk).transpose(1, 0, 2, 3, 4)
    kpos = jnp.arange(S)

    def one_block(args):
        qi, i = args
        s = jnp.einsum('bqhd,bkhd->bhqk', qi, k).astype(jnp.float32) * scale
        qpos = i * Q_BLOCK + jnp.arange(Q_BLOCK)
        mask = kpos[None, :] <= qpos[:, None]
        s = jnp.where(mask, s, NEG_INF)
        p = jax.nn.softmax(s, axis=-1).astype(v.dtype)
        return jnp.einsum('bhqk,bkhd->bqhd', p, v)

    out = lax.map(one_block, (qb, jnp.arange(nb)))
    return out.transpose(1, 0, 2, 3, 4).reshape(B, S, H, Dv)


def banded_causal_attention(q, k, v, span):
    N, L, H, D = q.shape
    blk = span
    nb = -(-L // blk)
    Lp = nb * blk
    pad = ((0, 0), (0, Lp - L), (0, 0), (0, 0))
    q, k, v = jnp.pad(q, pad), jnp.pad(k, pad), jnp.pad(v, pad)
    qb = q.reshape(N, nb, blk, H, D)

    def two_blocks(t):
        tb = jnp.pad(t, ((0, 0), (blk, 0), (0, 0), (0, 0))).reshape(N, nb + 1, blk, H, D)
        return jnp.concatenate([tb[:, :-1], tb[:, 1:]], axis=2)

    kb, vb = two_blocks(k), two_blocks(v)
    s = jnp.einsum('nbqhd,nbkhd->nbhqk', qb, kb).astype(jnp.float32) * (D ** -0.5)
    blk_idx = jnp.arange(nb)[:, None, None]
    qry_pos = blk_idx * blk + jnp.arange(blk)[None, :, None]
    key_pos = (blk_idx - 1) * blk + jnp.arange(2 * blk)[None, None, :]
    dist = qry_pos - key_pos
    mask = (dist >= 0) & (dist <= span) & (key_pos >= 0)
    s = jnp.where(mask[None, :, None], s, NEG_INF)
    m = jnp.max(s, axis=-1, keepdims=True)
    e = jnp.exp(s - m)
    denom = jnp.sum(e, axis=-1, keepdims=True)
    p = (e / denom).astype(v.dtype)
    o = jnp.einsum('nbhqk,nbkhd->nbqhd', p, vb).reshape(N, Lp, H, D)[:, :L]
    lse = (m + jnp.log(denom))[..., 0]
    lse = lse.transpose(0, 1, 3, 2).reshape(N, Lp, H)[:, :L]
    return o, lse


def to_strided(t, dil):
    B, S, H, D = t.shape
    return t.reshape(B, S // dil, dil, H, D).transpose(0, 2, 1, 3, 4).reshape(B * dil, S // dil, H, D)


def dilated_attention(q, k, v):
    B, S, H, D = q.shape
    outs, lses = [], []
    for window, dil in DIL_PATTERNS:
        L = S // dil
        o, lse = banded_causal_attention(to_strided(q, dil), to_strided(k, dil),
                                         to_strided(v, dil), window // dil)
        outs.append(o.reshape(B, dil, L, H, D).transpose(0, 2, 1, 3, 4).reshape(B, S, H, D))
        lses.append(lse.reshape(B, dil, L, H).transpose(0, 2, 1, 3).reshape(B, S, H))
    w = jax.nn.softmax(jnp.stack(lses, axis=0), axis=0)
    out = jnp.sum(w[..., None] * jnp.stack(outs, axis=0).astype(jnp.float32), axis=0)
    return out.astype(q.dtype)


def causal_depthwise_conv(u, w, b):
    K = w.shape[0]
    S = u.shape[1]
    up = jnp.pad(u, ((0, 0), (K - 1, 0), (0, 0)))
    y = b
    for kk in range(K):
        y = y + up[:, kk:kk + S] * w[kk]
    return y


def setup_inputs(seed: int = 0) -> dict:
    key = jax.random.key(seed)
    ks = jax.random.split(key, 24)
    nrm = jax.random.normal
    L = DEPTH

    def gain(k, n):
        return 1.0 + 0.05 * nrm(k, (L, n), jnp.float32)

    x = nrm(ks[0], (BATCH, SEQ, D_MODEL), jnp.float32)
    c = nrm(ks[1], (BATCH, D_MODEL), jnp.float32)
    positions = (jnp.arange(SEQ, dtype=jnp.int32)[None, :]
                 + jax.random.randint(ks[2], (BATCH, 1), 0, 4096, dtype=jnp.int32))
    return {
        "x": x,
        "c": c,
        "positions": positions,
        "w_ada": nrm(ks[3], (L, D_MODEL, 6 * D_MODEL), jnp.float32) * (0.5 * D_MODEL ** -0.5),
        "b_ada": 0.02 * nrm(ks[4], (L, 6 * D_MODEL), jnp.float32),
        "g_mix_norm": gain(ks[5], D_MODEL),
        "w_in": nrm(ks[6], (L, D_MODEL, IN_COLS), jnp.float32) * D_MODEL ** -0.5,
        "g_q_lat": gain(ks[7], MLA_Q_LORA),
        "w_q_b": nrm(ks[8], (L, MLA_Q_LORA, MLA_HEADS * (MLA_NOPE + MLA_ROPE)), jnp.float32) * MLA_Q_LORA ** -0.5,
        "g_kv_lat": gain(ks[9], MLA_KV_LORA),
        "w_kv_b": nrm(ks[10], (L, MLA_KV_LORA, MLA_HEADS * (MLA_NOPE + MLA_V)), jnp.float32) * MLA_KV_LORA ** -0.5,
        "g_mla_q_nope": gain(ks[11], MLA_NOPE),
        "g_mla_q_pe": gain(ks[12], MLA_ROPE),
        "g_mla_k_nope": gain(ks[13], MLA_NOPE),
        "g_mla_k_pe": gain(ks[14], MLA_ROPE),
        "g_dil_q": gain(ks[15], DIL_HEAD_DIM),
        "g_dil_k": gain(ks[16], DIL_HEAD_DIM),
        "w_o": nrm(ks[17], (L, MIX_WIDTH, D_MODEL), jnp.float32) * MIX_WIDTH ** -0.5,
        "g_ffn_norm": gain(ks[18], D_MODEL),
        "w_up": nrm(ks[19], (L, D_MODEL, 2 * D_FF), jnp.float32) * D_MODEL ** -0.5,
        "w_conv": nrm(ks[20], (L, CONV_WIDTH, 2 * D_FF), jnp.float32) * CONV_WIDTH ** -0.5,
        "b_conv": 0.02 * nrm(ks[21], (L, 2 * D_FF), jnp.float32),
        "w_down": nrm(ks[22], (L, D_FF, D_MODEL), jnp.float32) * D_FF ** -0.5,
    }


def reference(x, c, positions, w_ada, b_ada, g_mix_norm, w_in, g_q_lat, w_q_b, g_kv_lat, w_kv_b,
              g_mla_q_nope, g_mla_q_pe, g_mla_k_nope, g_mla_k_pe, g_dil_q, g_dil_k, w_o,
              g_ffn_norm, w_up, w_conv, b_conv, w_down):
    B, S, _ = x.shape
    for l in range(DEPTH):
        mod = jax.nn.silu(c) @ w_ada[l] + b_ada[l]
        sh1, sc1, g1, sh2, sc2, g2 = jnp.split(mod, 6, axis=-1)

        h = rms_norm(x, g_mix_norm[l]) * (1.0 + sc1[:, None]) + sh1[:, None]
        proj = h @ w_in[l]
        q_lat, kv_lat, k_pe, qd, kd, vd = jnp.split(proj, SPLITS, axis=-1)

        q = (rms_norm(q_lat, g_q_lat[l]) @ w_q_b[l]).reshape(B, S, MLA_HEADS, MLA_NOPE + MLA_ROPE)
        q_nope = rms_norm(q[..., :MLA_NOPE], g_mla_q_nope[l])
        q_pe = rope(rms_norm(q[..., MLA_NOPE:], g_mla_q_pe[l]), positions)
        kv = (rms_norm(kv_lat, g_kv_lat[l]) @ w_kv_b[l]).reshape(B, S, MLA_HEADS, MLA_NOPE + MLA_V)
        k_nope = rms_norm(kv[..., :MLA_NOPE], g_mla_k_nope[l])
        v_mla = kv[..., MLA_NOPE:]
        k_pe = rope(rms_norm(k_pe, g_mla_k_pe[l])[:, :, None, :], positions)
        k_mla = jnp.concatenate([k_nope, jnp.broadcast_to(k_pe, (B, S, MLA_HEADS, MLA_ROPE))], axis=-1)
        q_mla = jnp.concatenate([q_nope, q_pe], axis=-1)
        o_mla = causal_dense_attention(q_mla, k_mla, v_mla, (MLA_NOPE + MLA_ROPE) ** -0.5)

        qd = rope(rms_norm(qd.reshape(B, S, DIL_HEADS, DIL_HEAD_DIM), g_dil_q[l]), positions)
        kd = rope(rms_norm(kd.reshape(B, S, DIL_HEADS, DIL_HEAD_DIM), g_dil_k[l]), positions)
        vd = vd.reshape(B, S, DIL_HEADS, DIL_HEAD_DIM)
        o_dil = dilated_attention(qd, kd, vd)

        mix = jnp.concatenate([o_mla.reshape(B, S, MLA_HEADS * MLA_V),
                               o_dil.reshape(B, S, DIL_WIDTH)], axis=-1) @ w_o[l]
        x = x + g1[:, None] * mix

        h2 = rms_norm(x, g_ffn_norm[l]) * (1.0 + sc2[:, None]) + sh2[:, None]
        u = causal_depthwise_conv(h2 @ w_up[l], w_conv[l], b_conv[l])
        gate, val = jnp.split(u, 2, axis=-1)
        x = x + g2[:, None] * ((jax.nn.silu(gate) * val) @ w_down[l])
    return x
```

```python
import numpy as np
from contextlib import ExitStack

import concourse.bass as bass
import concourse.mybir as mybir
from concourse.bass_utils import run_bass_kernel_spmd

F32 = mybir.dt.float32
BF16 = mybir.dt.bfloat16
I32 = mybir.dt.int32
AF = mybir.ActivationFunctionType
ALU = mybir.AluOpType
AX = mybir.AxisListType

D = 1024
S_LEN = 2048
NT = 16
EPS = 1e-6
DFF = 2816
NJ = 22
TWO_PI = 6.283185307179586
PI = 3.141592653589793
C_HI = 6.28125
C_LO = TWO_PI - C_HI

G_QLAT, G_KVLAT, G_QN, G_QP, G_KN, G_KP, G_DQ, G_DK = 0, 512, 768, 832, 864, 928, 960, 1024
G_TOT = 1088


class Sched:
    CE = ("pe", "act", "dve", "pool")
    STRICT = True

    def __init__(self, nc):
        self.nc = nc
        self.E = {"pe": nc.tensor, "act": nc.scalar, "dve": nc.vector, "pool": nc.gpsimd, "sp": nc.sync}
        self.sem = {e: nc.alloc_semaphore(name=f"s_{e}") for e in self.CE}
        self.cnt = {e: 0 for e in self.CE}
        self.known = {e: {} for e in self.E}
        self.res = {}
        self.dsem = {}
        self.nwaits = 0

    def _wait(self, eng, tok):
        src, val = tok
        if self.known[eng].get(src, 0) >= val:
            return
        sem = self.sem[src] if src in self.sem else self.dsem[src[2:]][0]
        self.E[eng].wait_ge(sem, val)
        self.known[eng][src] = val
        self.nwaits += 1

    def _deps(self, eng, reads, writes):
        for r in reads:
            st = self.res.get(r)
            if st and st[0]:
                self._wait(eng, st[0])
            if st and r.startswith("ps"):
                for t in st[1].values():
                    if t[0] != eng:
                        self._wait(eng, t)
        strict = self.STRICT and eng != "pe"
        for w in writes:
            st = self.res.get(w)
            if st:
                if st[0] and (st[0][0] != eng or strict):
                    self._wait(eng, st[0])
                for t in st[1].values():
                    if t[0] != eng or strict:
                        self._wait(eng, t)

    def _commit(self, tok, reads, writes):
        for r in reads:
            st = self.res.setdefault(r, [None, {}])
            st[1][tok[0]] = tok
        for w in writes:
            self.res[w] = [tok, {}]

    def op(self, eng, fn, reads=(), writes=()):
        self._deps(eng, reads, writes)
        ins = fn(self.E[eng])
        self.cnt[eng] += 1
        ins.then_inc(self.sem[eng], 1)
        self._commit((eng, self.cnt[eng]), reads, writes)

    def dma(self, q, out, in_, reads, writes, key):
        self._deps(q, reads, writes)
        if key not in self.dsem:
            self.dsem[key] = [self.nc.alloc_semaphore(name="d_" + key), 0]
        ins = self.E[q].dma_start(out=out, in_=in_)
        self.dsem[key][1] += 16
        ins.then_inc(self.dsem[key][0], 16)
        self._commit(("d:" + key, self.dsem[key][1]), reads, writes)

    def barrier(self, engines=None):
        for e in (engines or self.E):
            for f in self.CE:
                if f != e and self.cnt[f] > 0:
                    self._wait(e, (f, self.cnt[f]))
            for key, (sem, c) in self.dsem.items():
                if c > 0:
                    self._wait(e, ("d:" + key, c))
        if engines is None:
            self.res = {}

    def finish(self):
        self.barrier(engines=["sp"])


def build_program(stop_after="F", taps=()):
    nc = bass.Bass("TRN2", target_bir_lowering=False)
    dr = {}

    def din(name, shape, dt=F32):
        dr[name] = nc.dram_tensor(name, list(shape), dt, kind="ExternalInput").ap()
        return dr[name]

    x_d = din("x", [S_LEN, D])
    ccol_d = din("ccol", [128, 8])
    pos_d = din("pos", [128, NT], I32)
    wada_d = din("w_ada", [128, 8, 6 * D])
    badac_d = din("b_ada_col", [128, 48])
    badag_d = din("b_ada_g", [1, 2048])
    gcols_d = din("gcols", [128, 16])
    gains_d = din("gains", [1, G_TOT])
    invf_d = din("invf", [1, 48])
    winM_d = din("w_inM", [128, 8, 800])
    winD_d = din("w_inD", [128, 8, 1536])
    wqb_d = din("w_qb", [128, 4, 768])
    wkvb_d = din("w_kvb", [128, 2, 1024])
    wo_d = din("w_o", [128, 8, 1024])
    wup_d = din("w_up", [NJ, 128, 8, 2, 128])
    wdn_d = din("w_down", [128, NJ, 1024])
    convc_d = din("convcol", [128, 4, 2 * NJ])
    masks_d = din("masks", [128, 17 * 128])
    out_d = nc.dram_tensor("out", [S_LEN, D], F32, kind="ExternalOutput").ap()
    tap_d = {}

    S = Sched(nc)
    L0 = ExitStack()

    uid = [0]

    def sb(stack, name, shape, dt=F32):
        uid[0] += 1
        return stack.enter_context(nc.sbuf_tensor(f"sb{uid[0]}_{name}", list(shape), dt))

    with L0:
        ident = sb(L0, "ident", [128, 128], BF16)
        ones32 = sb(L0, "ones32", [128, 64], F32)
        gains = sb(L0, "gains_sb", [128, G_TOT], F32)
        cosT = sb(L0, "cosT", [128, NT, 48], F32)
        sinT = sb(L0, "sinT", [128, NT, 48], F32)
        gv1 = sb(L0, "gv1", [128, 8], F32)
        sh1 = sb(L0, "sh1", [128, 8], F32)
        gv2 = sb(L0, "gv2", [128, 8], F32)
        sh2 = sb(L0, "sh2", [128, 8], F32)
        convc = sb(L0, "convc", [128, 4, 2 * NJ], F32)
        g12 = sb(L0, "g12", [128, 2048], F32)
        wmask = sb(L0, "wmask", [128, 17 * 128], BF16)
        epsb = sb(L0, "epsb", [128, 1], F32)
        R1 = sb(L0, "R1", [128, 45056], BF16)
        R2 = sb(L0, "R2", [128, 16384], BF16)
        psb = [L0.enter_context(nc.psum_tensor(f"psb{i}", [128, 1024], BF16)) for i in range(2)]
        psf = [L0.enter_context(nc.psum_tensor(f"psf{i}", [128, 512], F32)) for i in range(6)]

        mixT = R2[:, :].rearrange("p (c n) -> p c n", n=S_LEN)

        def tap(name, ap, shape, dt=F32, key=None):
            if name not in taps:
                return
            t = nc.dram_tensor("tap_" + name, list(shape), dt, kind="ExternalOutput").ap()
            tap_d[name] = t
            S.barrier(engines=["sp"])
            S.dma("sp", out=t, in_=ap, reads=[], writes=[], key="tap")

        def mod_chunks(ns, wa_bufs, wkeys, scb_t, screp_t, badag_t, pcol, pcol_key, pg, pg_key, n_last):
            for n in ns:
                buf = wa_bufs[n % 2]
                wkey = wkeys[n % 2]
                kind = n // 2
                if kind in (2, 5):
                    for k in range(8):
                        S.op("pe", lambda e, k=k: e.matmul(pg, screp_t[:, k, :], buf[:, k, :], start=(k == 0), stop=(k == 7)),
                             ["screp", wkey], [pg_key])
                        if k % 2 == 1:
                            yield
                    off = (0 if kind == 2 else 1024) + (n % 2) * 512
                    S.op("dve", lambda e: e.tensor_tensor(g12[:, off:off + 512], pg, badag_t[:, off:off + 512], ALU.add),
                         [pg_key, "badag"], ["g12"])
                else:
                    for c4 in range(4):
                        idx = n * 4 + c4
                        for k in range(8):
                            S.op("pe", lambda e, k=k: e.matmul(pcol[:, idx:idx + 1], buf[:, k, c4 * 128:(c4 + 1) * 128],
                                                               scb_t[:, k:k + 1], start=(k == 0), stop=(k == 7)),
                                 ["scb", wkey], [pcol_key])
                        yield
                if n + 2 <= n_last:
                    S.dma("pool", buf[:], wada_d[:, :, (n + 2) * 512:(n + 3) * 512], [], [wkey], wkey)
                for _ in range(3):
                    yield

        with ExitStack() as L1:
            identf = sb(L1, "identf", [128, 128], F32)
            cc = sb(L1, "cc", [128, 8], F32)
            scb = sb(L1, "scb", [128, 8], BF16)
            screp = sb(L1, "screp", [128, 8, 128], BF16)
            wa = [sb(L1, f"wa{i}", [128, 8, 512], BF16) for i in range(2)]
            badac = sb(L1, "badac", [128, 48], F32)
            gcols = sb(L1, "gcols", [128, 16], F32)
            modc = sb(L1, "modc", [128, 48], F32)
            posi = sb(L1, "posi", [128, NT], I32)
            posf = sb(L1, "posf", [128, NT], F32)
            invf = sb(L1, "invf_sb", [128, 48], F32)
            ang = sb(L1, "ang", [128, NT * 48], F32)
            tq = sb(L1, "tq", [128, NT * 48], F32)
            ki = sb(L1, "ki", [128, NT * 48], I32)
            kf = sb(L1, "kf", [128, NT * 48], F32)
            rr = sb(L1, "rr", [128, NT * 48], F32)
            rc = sb(L1, "rc", [128, NT * 48], F32)

            S.dma("sp", cc[:], ccol_d, [], ["cc"], "c0")
            S.dma("sp", badac[:], badac_d, [], ["badac"], "c1")
            S.dma("sp", gcols[:], gcols_d, [], ["gcols"], "c2")
            S.dma("sp", posi[:], pos_d, [], ["posi"], "c3")
            S.dma("sp", invf[:], invf_d.partition_broadcast(128), [], ["invf"], "c4")
            S.dma("sp", gains[:], gains_d.partition_broadcast(128), [], ["gains"], "c5")
            S.dma("sp", convc[:], convc_d, [], ["convc"], "c6")
            S.dma("pool", wmask[:], masks_d, [], ["wmask"], "c8")
            for n in range(2):
                S.dma("pool", wa[n][:], wada_d[:, :, n * 512:(n + 1) * 512], [], [f"wa{n}"], f"wa{n}")

            S.op("pool", lambda e: e.memset(identf[:], 1.0), [], ["identf"])
            S.op("pool", lambda e: e.affine_select(out=identf[:], in_=identf[:], pattern=[[-1, 128]],
                                                   compare_op=ALU.is_equal, fill=0.0, base=0,
                                                   channel_multiplier=1), ["identf"], ["identf"])
            S.op("dve", lambda e: e.tensor_copy(ident[:], identf[:]), ["identf"], ["ident"])
            S.op("dve", lambda e: e.memset(ones32[:], 1.0), [], ["ones32"])
            S.op("dve", lambda e: e.memset(epsb[:], EPS), [], ["epsb"])

            S.op("act", lambda e: e.activation(out=scb[:], in_=cc[:], func=AF.Silu), ["cc"], ["scb"])
            S.op("dve", lambda e: e.tensor_copy(screp[:], scb[:].unsqueeze(2).broadcast_to([128, 8, 128])),
                 ["scb"], ["screp"])

            for _ in mod_chunks(range(4), wa, ["wa0", "wa1"], scb, screp, None, psf[0], "psf0", None, None, 3):
                pass
            S.op("dve", lambda e: e.tensor_tensor(modc[:, 0:16], psf[0][:, 0:16], badac[:, 0:16], ALU.add),
                 ["psf0", "badac"], ["modc"])
            S.op("dve", lambda e: e.scalar_tensor_tensor(out=gv1[:], in0=modc[:, 8:16], scalar=1.0, in1=gcols[:, 0:8],
                                                        op0=ALU.add, op1=ALU.mult), ["modc", "gcols"], ["gv1"])
            S.op("dve", lambda e: e.tensor_copy(sh1[:], modc[:, 0:8]), ["modc"], ["sh1"])

            S.op("dve", lambda e: e.tensor_copy(posf[:], posi[:]), ["posi"], ["posf"])
            angv = ang[:].rearrange("p (t f) -> p t f", f=48)
            S.op("dve", lambda e: e.tensor_tensor(angv, posf[:].unsqueeze(2).broadcast_to([128, NT, 48]),
                                                  invf[:].unsqueeze(1).broadcast_to([128, NT, 48]), ALU.mult),
                 ["posf", "invf"], ["ang"])
            S.op("dve", lambda e: e.tensor_scalar(tq[:], ang[:], 1.0 / TWO_PI, None, ALU.mult), ["ang"], ["tq"])
            S.op("dve", lambda e: e.tensor_copy(ki[:], tq[:]), ["tq"], ["ki"])
            S.op("dve", lambda e: e.tensor_copy(kf[:], ki[:]), ["ki"], ["kf"])
            S.op("dve", lambda e: e.scalar_tensor_tensor(out=rr[:], in0=kf[:], scalar=-C_HI, in1=ang[:],
                                                        op0=ALU.mult, op1=ALU.add), ["kf", "ang"], ["rr"])
            S.op("dve", lambda e: e.scalar_tensor_tensor(out=rr[:], in0=kf[:], scalar=-C_LO, in1=rr[:],
                                                        op0=ALU.mult, op1=ALU.add), ["kf", "rr"], ["rr"])
            S.op("dve", lambda e: e.tensor_scalar(rc[:], rr[:], PI / 2, -TWO_PI, ALU.is_gt, ALU.mult), ["rr"], ["rc"])
            S.op("dve", lambda e: e.scalar_tensor_tensor(out=rc[:], in0=rr[:], scalar=PI / 2, in1=rc[:],
                                                        op0=ALU.add, op1=ALU.add), ["rr", "rc"], ["rc"])
            for buf, key in ((rr, "rr"), (rc, "rc")):
                S.op("dve", lambda e, buf=buf: e.tensor_scalar(buf[:], buf[:], PI, -PI, ALU.min, ALU.max), [key], [key])
            S.op("act", lambda e: e.activation(out=sinT[:].rearrange("p t f -> p (t f)"), in_=rr[:], func=AF.Sin),
                 ["rr"], ["sinT"])
            S.op("act", lambda e: e.activation(out=cosT[:].rearrange("p t f -> p (t f)"), in_=rc[:], func=AF.Sin),
                 ["rc"], ["cosT"])
            tap("gv1", gv1[:], [128, 8])
            tap("sh1", sh1[:], [128, 8])
            tap("cosT", cosT[:].rearrange("p t f -> p (t f)"), [128, NT * 48])
            tap("sinT", sinT[:].rearrange("p t f -> p (t f)"), [128, NT * 48])
            S.barrier()

        def rstd_from_ssq(st, ssq_ap, out_ap, n, key_in, key_out, scr_ap, key_scr):
            S.op("act", lambda e: e.activation(out=scr_ap, in_=ssq_ap, func=AF.Sqrt, scale=1.0 / n, bias=epsb[:, 0:1]),
                 [key_in], [key_scr])
            S.op("dve", lambda e: e.reciprocal(out_ap, scr_ap), [key_scr], [key_out])

        def norm_transpose(xtile, key_x, rstd_ap, key_r, gv, sh, xn, hdst, key_h, pb, key_pb):
            S.op("dve", lambda e: e.tensor_scalar(xn[:], xtile, rstd_ap, None, ALU.mult), [key_x, key_r], ["xn"])
            for c in range(8):
                bk = psb[c // 4]
                S.op("pe", lambda e, c=c, bk=bk: e.transpose(bk[:, (c % 4) * 128:(c % 4 + 1) * 128], xn[:, c * 128:(c + 1) * 128], ident[:]),
                     ["xn", "ident"], [f"psb{c // 4}"])
            for cc in range(4):
                c = cc
                S.op("act", lambda e, c=c: e.activation(out=hdst[:, c, :], in_=psb[0][:, (c % 4) * 128:(c % 4 + 1) * 128],
                                                        func=AF.Identity, scale=gv[:, c:c + 1], bias=sh[:, c:c + 1]),
                     ["psb0", "gv", "sh"], [f"{key_h}_{c}"])
                c = 4 + cc
                S.op("dve", lambda e, c=c: e.tensor_scalar(hdst[:, c, :], psb[1][:, (c % 4) * 128:(c % 4 + 1) * 128],
                                                           gv[:, c:c + 1], sh[:, c:c + 1], ALU.mult, ALU.add),
                     ["psb1", "gv", "sh"], [f"{key_h}_{c}"])

        def group_norm(src3, nh, dh, gain_ap, dst3, st, tagp, src_keys=None, sfx=""):
            sq, ss, rs, rs2 = st
            sqv = sq[:, 0:nh * dh].rearrange("p (h d) -> p h d", d=dh)
            src_keys = src_keys or [tagp + "src"]
            ksq, kss, krs, krs2 = "sq" + sfx, "ss" + sfx, "rsg" + sfx, "rs2g" + sfx
            S.op("act", lambda e: e.activation(out=sqv, in_=src3, func=AF.Square), src_keys, [ksq])
            yield
            S.op("dve", lambda e: e.tensor_reduce(out=ss[:, 0:nh], in_=sqv, axis=AX.X, op=ALU.add), [ksq], [kss])
            yield
            S.op("act", lambda e: e.activation(out=rs[:, 0:nh], in_=ss[:, 0:nh], func=AF.Sqrt, scale=1.0 / dh, bias=epsb[:, 0:1]),
                 [kss], [krs])
            yield
            S.op("dve", lambda e: e.reciprocal(rs2[:, 0:nh], rs[:, 0:nh]), [krs], [krs2])
            yield
            S.op("dve", lambda e: e.tensor_tensor(sqv, src3, gain_ap.unsqueeze(1).broadcast_to([128, nh, dh]), ALU.mult),
                 src_keys + ["gains"], [ksq])
            yield
            S.op("dve", lambda e: e.tensor_tensor(dst3, sqv, rs2[:, 0:nh].unsqueeze(2).broadcast_to([128, nh, dh]), ALU.mult),
                 [ksq, krs2], [tagp + "dst"])
            yield

        def rope(src3, nh, half, cos_ap, sin_ap, dst3, tmp, key_src, key_dst, sfx=""):
            x1 = src3[:, :, 0:half]
            x2 = src3[:, :, half:2 * half]
            cb = cos_ap.unsqueeze(1).broadcast_to([128, nh, half])
            sbb = sin_ap.unsqueeze(1).broadcast_to([128, nh, half])
            ta = tmp[0][:, 0:nh * half].rearrange("p (h d) -> p h d", d=half)
            tb = tmp[1][:, 0:nh * half].rearrange("p (h d) -> p h d", d=half)
            ka, kb = "ropa" + sfx, "ropb" + sfx
            S.op("dve", lambda e: e.tensor_tensor(ta, x1, cb, ALU.mult), [key_src, "cosT"], [ka])
            yield
            S.op("dve", lambda e: e.tensor_tensor(tb, x2, sbb, ALU.mult), [key_src, "sinT"], [kb])
            yield
            S.op("dve", lambda e: e.tensor_tensor(dst3[:, :, 0:half], ta, tb, ALU.subtract), [ka, kb], [key_dst])
            yield
            S.op("dve", lambda e: e.tensor_tensor(ta, x1, sbb, ALU.mult), [key_src, "sinT"], [ka])
            yield
            S.op("dve", lambda e: e.tensor_tensor(tb, x2, cb, ALU.mult), [key_src, "cosT"], [kb])
            yield
            S.op("dve", lambda e: e.tensor_tensor(dst3[:, :, half:2 * half], ta, tb, ALU.add), [ka, kb], [key_dst])
            yield

        def run_interleaved(gen_fns, max_active=2):
            locks = {}
            pending = list(gen_fns)
            active = []

            def maybe_start():
                if pending and len(active) < max_active and (not active or active[-1]["spawned"]):
                    active.append({"g": pending.pop(0)(), "req": None, "spawned": False})

            maybe_start()
            while active:
                progressed = False
                for ent in list(active):
                    g = ent["g"]
                    if ent["req"] is not None:
                        if locks.get(ent["req"]) is None:
                            locks[ent["req"]] = g
                            ent["req"] = None
                        else:
                            continue
                    try:
                        r = next(g)
                    except StopIteration:
                        active.remove(ent)
                        assert not [k for k, v in locks.items() if v is g], locks
                        progressed = True
                        maybe_start()
                        continue
                    progressed = True
                    if r is None:
                        pass
                    elif r == "spawn":
                        ent["spawned"] = True
                        maybe_start()
                    elif r[0] == "acq":
                        assert locks.get(r[1]) is not g, r
                        if locks.get(r[1]) is None:
                            locks[r[1]] = g
                        else:
                            ent["req"] = r[1]
                    elif r[0] == "rel":
                        assert locks.get(r[1]) is g, r
                        locks[r[1]] = None
                assert progressed, ("interleave deadlock", locks)

        def attention(qT_of, kT_of, v_of, scale, dil, chunk0, Pt, rec, bcs, side=None):
            chunks = []
            gidx = 0
            for h in range(8):
                for QB in range(4):
                    nkt = 4 * (QB + 1)
                    for c in range(nkt):
                        i0 = max(0, c - 4 * QB)
                        chunks.append(dict(h=h, QB=QB, c=c, i0=i0, q0=i0 * 128, n=512 - i0 * 128,
                                           first=(c == 0), last=(c == nkt - 1), g=gidx))
                    gidx += 1
            LA = 2
            deferred = []

            def emit_S(i):
                ck = chunks[i]
                h, QB, c, q0, n = ck["h"], ck["QB"], ck["c"], ck["q0"], ck["n"]
                qT, kT = qT_of(h), kT_of(h)
                sp_ = psf[i % 3]; skey = f"psf{i % 3}"
                P = Pt[i % 3]; pkey = f"P{i % 3}"
                S.op("pe", lambda e: e.matmul(sp_[:, 0:n], kT[:, c * 128:(c + 1) * 128],
                                              qT[:, QB * 512 + q0:(QB + 1) * 512], start=True, stop=True),
                     ["KT", "QT"], [skey])
                S.op("act", lambda e: e.activation(out=P[:, 0:n], in_=sp_[:, 0:n], func=AF.Exp, scale=scale),
                     [skey], [pkey])
                if dil:
                    d0 = 4 * QB + ck["i0"] - c
                    S.op("dve", lambda e: e.tensor_tensor(P[:, 0:n], P[:, 0:n], wmask[:, d0 * 128:d0 * 128 + n], ALU.mult),
                         [pkey, "wmask"], [pkey])
                elif c >= 4 * QB:
                    S.op("dve", lambda e: e.tensor_tensor(P[:, 0:128], P[:, 0:128], wmask[:, 16 * 128:17 * 128], ALU.mult),
                         [pkey, "wmask"], [pkey])

            def emit_PV(i, it):
                ck = chunks[i]
                h, QB, c, q0, n, g = ck["h"], ck["QB"], ck["c"], ck["q0"], ck["n"], ck["g"]
                ot = psf[3 + (g % 2)]; okey = f"psf{3 + (g % 2)}"
                P = Pt[i % 3]; pkey = f"P{i % 3}"
                S.op("pe", lambda e: e.matmul(ot[:, q0:512], v_of(h, c), P[:, 0:n], start=ck["first"], stop=ck["last"]),
                     [pkey, "V"], [okey])
                if not ck["last"]:
                    return
                nlo, dlo = (0, 64) if h % 2 == 0 else (64, 0)
                rk = f"rec{g % 2}"
                rr_ = rec[g % 2]
                S.op("act", lambda e: e.activation(out=rr_[dlo:dlo + 1, :], in_=ot[dlo:dlo + 1, :], func=AF.Ln), [okey], [rk])
                S.op("act", lambda e: e.activation(out=rr_[dlo:dlo + 1, :], in_=rr_[dlo:dlo + 1, :], func=AF.Exp, scale=-1.0), [rk], [rk])
                ch = chunk0 + h // 2

                def fin():
                    S.op("pe", lambda e: e.matmul(psf[5][nlo:nlo + 64, :], ones32[dlo:dlo + 1, 0:64], rr_[dlo:dlo + 1, :],
                                                  start=True, stop=True), [rk, "ones32"], ["psf5"])
                    S.op("dve", lambda e: e.tensor_copy(bcs[nlo:nlo + 64, :], psf[5][nlo:nlo + 64, :]), ["psf5"], ["bcs"])
                    S.op("dve", lambda e: e.tensor_tensor(mixT[nlo:nlo + 64, ch, QB * 512:(QB + 1) * 512], ot[nlo:nlo + 64, :],
                                                          bcs[nlo:nlo + 64, :], ALU.mult), [okey, "bcs"], ["mixT"])
                deferred.append((it + 2, fin))

            nC = len(chunks)
            for it in range(nC + LA + 3):
                if side is not None and it % 3 == 2:
                    next(side, None)
                if it < nC:
                    emit_S(it)
                j = it - LA
                if 0 <= j < nC:
                    emit_PV(j, it)
                while deferred and deferred[0][0] <= it:
                    deferred.pop(0)[1]()
            assert not deferred

        def v_layout_store(Vx, t, src_ps_list, key_src_list, e_act=True):
            pass

        if stop_after != "0":
            QT = R1[:, 0:16384].rearrange("p (h n) -> p h n", n=S_LEN)
            KT = R1[:, 16384:32768].rearrange("p (h n) -> p h n", n=S_LEN)
            Vm = R1[:, 32768:45056].rearrange("p (t j c) -> p t j c", j=4, c=192)
            QdT = R1[:, 0:8192].rearrange("p (j n) -> p j n", n=S_LEN)
            QdT1 = R1[:, 16384:24576].rearrange("p (j n) -> p j n", n=S_LEN)
            KdT = R1[:, 8192:16384].rearrange("p (j n) -> p j n", n=S_LEN)
            Vd = Vm
            S.op("pool", lambda e: e.memset(Vm[:, :, :, 64:128], 1.0), [], ["V"])

            with ExitStack() as L2:
                winM = sb(L2, "winM", [128, 8, 800], BF16)
                wqb = sb(L2, "wqb", [128, 4, 768], BF16)
                wkvb = sb(L2, "wkvb", [128, 2, 1024], BF16)
                xt = [sb(L2, f"xt{i}", [128, D], F32) for i in range(2)]
                xn = sb(L2, "xn", [128, D], BF16)
                hT = [sb(L2, f"hT{i}", [128, 8, 128], BF16) for i in range(2)]
                st_ssq = sb(L2, "st_ssq", [128, 8], F32)
                st_rs = sb(L2, "st_rs", [128, 8], F32)
                st_rs2 = sb(L2, "st_rs2", [128, 8], F32)
                sqs = [sb(L2, f"sq{i}", [128, 1280], F32) for i in range(2)]
                sss = [sb(L2, f"ss{i}", [128, 24], F32) for i in range(2)]
                rsgs = [sb(L2, f"rsg{i}", [128, 24], F32) for i in range(2)]
                rs2gs = [sb(L2, f"rs2g{i}", [128, 24], F32) for i in range(2)]
                qln = sb(L2, "qln", [128, 512], BF16)
                kvn = sb(L2, "kvn", [128, 256], BF16)
                latT = sb(L2, "latT", [128, 6, 128], BF16)
                kpe = sb(L2, "kpe", [128, 32], F32)
                kpe2 = sb(L2, "kpe2", [128, 32], F32)
                kper = [sb(L2, f"kper{i}", [128, 32], BF16) for i in range(2)]
                ropts = [[sb(L2, f"ropt{p}{i}", [128, 256], F32) for i in range(2)] for p in range(2)]

                S.dma("pool", winM[:], winM_d, [], ["winM"], "winM")
                S.dma("pool", wqb[:], wqb_d, [], ["wqb"], "wqb")
                S.dma("pool", wkvb[:], wkvb_d, [], ["wkvb"], "wkvb")
                S.dma("sp", xt[0][:], x_d[0:128, :], [], ["xt0"], "xt0")
                def tileM(t):
                    b = t % 2
                    sq = sqs[b]
                    gst = (sqs[b], sss[b], rsgs[b], rs2gs[b])
                    ropt = ropts[b]
                    junk = sqs[b][:, 0:512].bitcast(BF16)
                    yield ("acq", "X")
                    if t + 1 < NT:
                        S.dma("sp", xt[1 - b][:], x_d[(t + 1) * 128:(t + 2) * 128, :], [], [f"xt{1 - b}"], f"xt{1 - b}")
                    S.op("act", lambda e, b=b: e.activation(out=junk, in_=xt[b][:], func=AF.Square,
                                                            accum_out=st_ssq[:, 0:1]), [f"xt{b}"], [f"sq{b}", "ssq0"])
                    rstd_from_ssq(None, st_ssq[:, 0:1], st_rs[:, 0:1], D, "ssq0", "rs0", st_ssq[:, 1:2], "ssq0b")
                    yield
                    yield ("acq", "psb")
                    norm_transpose(xt[b][:], f"xt{b}", st_rs[:, 0:1], "rs0", gv1, sh1, xn, hT[b], f"hT{b}", psb[0], "psb0")
                    yield ("rel", "psb")
                    yield ("rel", "X")
                    yield "spawn"
                    yield ("acq", "P")
                    for k in range(8):
                        S.op("pe", lambda e, k=k, b=b: e.matmul(psf[0][:, :], hT[b][:, k, :], winM[:, k, 0:512],
                                                                start=(k == 0), stop=(k == 7)), [f"hT{b}_{k}", "winM"], ["psf0"])
                    for k in range(8):
                        S.op("pe", lambda e, k=k, b=b: e.matmul(psf[1][:, 0:288], hT[b][:, k, :], winM[:, k, 512:800],
                                                                start=(k == 0), stop=(k == 7)), [f"hT{b}_{k}", "winM"], ["psf1"])
                    yield
                    S.op("act", lambda e: e.activation(out=sq[:, 0:512], in_=psf[0][:, :], func=AF.Square,
                                                       accum_out=st_ssq[:, 4:5]), ["psf0"], [f"sq{b}", "ssqP0"])
                    yield
                    S.op("act", lambda e: e.activation(out=sq[:, 512:768], in_=psf[1][:, 0:256], func=AF.Square, scale=2.0 ** 0.5,
                                                       accum_out=st_ssq[:, 5:6]), ["psf1"], [f"sq{b}", "ssqP1"])
                    yield
                    S.op("act", lambda e: e.activation(out=sq[:, 768:800], in_=psf[1][:, 256:288], func=AF.Square, scale=4.0,
                                                       accum_out=st_ssq[:, 6:7]), ["psf1"], [f"sq{b}", "ssqP2"])
                    yield
                    S.op("act", lambda e: e.activation(out=st_rs[:, 4:7], in_=st_ssq[:, 4:7], func=AF.Sqrt, scale=1.0 / 512, bias=epsb[:, 0:1]),
                         ["ssqP0", "ssqP1", "ssqP2"], ["rsP"])
                    yield
                    S.op("dve", lambda e: e.reciprocal(st_rs2[:, 4:7], st_rs[:, 4:7]), ["rsP"], ["rs2P"])
                    yield
                    S.op("dve", lambda e: e.scalar_tensor_tensor(out=qln[:], in0=psf[0][:, :], scalar=st_rs2[:, 4:5],
                                                                in1=gains[:, G_QLAT:G_QLAT + 512], op0=ALU.mult, op1=ALU.mult),
                         ["psf0", "rs2P", "gains"], ["qln"])
                    yield
                    S.op("dve", lambda e: e.scalar_tensor_tensor(out=kvn[:], in0=psf[1][:, 0:256], scalar=st_rs2[:, 5:6],
                                                                in1=gains[:, G_KVLAT:G_KVLAT + 256], op0=ALU.mult, op1=ALU.mult),
                         ["psf1", "rs2P", "gains"], ["kvn"])
                    yield
                    S.op("dve", lambda e: e.scalar_tensor_tensor(out=qpn9s[b][:, 8, :], in0=psf[1][:, 256:288], scalar=st_rs2[:, 6:7],
                                                                in1=gains[:, G_KP:G_KP + 32], op0=ALU.mult, op1=ALU.mult),
                         ["psf1", "rs2P", "gains"], [f"kpe9{b}"])
                    yield
                    yield ("acq", "Q")
                    yield ("acq", "psb")
                    for c in range(4):
                        S.op("pe", lambda e, c=c: e.transpose(psb[1][:, c * 128:(c + 1) * 128], qln[:, c * 128:(c + 1) * 128], ident[:]),
                             ["qln", "ident"], ["psb1"])
                    for c in range(2):
                        S.op("pe", lambda e, c=c: e.transpose(psb[1][:, (4 + c) * 128:(5 + c) * 128], kvn[:, c * 128:(c + 1) * 128], ident[:]),
                             ["kvn", "ident"], ["psb1"])
                    S.op("act", lambda e: e.activation(out=latT[:].rearrange("p c n -> p (c n)"), in_=psb[1][:, 0:768], func=AF.Copy),
                         ["psb1"], ["latT"])
                    yield ("rel", "psb")
                    yield ("rel", "P")
                    for (pp, key, c0, cn) in ((psf[2], "psf2", 0, 512), (psf[3], "psf3", 512, 256)):
                        for k in range(4):
                            S.op("pe", lambda e, pp=pp, k=k, c0=c0, cn=cn: e.matmul(pp[:, 0:cn], latT[:, k, :], wqb[:, k, c0:c0 + cn],
                                                                                    start=(k == 0), stop=(k == 3)), ["latT", "wqb"], [key])
                    for (pp, key, c0) in ((psf[4], "psf4", 0), (psf[5], "psf5", 512)):
                        for k in range(2):
                            S.op("pe", lambda e, pp=pp, k=k, c0=c0: e.matmul(pp[:, :], latT[:, 4 + k, :], wkvb[:, k, c0:c0 + 512],
                                                                             start=(k == 0), stop=(k == 1)), ["latT", "wkvb"], [key])
                    yield
                    qf_b, kvf_b, qpn_b, Qtok_b, Ktok_b = qfs[b], kvfs[b], qpn9s[b][:, 0:8, :], Qtoks[b], Ktoks[b]
                    pb = str(b)
                    S.op("act", lambda e: e.activation(out=qf_b[:, 0:512], in_=psf[2][:, :], func=AF.Copy), ["psf2"], ["qsrcA" + pb])
                    S.op("dve", lambda e: e.tensor_copy(qf_b[:, 512:768], psf[3][:, 0:256]), ["psf3"], ["qsrcB" + pb])
                    S.op("act", lambda e: e.activation(out=kvf_b[:, 0:512], in_=psf[4][:, :], func=AF.Copy), ["psf4"], ["knsrcA" + pb])
                    S.op("dve", lambda e: e.tensor_copy(kvf_b[:, 512:1024], psf[5][:, :]), ["psf5"], ["knsrcB" + pb])
                    yield ("rel", "Q")
                    q3 = qf_b.rearrange("p (h d) -> p h d", d=96)
                    kv3 = kvf_b.rearrange("p (h d) -> p h d", d=128)
                    qsk = ["qsrcA" + pb, "qsrcB" + pb]
                    ksk = ["knsrcA" + pb, "knsrcB" + pb]
                    sqb, ssb, rsb, rs2b = gst
                    sqA = sqb[:, 0:512].rearrange("p (h d) -> p h d", d=64)
                    sqB = sqb[:, 512:768].rearrange("p (h d) -> p h d", d=32)
                    sqC = sqb[:, 768:1280].rearrange("p (h d) -> p h d", d=64)
                    ksq = "sq" + pb
                    S.op("act", lambda e: e.activation(out=sqA, in_=q3[:, :, 0:64], func=AF.Square), qsk, [ksq + "A"])
                    yield
                    S.op("act", lambda e: e.activation(out=sqB, in_=q3[:, :, 64:96], func=AF.Square, scale=2.0 ** 0.5), qsk, [ksq + "B"])
                    yield
                    S.op("act", lambda e: e.activation(out=sqC, in_=kv3[:, :, 0:64], func=AF.Square), ksk, [ksq + "C"])
                    yield
                    S.op("dve", lambda e: e.tensor_reduce(out=ssb[:, 0:8], in_=sqA, axis=AX.X, op=ALU.add), [ksq + "A"], ["ss" + pb])
                    yield
                    S.op("dve", lambda e: e.tensor_reduce(out=ssb[:, 8:16], in_=sqB, axis=AX.X, op=ALU.add), [ksq + "B"], ["ss" + pb])
                    yield
                    S.op("dve", lambda e: e.tensor_reduce(out=ssb[:, 16:24], in_=sqC, axis=AX.X, op=ALU.add), [ksq + "C"], ["ss" + pb])
                    yield
                    S.op("act", lambda e: e.activation(out=rsb[:, 0:24], in_=ssb[:, 0:24], func=AF.Sqrt, scale=1.0 / 64, bias=epsb[:, 0:1]),
                         ["ss" + pb], ["rsg" + pb])
                    yield
                    S.op("dve", lambda e: e.tensor_tensor(sqA, q3[:, :, 0:64], gains[:, G_QN:G_QN + 64].unsqueeze(1).broadcast_to([128, 8, 64]), ALU.mult),
                         qsk + ["gains"], [ksq + "A"])
                    yield
                    S.op("dve", lambda e: e.tensor_tensor(sqB, q3[:, :, 64:96], gains[:, G_QP:G_QP + 32].unsqueeze(1).broadcast_to([128, 8, 32]), ALU.mult),
                         qsk + ["gains"], [ksq + "B"])
                    yield
                    S.op("dve", lambda e: e.tensor_tensor(sqC, kv3[:, :, 0:64], gains[:, G_KN:G_KN + 64].unsqueeze(1).broadcast_to([128, 8, 64]), ALU.mult),
                         ksk + ["gains"], [ksq + "C"])
                    yield
                    S.op("dve", lambda e: e.reciprocal(rs2b[:, 0:24], rsb[:, 0:24]), ["rsg" + pb], ["rs2g" + pb])
                    yield
                    S.op("dve", lambda e: e.tensor_tensor(qpn_b, sqB, rs2b[:, 8:16].unsqueeze(2).broadcast_to([128, 8, 32]), ALU.mult),
                         [ksq + "B", "rs2g" + pb, ksq], ["qp" + pb + "dst"])
                    yield
                    S.op("dve", lambda e: e.tensor_tensor(Qtok_b[:, :, 0:64], sqA, rs2b[:, 0:8].unsqueeze(2).broadcast_to([128, 8, 64]), ALU.mult),
                         [ksq + "A", "rs2g" + pb, ksq], ["q" + pb + "dst"])
                    yield
                    S.op("dve", lambda e: e.tensor_tensor(Ktok_b[:, :, 0:64], sqC, rs2b[:, 16:24].unsqueeze(2).broadcast_to([128, 8, 64]), ALU.mult),
                         [ksq + "C", "rs2g" + pb, ksq], ["kn" + pb + "dst"])
                    yield
                    S.op("dve", lambda e: e.tensor_copy(qpn9s[b][:, 8, 0:1], qpn9s[b][:, 8, 0:1]), [f"kpe9{b}", "qp" + pb + "dst"], ["qp9" + pb])
                    yield from rope(qpn9s[b], 9, 16, cosT[:, t, 0:16], sinT[:, t, 0:16], rop9s[b], ropt, "qp9" + pb, "rop9" + pb, sfx=pb)
                    yield
                    S.op("dve", lambda e: e.tensor_copy(Qtok_b[:, :, 64:96], rop9s[b][:, 0:8, :]), ["rop9" + pb], ["q" + pb + "dst"])
                    S.op("dve", lambda e: e.tensor_copy(Ktok_b[:, :, 64:96], rop9s[b][:, 8:9, :].broadcast_to([128, 8, 32])),
                         ["rop9" + pb], ["kn" + pb + "dst"])
                    vsrc = kvf_b.rearrange("p (j e d) -> p j e d", e=2, d=128)[:, :, :, 64:128]
                    vdst = Vm[:, t, :, :].rearrange("p j (e d) -> p j e d", d=64)[:, :, 0:3:2, :]
                    S.op("pool", lambda e, vsrc=vsrc, vdst=vdst: e.tensor_copy(vdst, vsrc), ksk, ["V"])
                    yield
                    yield ("acq", "psb")
                    for h in range(8):
                        S.op("pe", lambda e, h=h: e.transpose(psb[0][0:96, h * 128:(h + 1) * 128], Qtok_b[:, h, :], ident[:]),
                             ["q" + pb + "dst", "ident"], ["psb0"])
                    S.op("act", lambda e, t=t: e.activation(out=QT[0:96, :, t * 128:(t + 1) * 128],
                                                            in_=psb[0][0:96, :].rearrange("p (h n) -> p h n", n=128), func=AF.Copy),
                         ["psb0"], ["QT"])
                    yield
                    for h in range(8):
                        S.op("pe", lambda e, h=h: e.transpose(psb[1][0:96, h * 128:(h + 1) * 128], Ktok_b[:, h, :], ident[:]),
                             ["kn" + pb + "dst", "ident"], ["psb1"])
                    S.op("dve", lambda e, t=t: e.tensor_copy(KT[0:96, :, t * 128:(t + 1) * 128],
                                                             psb[1][0:96, :].rearrange("p (h n) -> p h n", n=128)),
                         ["psb1"], ["KT"])
                    yield ("rel", "psb")

                R2f = R2[:, :].bitcast(F32)
                qfs = [R2f[:, p * 2048 + 0:p * 2048 + 768] for p in range(2)]
                kvfs = [R2f[:, p * 2048 + 768:p * 2048 + 1792] for p in range(2)]
                qpn9s = [R2f[:, 5632 + p * 288:5632 + (p + 1) * 288].rearrange("p (h d) -> p h d", d=32) for p in range(2)]
                rop9s = [R2[:, 12416 + p * 288:12416 + (p + 1) * 288].rearrange("p (h d) -> p h d", d=32) for p in range(2)]
                Qtoks = [R2[:, 8192 + p * 1536:8192 + p * 1536 + 768].rearrange("p (h d) -> p h d", d=96) for p in range(2)]
                Ktoks = [R2[:, 8192 + p * 1536 + 768:8192 + (p + 1) * 1536].rearrange("p (h d) -> p h d", d=96) for p in range(2)]
                run_interleaved([(lambda t=t: tileM(t)) for t in range(NT)])
                tap("QT", QT[0:96, :, :].rearrange("p h n -> p (h n)"), [96, 8 * S_LEN], BF16)
                tap("KT", KT[0:96, :, :].rearrange("p h n -> p (h n)"), [96, 8 * S_LEN], BF16)
                tap("Vm", Vm.rearrange("p t j c -> p (t j c)"), [128, 16 * 768], BF16)
                S.barrier()

        if stop_after not in ("0", "AM"):
            with ExitStack() as L2:
                winD = sb(L2, "winD", [128, 8, 1536], BF16)
                Pt = [sb(L2, f"Pt{i}", [128, 512], BF16) for i in range(3)]
                rec = [sb(L2, f"rec{i}", [128, 512], F32) for i in range(2)]
                bcs = sb(L2, "bcs", [128, 512], F32)
                S.dma("pool", winD[:], winD_d, [], ["winD"], "winD")
                with ExitStack() as Lmod:
                    wa2 = [sb(Lmod, f"wa2_{i}", [128, 8, 512], BF16) for i in range(2)]
                    badag2 = sb(Lmod, "badag2", [128, 2048], F32)
                    cc2 = sb(Lmod, "cc2", [128, 8], F32)
                    scb2 = sb(Lmod, "scb2", [128, 8], BF16)
                    screp2 = sb(Lmod, "screp2", [128, 8, 128], BF16)
                    badac2 = sb(Lmod, "badac2", [128, 48], F32)
                    gcols2 = sb(Lmod, "gcols2", [128, 16], F32)
                    modc2 = sb(Lmod, "modc2", [128, 48], F32)
                    pcol2 = psb[0][:, :].bitcast(F32)
                    pg2 = psb[1][:, :].bitcast(F32)

                    def side_mod():
                        S.dma("sp", cc2[:], ccol_d, [], ["cc2"], "m0")
                        S.dma("sp", badac2[:], badac_d, [], ["badac2"], "m1")
                        S.dma("sp", gcols2[:], gcols_d, [], ["gcols2"], "m2")
                        S.dma("sp", badag2[:], badag_d.partition_broadcast(128), [], ["badag"], "m3")
                        for n in (4, 5):
                            S.dma("pool", wa2[n % 2][:], wada_d[:, :, n * 512:(n + 1) * 512], [], [f"wa2{n % 2}"], f"wa2{n % 2}")
                        for _ in range(12):
                            yield
                        S.op("act", lambda e: e.activation(out=scb2[:], in_=cc2[:], func=AF.Silu), ["cc2"], ["scb"])
                        yield
                        S.op("dve", lambda e: e.tensor_copy(screp2[:], scb2[:].unsqueeze(2).broadcast_to([128, 8, 128])),
                             ["scb"], ["screp"])
                        for _ in range(4):
                            yield
                        yield from mod_chunks(range(4, 12), wa2, ["wa20", "wa21"], scb2, screp2, badag2, pcol2, "psb0", pg2, "psb1", 11)
                        S.op("dve", lambda e: e.tensor_tensor(modc2[:, 24:40], pcol2[:, 24:40], badac2[:, 24:40], ALU.add),
                             ["psb0", "badac2"], ["modc2"])
                        yield
                        S.op("dve", lambda e: e.scalar_tensor_tensor(out=gv2[:], in0=modc2[:, 32:40], scalar=1.0, in1=gcols2[:, 8:16],
                                                                    op0=ALU.add, op1=ALU.mult), ["modc2", "gcols2"], ["gv2"])
                        S.op("dve", lambda e: e.tensor_copy(sh2[:], modc2[:, 24:32]), ["modc2"], ["sh2"])

                    sidegen = side_mod()
                    attention(lambda h: QT[0:96, h, :], lambda h: KT[0:96, h, :],
                              lambda h, c: Vm[:, c, h // 2, (h % 2) * 64:(h % 2) * 64 + 128],
                              96 ** -0.5, False, 0, Pt, rec, bcs, side=sidegen)
                    for _ in sidegen:
                        pass
                    tap("g12", g12[:], [128, 2048])
                tap("mixM", mixT[:, 0:4, :].rearrange("p c n -> p (c n)"), [128, 4 * S_LEN], BF16)
                S.barrier()
                if stop_after != "M":
                    with ExitStack() as L3:
                        xt = [sb(L3, f"xtd{i}", [128, D], F32) for i in range(2)]
                        xn = sb(L3, "xnd", [128, D], BF16)
                        hT = [sb(L3, f"hTd{i}", [128, 8, 128], BF16) for i in range(2)]
                        st_ssq = sb(L3, "std_ssq", [128, 4], F32)
                        st_rs = sb(L3, "std_rs", [128, 4], F32)
                        junkd = sb(L3, "junkd", [128, D], BF16)
                        sqA = sb(L3, "sqA", [128, 1024], F32)
                        ssA = sb(L3, "ssA", [128, 16], F32)
                        rsA = sb(L3, "rsA", [128, 16], F32)
                        rs2A = sb(L3, "rs2A", [128, 16], F32)
                        ropt2 = [sb(L3, f"ropt2{i}", [128, 512], F32) for i in range(2)]
                        tokA = sb(L3, "tokA", [128, 1024], BF16)
                        R2f = R2[:, :].bitcast(F32)
                        qk_f = R2f[:, 4096:5120]
                        qk_n = R2f[:, 5120:6144]
                        S.op("pool", lambda e: e.memset(QdT[64:128, :, :], 0.0), [], ["QT"])
                        S.op("pool", lambda e: e.memset(QdT1[0:64, :, :], 0.0), [], ["QT"])
                        S.dma("sp", xt[0][:], x_d[0:128, :], [], ["xt0"], "xt0")

                        def tileD(t):
                            b = t % 2
                            yield ("acq", "X")
                            if t + 1 < NT:
                                S.dma("sp", xt[1 - b][:], x_d[(t + 1) * 128:(t + 2) * 128, :], [], [f"xt{1 - b}"], f"xt{1 - b}")
                            S.op("act", lambda e: e.activation(out=junkd[:], in_=xt[b][:], func=AF.Square,
                                                               accum_out=st_ssq[:, 0:1]), [f"xt{b}"], ["junkd", "ssq0"])
                            rstd_from_ssq(None, st_ssq[:, 0:1], st_rs[:, 0:1], D, "ssq0", "rs0", st_ssq[:, 1:2], "ssq0b")
                            yield
                            yield ("acq", "psb")
                            norm_transpose(xt[b][:], f"xt{b}", st_rs[:, 0:1], "rs0", gv1, sh1, xn, hT[b], f"hT{b}", psb[0], "psb0")
                            yield ("rel", "psb")
                            yield ("rel", "X")
                            yield "spawn"
                            yield ("acq", "P")
                            for (pp, key, c0) in ((psf[0], "psf0", 0), (psf[1], "psf1", 512), (psf[2], "psf2", 1024)):
                                for k in range(8):
                                    S.op("pe", lambda e, pp=pp, k=k, c0=c0: e.matmul(pp[:, :], hT[b][:, k, :], winD[:, k, c0:c0 + 512],
                                                                                   start=(k == 0), stop=(k == 7)),
                                         [f"hT{b}_{k}", "winD"], [key])
                                yield
                            yield ("acq", "N")
                            S.op("act", lambda e: e.activation(out=qk_f[:, 0:512], in_=psf[0][:, :], func=AF.Copy), ["psf0"], ["qkfA"])
                            S.op("dve", lambda e: e.tensor_copy(qk_f[:, 512:1024], psf[1][:, :]), ["psf1"], ["qkfB"])
                            vsrc = psf[2][:, :].rearrange("p (j e d) -> p j e d", e=2, d=64)
                            vdst = Vd[:, t, :, :].rearrange("p j (e d) -> p j e d", d=64)[:, :, 0:3:2, :]
                            S.op("act", lambda e: e.activation(out=vdst, in_=vsrc, func=AF.Copy), ["psf2"], ["V"])
                            yield ("rel", "P")
                            src3 = qk_f.rearrange("p (h d) -> p h d", d=64)
                            src4 = qk_f.rearrange("p (s h d) -> p s h d", s=2, h=8)
                            sq3 = sqA[:, :].rearrange("p (h d) -> p h d", d=64)
                            sq4 = sqA[:, :].rearrange("p (s h d) -> p s h d", s=2, h=8)
                            dst3 = qk_n.rearrange("p (h d) -> p h d", d=64)
                            g4 = gains[:, G_DQ:G_DQ + 128].rearrange("p (s d) -> p s d", d=64).unsqueeze(2).broadcast_to([128, 2, 8, 64])
                            S.op("act", lambda e: e.activation(out=sq3, in_=src3, func=AF.Square), ["qkfA", "qkfB"], ["sqA"])
                            yield
                            S.op("dve", lambda e: e.tensor_reduce(out=ssA[:, :], in_=sq3, axis=AX.X, op=ALU.add), ["sqA"], ["ssA"])
                            yield
                            S.op("act", lambda e: e.activation(out=rsA[:, :], in_=ssA[:, :], func=AF.Sqrt, scale=1.0 / 64, bias=epsb[:, 0:1]), ["ssA"], ["rsA"])
                            yield
                            S.op("dve", lambda e: e.reciprocal(rs2A[:, :], rsA[:, :]), ["rsA"], ["rs2A"])
                            yield
                            S.op("dve", lambda e: e.tensor_tensor(sq4, src4, g4, ALU.mult), ["qkfA", "qkfB", "gains"], ["sqA"])
                            yield
                            S.op("dve", lambda e: e.tensor_tensor(dst3, sq3, rs2A[:, :].unsqueeze(2).broadcast_to([128, 16, 64]), ALU.mult),
                                 ["sqA", "rs2A"], ["qkn"])
                            yield
                            yield from rope(dst3, 16, 32, cos_ap=cosT[:, t, 16:48], sin_ap=sinT[:, t, 16:48],
                                            dst3=tokA[:, :].rearrange("p (h d) -> p h d", d=64), tmp=ropt2, key_src="qkn", key_dst="tokA", sfx="D")
                            yield
                            yield ("acq", "psb")
                            for j in range(4):
                                S.op("pe", lambda e, j=j: e.transpose(psb[0][:, j * 128:(j + 1) * 128], tokA[:, j * 128:(j + 1) * 128], ident[:]),
                                     ["tokA", "ident"], ["psb0"])
                            for j in range(4):
                                S.op("pe", lambda e, j=j: e.transpose(psb[1][:, j * 128:(j + 1) * 128], tokA[:, 512 + j * 128:512 + (j + 1) * 128], ident[:]),
                                     ["tokA", "ident"], ["psb1"])
                            yield
                            S.op("act", lambda e: e.activation(
                                out=QdT[0:64, :, t * 128:(t + 1) * 128],
                                in_=psb[0][0:64, 0:512].rearrange("p (j n) -> p j n", n=128), func=AF.Copy), ["psb0"], ["QT"])
                            S.op("act", lambda e: e.activation(
                                out=QdT1[64:128, :, t * 128:(t + 1) * 128],
                                in_=psb[0][64:128, 0:512].rearrange("p (j n) -> p j n", n=128), func=AF.Copy), ["psb0"], ["QT"])
                            S.op("dve", lambda e: e.tensor_copy(
                                KdT[:, :, t * 128:(t + 1) * 128],
                                psb[1][:, 0:512].rearrange("p (j n) -> p j n", n=128)), ["psb1"], ["KT"])
                            yield ("rel", "psb")
                            yield ("rel", "N")

                        run_interleaved([(lambda t=t: tileD(t)) for t in range(NT)])
                        tap("QdT", QdT.rearrange("p j n -> p (j n)"), [128, 4 * S_LEN], BF16)
                        tap("KdT", KdT.rearrange("p j n -> p (j n)"), [128, 4 * S_LEN], BF16)
                        tap("Vd", Vd.rearrange("p t j c -> p (t j c)"), [128, 16 * 768], BF16)
                        S.barrier()
                    attention(lambda h: (QdT if h % 2 == 0 else QdT1)[:, h // 2, :],
                              lambda h: KdT[:, h // 2, :],
                              lambda h, c: Vd[:, c, h // 2, (h % 2) * 64:(h % 2) * 64 + 128],
                              64 ** -0.5, True, 4, Pt, rec, bcs)
                    tap("mixT", mixT.rearrange("p c n -> p (c n)"), [128, 8 * S_LEN], BF16)
                    S.barrier()

        if stop_after in ("O", "F"):
            x1 = R1[:, 0:32768].bitcast(F32).rearrange("p (t n) -> p t n", n=D)
            h2T = R1[:, 32768:40960].rearrange("p (c n) -> p c n", n=1024)
            wub = [R1[:, 40960 + i * 2048:40960 + (i + 1) * 2048].rearrange("p (k g c) -> p k g c", g=2, c=128) for i in range(2)]
            wub += [R2[:, 11264 + i * 2048:11264 + (i + 1) * 2048].rearrange("p (k g c) -> p k g c", g=2, c=128) for i in range(2)]
            NWB = 4

            def h2_norm(t, junk_ap, xnb, xkey, ssq, rs):
                S.op("act", lambda e: e.activation(out=junk_ap, in_=x1[:, t, :], func=AF.Square, accum_out=ssq[:, 0:1]),
                     [f"x1_{t}"], ["junkh", "sg", "ssq0"])
                rstd_from_ssq(None, ssq[:, 0:1], rs[:, 0:1], D, "ssq0", "rs0", ssq[:, 1:2], "ssq0b")
                S.op("dve", lambda e: e.tensor_scalar(xnb[:], x1[:, t, :], rs[:, 0:1], None, ALU.mult), [f"x1_{t}", "rs0"], [xkey])

            def h2_trans(t, xnb, xkey):
                tt = t % 8
                hdst = h2T[:, :, tt * 128:(tt + 1) * 128]
                for c in range(8):
                    S.op("pe", lambda e, c=c: e.transpose(psb[c // 4][:, (c % 4) * 128:(c % 4 + 1) * 128],
                                                          xnb[:, c * 128:(c + 1) * 128], ident[:]), [xkey, "ident"], [f"psb{c // 4}"])
                for cc in range(4):
                    S.op("act", lambda e, c=cc: e.activation(out=hdst[:, c, :], in_=psb[0][:, (c % 4) * 128:(c % 4 + 1) * 128],
                                                             func=AF.Identity, scale=gv2[:, c:c + 1], bias=sh2[:, c:c + 1]),
                         ["psb0"], [f"h2T_{cc}"])
                    S.op("dve", lambda e, c=4 + cc: e.tensor_scalar(hdst[:, c, :], psb[1][:, (c % 4) * 128:(c % 4 + 1) * 128],
                                                                    gv2[:, c:c + 1], sh2[:, c:c + 1], ALU.mult, ALU.add),
                         ["psb1"], [f"h2T_{4 + cc}"])
            with ExitStack() as L2:
                wo = sb(L2, "wo", [128, 8, 1024], BF16)
                xt = [sb(L2, f"xto{i}", [128, D], F32) for i in range(2)]
                tmpo = sb(L2, "tmpo", [128, 512], F32)
                junk_o = sb(L2, "junko", [128, D], BF16)
                xn_o = [sb(L2, f"xno{i}", [128, D], BF16) for i in range(3)]
                sso = sb(L2, "sso", [128, 4], F32)
                rso = sb(L2, "rso", [128, 4], F32)
                S.dma("pool", wo[:], wo_d, [], ["wo"], "wo")
                S.dma("sp", xt[0][:], x_d[0:128, :], [], ["xt0"], "xt0")
                for t in range(NT):
                    b = t % 2
                    if t + 1 < NT:
                        S.dma("sp", xt[1 - b][:], x_d[(t + 1) * 128:(t + 2) * 128, :], [], [f"xt{1 - b}"], f"xt{1 - b}")
                    for nh in range(2):
                        pp = psf[(2 * t + nh) % 4]
                        key = f"psf{(2 * t + nh) % 4}"
                        for k in range(8):
                            S.op("pe", lambda e, pp=pp, k=k, t=t, nh=nh: e.matmul(pp[:, :], mixT[:, k, t * 128:(t + 1) * 128],
                                                                                  wo[:, k, nh * 512:(nh + 1) * 512],
                                                                                  start=(k == 0), stop=(k == 7)), ["mixT", "wo"], [key])
                        S.op("dve", lambda e, pp=pp, nh=nh: e.tensor_tensor(tmpo[:], pp[:, :], g12[:, nh * 512:(nh + 1) * 512], ALU.mult),
                             [key, "g12"], ["tmpo"])
                        S.op("pool", lambda e, t=t, nh=nh, b=b: e.tensor_tensor(x1[:, t, nh * 512:(nh + 1) * 512], tmpo[:],
                                                                               xt[b][:, nh * 512:(nh + 1) * 512], ALU.add),
                             ["tmpo", f"xt{b}"], [f"x1_{t}"])
                    if t < 8:
                        h2_norm(t, junk_o[:], xn_o[t % 3], f"xno{t % 3}", sso, rso)
                    if 2 <= t <= 9:
                        h2_trans(t - 2, xn_o[(t - 2) % 3], f"xno{(t - 2) % 3}")
                    if t == 10:
                        for i in range(2):
                            S.dma("pool", wub[i], wup_d[i], [], [f"wub{i}"], f"wub{i}")
                tap("x1", R1[:, 0:32768].bitcast(F32), [128, 16 * D])
                S.barrier()

        if stop_after == "F":
            aT = R2[:, 0:11264].rearrange("p (j n) -> p j n", n=1024)
            with ExitStack() as L2:
                wdn = sb(L2, "wdn", [128, 11, 1024], BF16)
                xn_f = [sb(L2, f"xnf{i}", [128, D], BF16) for i in range(2)]
                st_ssq = sb(L2, "stf_ssq", [128, 4], F32)
                st_rs = sb(L2, "stf_rs", [128, 4], F32)
                ug = sb(L2, "ug", [128, 1026], F32)
                uv = sb(L2, "uv", [128, 1026], F32)
                yg = sb(L2, "yg", [128, 1024], F32)
                yv = sb(L2, "yv", [128, 1024], F32)
                sg = sb(L2, "sg", [128, 1024], F32)
                halo = sb(L2, "halo", [128, 2 * NJ, 2], F32)
                tmpf = sb(L2, "tmpf", [128, 512], F32)
                S.dma("pool", wub[2], wup_d[2], [], ["wub2"], "wub2")
                S.dma("pool", wdn[:], wdn_d[:, 0:11, :], [], ["wdn"], "wdn")
                junk_f = sg[:, 0:512].bitcast(BF16)
                for H in range(2):
                    for JG in range(2):
                        for jj in range(11):
                            j = JG * 11 + jj
                            seq = (H * 2 + JG) * 11 + jj
                            wb = wub[seq % NWB]
                            wkey = f"wub{seq % NWB}"
                            if seq + NWB - 1 < 44:
                                nk = f"wub{(seq + NWB - 1) % NWB}"
                                S.dma("pool", wub[(seq + NWB - 1) % NWB], wup_d[(seq + NWB - 1) % 22], [], [nk], nk)
                            banks = {}
                            for tb in range(2):
                                for g_ in range(2):
                                    bi = (4 * seq + 2 * tb + g_) % 6
                                    banks[(tb, g_)] = bi
                                    for k in range(8):
                                        S.op("pe", lambda e, k=k, g_=g_, tb=tb, bi=bi: e.matmul(
                                            psf[bi][:, :], wb[:, k, g_, :], h2T[:, k, tb * 512:(tb + 1) * 512],
                                            start=(k == 0), stop=(k == 7)), [wkey, f"h2T_{k}"], [f"psf{bi}"])
                            for (g_, usb, ukey, ysb, ykey) in ((0, ug, "ug", yg, "yg"), (1, uv, "uv", yv, "yv")):
                                fc = j + g_ * NJ
                                if H == 0:
                                    S.op("pool", lambda e: e.memset(usb[:, 0:2], 0.0), [], [ukey])
                                else:
                                    S.op("pool", lambda e: e.tensor_copy(usb[:, 0:2], halo[:, fc, :]), [f"halo{fc}"], [ukey])
                                for tb in range(2):
                                    bi = banks[(tb, g_)]
                                    S.op("act", lambda e, tb=tb, bi=bi: e.activation(out=usb[:, 2 + tb * 512:514 + tb * 512], in_=psf[bi][:, :],
                                                                                   func=AF.Copy), [f"psf{bi}"], [ukey])
                                    S.op("act", lambda e, tb=tb, bi=bi: e.activation(
                                        out=ysb[:, tb * 512:(tb + 1) * 512], in_=psf[bi][:, :], func=AF.Identity,
                                        scale=convc[:, 2, fc:fc + 1], bias=convc[:, 3, fc:fc + 1]), [f"psf{bi}", "convc"], [ykey])
                                if H == 0:
                                    S.op("pool", lambda e: e.tensor_copy(halo[:, fc, :], usb[:, 1024:1026]), [ukey], [f"halo{fc}"])
                                S.op("dve", lambda e: e.scalar_tensor_tensor(
                                    out=ysb[:], in0=usb[:, 1:1025], scalar=convc[:, 1, fc:fc + 1], in1=ysb[:], op0=ALU.mult, op1=ALU.add),
                                    [ukey, ykey, "convc"], [ykey])
                                S.op("dve", lambda e: e.scalar_tensor_tensor(
                                    out=ysb[:], in0=usb[:, 0:1024], scalar=convc[:, 0, fc:fc + 1], in1=ysb[:], op0=ALU.mult, op1=ALU.add),
                                    [ukey, ykey, "convc"], [ykey])
                            S.op("act", lambda e: e.activation(out=sg[:], in_=yg[:], func=AF.Silu), ["yg"], ["sg"])
                            S.op("dve", lambda e: e.tensor_tensor(aT[:, jj, :], sg[:], yv[:], ALU.mult), ["sg", "yv"], ["aT"])
                        for tt in range(8):
                            t = H * 8 + tt
                            for nh in range(2):
                                pp = psf[4 + ((2 * tt + nh) % 2)]
                                pkey = f"psf{4 + ((2 * tt + nh) % 2)}"
                                for jj in range(11):
                                    S.op("pe", lambda e, pp=pp, jj=jj, tt=tt, nh=nh, JG=JG: e.matmul(
                                        pp[:, :], aT[:, jj, tt * 128:(tt + 1) * 128], wdn[:, jj, nh * 512:(nh + 1) * 512],
                                        start=(jj == 0), stop=(jj == 10)), ["aT", "wdn"], [pkey])
                                S.op("dve", lambda e, pp=pp, nh=nh: e.tensor_tensor(tmpf[:], pp[:, :], g12[:, 1024 + nh * 512:1024 + (nh + 1) * 512],
                                                                                  ALU.mult), [pkey, "g12"], ["tmpf"])
                                S.op("pool", lambda e, t=t, nh=nh: e.tensor_tensor(x1[:, t, nh * 512:(nh + 1) * 512], tmpf[:],
                                                                                 x1[:, t, nh * 512:(nh + 1) * 512], ALU.add),
                                     ["tmpf", f"x1_{t}"], [f"x1_{t}"])
                            if JG == 1:
                                S.dma("sp", out_d[t * 128:(t + 1) * 128, :], x1[:, t, :], [f"x1_{t}"], [], "outd")
                            if H == 0 and JG == 1:
                                h2_norm(8 + tt, junk_f, xn_f[tt % 2], f"xnf{tt % 2}", st_ssq, st_rs)
                                if tt >= 1:
                                    h2_trans(8 + tt - 1, xn_f[(tt - 1) % 2], f"xnf{(tt - 1) % 2}")
                        if H == 0 and JG == 1:
                            h2_trans(15, xn_f[1], "xnf1")
                        if not (H == 1 and JG == 1):
                            nJG = 1 - JG
                            S.dma("pool", wdn[:], wdn_d[:, nJG * 11:(nJG + 1) * 11, :], ["dummy"], ["wdn"], "wdn")
        else:
            with ExitStack() as L2:
                z = sb(L2, "zout", [128, D], F32)
                S.op("dve", lambda e: e.memset(z[:], 0.0), [], ["z"])
                for t in range(NT):
                    S.dma("sp", out_d[t * 128:(t + 1) * 128, :], z[:], ["z"], [], "outd")
        S.finish()
    return nc, list(tap_d.keys())


def _host_constants():
    ki = np.arange(128)[:, None]
    col = np.arange(16 * 128)[None, :]
    dist = col - ki
    cnt = ((dist >= 0) & (dist <= 128)).astype(np.float32)
    cnt += ((dist >= 0) & (dist <= 512) & (dist % 4 == 0)).astype(np.float32)
    cnt += ((dist >= 0) & (dist <= 2048) & (dist % 16 == 0)).astype(np.float32)
    caus = (np.arange(128)[None, :] >= ki).astype(np.float32)
    masks = np.concatenate([cnt, caus], axis=1).astype(np.float32)
    inv_m = np.power(np.float32(10000.0), (-2.0 * np.arange(16, dtype=np.float32) / np.float32(32))).astype(np.float32)
    inv_d = np.power(np.float32(10000.0), (-2.0 * np.arange(32, dtype=np.float32) / np.float32(64))).astype(np.float32)
    invf = np.concatenate([inv_m, inv_d])[None, :].astype(np.float32)
    return masks, invf


def _prep_inputs(inp):
    f = lambda a: np.ascontiguousarray(np.asarray(a))
    masks, invf = _host_constants()
    w_ada = f(inp["w_ada"])[0]
    b_ada = f(inp["b_ada"])[0]
    w_in = f(inp["w_in"])[0]
    shared = {
        "w_ada": f(w_ada.reshape(8, 128, 6 * D).transpose(1, 0, 2)),
        "b_ada_col": f(b_ada.reshape(48, 128).T),
        "b_ada_g": f(np.concatenate([b_ada[2048:3072], b_ada[5120:6144]])[None, :]),
        "gcols": f(np.concatenate([f(inp["g_mix_norm"])[0].reshape(8, 128).T, f(inp["g_ffn_norm"])[0].reshape(8, 128).T], axis=1)),
        "gains": f(np.concatenate([f(inp["g_q_lat"])[0], f(inp["g_kv_lat"])[0], f(inp["g_mla_q_nope"])[0], f(inp["g_mla_q_pe"])[0],
                                   f(inp["g_mla_k_nope"])[0], f(inp["g_mla_k_pe"])[0], f(inp["g_dil_q"])[0], f(inp["g_dil_k"])[0]])[None, :]),
        "invf": invf,
        "w_inM": f(w_in[:, 0:800].reshape(8, 128, 800).transpose(1, 0, 2)),
        "w_inD": f(w_in[:, 800:2336].reshape(8, 128, 1536).transpose(1, 0, 2)),
        "w_qb": f(f(inp["w_q_b"])[0].reshape(4, 128, 768).transpose(1, 0, 2)),
        "w_kvb": f(f(inp["w_kv_b"])[0].reshape(2, 128, 1024).transpose(1, 0, 2)),
        "w_o": f(f(inp["w_o"])[0].reshape(8, 128, 1024).transpose(1, 0, 2)),
        "w_up": f(f(inp["w_up"])[0].reshape(8, 128, 2, NJ, 128).transpose(3, 1, 0, 2, 4)),
        "w_down": f(f(inp["w_down"])[0].reshape(NJ, 128, 1024).transpose(1, 0, 2)),
        "convcol": f(np.concatenate([f(inp["w_conv"])[0], f(inp["b_conv"])], axis=0).reshape(4, 2 * NJ, 128).transpose(2, 0, 1)),
        "masks": masks,
    }
    shared = {k: v.astype(np.float32) for k, v in shared.items()}
    x = f(inp["x"]); c = f(inp["c"]); pos = f(inp["positions"])
    maps = []
    for b in range(8):
        m = dict(shared)
        m["x"] = f(x[b]).astype(np.float32)
        m["ccol"] = f(c[b].reshape(8, 128).T).astype(np.float32)
        m["pos"] = f(pos[b].reshape(NT, 128).T).astype(np.int32)
        maps.append(m)
    return maps


_CACHE = {}


def kernel(**inputs):
    maps = _prep_inputs(inputs)
    if "nc" not in _CACHE:
        _CACHE["nc"] = build_program("F")[0]
    res = run_bass_kernel_spmd(_CACHE["nc"], maps, core_ids=list(range(8)))
    out = np.stack([np.asarray(r["out"]).reshape(S_LEN, D) for r in res.results], axis=0)
    return out.astype(np.float32)
```

```python
import numpy as np
from contextlib import ExitStack

import concourse.bass as bass
import concourse.mybir as mybir
from concourse.bass_utils import run_bass_kernel_spmd

F32 = mybir.dt.float32
BF16 = mybir.dt.bfloat16
I32 = mybir.dt.int32
AF = mybir.ActivationFunctionType
ALU = mybir.AluOpType
AX = mybir.AxisListType

D = 1024
S_LEN = 2048
NT = 16
EPS = 1e-6
DFF = 2816
NJ = 22
TWO_PI = 6.283185307179586
PI = 3.141592653589793
C_HI = 6.28125
C_LO = TWO_PI - C_HI

G_QLAT, G_KVLAT, G_QN, G_QP, G_KN, G_KP, G_DQ, G_DK = 0, 512, 768, 832, 864, 928, 960, 1024
G_TOT = 1088


class Sched:
    CE = ("pe", "act", "dve", "pool")
    STRICT = True

    def __init__(self, nc):
        self.nc = nc
        self.E = {"pe": nc.tensor, "act": nc.scalar, "dve": nc.vector, "pool": nc.gpsimd, "sp": nc.sync}
        self.sem = {e: nc.alloc_semaphore(name=f"s_{e}") for e in self.CE}
        self.cnt = {e: 0 for e in self.CE}
        self.known = {e: {} for e in self.E}
        self.res = {}
        self.dsem = {}
        self.nwaits = 0

    def _wait(self, eng, tok):
        src, val = tok
        if self.known[eng].get(src, 0) >= val:
            return
        sem = self.sem[src] if src in self.sem else self.dsem[src[2:]][0]
        self.E[eng].wait_ge(sem, val)
        self.known[eng][src] = val
        self.nwaits += 1

    def _deps(self, eng, reads, writes):
        for r in reads:
            st = self.res.get(r)
            if st and st[0]:
                self._wait(eng, st[0])
            if st and r.startswith("ps"):
                for t in st[1].values():
                    if t[0] != eng:
                        self._wait(eng, t)
        strict = self.STRICT and eng != "pe"
        for w in writes:
            st = self.res.get(w)
            if st:
                if st[0] and (st[0][0] != eng or strict):
                    self._wait(eng, st[0])
                for t in st[1].values():
                    if t[0] != eng or strict:
                        self._wait(eng, t)

    def _commit(self, tok, reads, writes):
        for r in reads:
            st = self.res.setdefault(r, [None, {}])
            st[1][tok[0]] = tok
        for w in writes:
            self.res[w] = [tok, {}]

    def op(self, eng, fn, reads=(), writes=()):
        self._deps(eng, reads, writes)
        ins = fn(self.E[eng])
        self.cnt[eng] += 1
        ins.then_inc(self.sem[eng], 1)
        self._commit((eng, self.cnt[eng]), reads, writes)

    def dma(self, q, out, in_, reads, writes, key):
        self._deps(q, reads, writes)
        if key not in self.dsem:
            self.dsem[key] = [self.nc.alloc_semaphore(name="d_" + key), 0]
        ins = self.E[q].dma_start(out=out, in_=in_)
        self.dsem[key][1] += 16
        ins.then_inc(self.dsem[key][0], 16)
        self._commit(("d:" + key, self.dsem[key][1]), reads, writes)

    def barrier(self, engines=None):
        for e in (engines or self.E):
            for f in self.CE:
                if f != e and self.cnt[f] > 0:
                    self._wait(e, (f, self.cnt[f]))
            for key, (sem, c) in self.dsem.items():
                if c > 0:
                    self._wait(e, ("d:" + key, c))
        if engines is None:
            self.res = {}

    def finish(self):
        self.barrier(engines=["sp"])


def build_program(stop_after="F", taps=()):
    nc = bass.Bass("TRN2", target_bir_lowering=False)
    dr = {}

    def din(name, shape, dt=F32):
        dr[name] = nc.dram_tensor(name, list(shape), dt, kind="ExternalInput").ap()
        return dr[name]

    x_d = din("x", [S_LEN, D])
    ccol_d = din("ccol", [128, 8])
    pos_d = din("pos", [128, NT], I32)
    wada_d = din("w_ada", [128, 8, 6 * D])
    badac_d = din("b_ada_col", [128, 48])
    badag_d = din("b_ada_g", [1, 2048])
    gcols_d = din("gcols", [128, 16])
    gains_d = din("gains", [1, G_TOT])
    invf_d = din("invf", [1, 48])
    winM_d = din("w_inM", [128, 8, 800])
    winD_d = din("w_inD", [128, 8, 1536])
    wqb_d = din("w_qb", [128, 4, 768])
    wkvb_d = din("w_kvb", [128, 2, 1024])
    wo_d = din("w_o", [128, 8, 1024])
    wup_d = din("w_up", [NJ, 128, 8, 2, 128])
    wdn_d = din("w_down", [128, NJ, 1024])
    convc_d = din("convcol", [128, 4, 2 * NJ])
    masks_d = din("masks", [128, 17 * 128])
    out_d = nc.dram_tensor("out", [S_LEN, D], F32, kind="ExternalOutput").ap()
    tap_d = {}

    S = Sched(nc)
    L0 = ExitStack()

    uid = [0]

    def sb(stack, name, shape, dt=F32):
        uid[0] += 1
        return stack.enter_context(nc.sbuf_tensor(f"sb{uid[0]}_{name}", list(shape), dt))

    with L0:
        ident = sb(L0, "ident", [128, 128], BF16)
        ones32 = sb(L0, "ones32", [128, 64], F32)
        gains = sb(L0, "gains_sb", [128, G_TOT], F32)
        cosT = sb(L0, "cosT", [128, NT, 48], F32)
        sinT = sb(L0, "sinT", [128, NT, 48], F32)
        gv1 = sb(L0, "gv1", [128, 8], F32)
        sh1 = sb(L0, "sh1", [128, 8], F32)
        gv2 = sb(L0, "gv2", [128, 8], F32)
        sh2 = sb(L0, "sh2", [128, 8], F32)
        convc = sb(L0, "convc", [128, 4, 2 * NJ], F32)
        g12 = sb(L0, "g12", [128, 2048], F32)
        wmask = sb(L0, "wmask", [128, 17 * 128], BF16)
        epsb = sb(L0, "epsb", [128, 1], F32)
        R1 = sb(L0, "R1", [128, 45056], BF16)
        R2 = sb(L0, "R2", [128, 16384], BF16)
        psb = [L0.enter_context(nc.psum_tensor(f"psb{i}", [128, 1024], BF16)) for i in range(2)]
        psf = [L0.enter_context(nc.psum_tensor(f"psf{i}", [128, 512], F32)) for i in range(6)]

        mixT = R2[:, :].rearrange("p (c n) -> p c n", n=S_LEN)

        def tap(name, ap, shape, dt=F32, key=None):
            if name not in taps:
                return
            t = nc.dram_tensor("tap_" + name, list(shape), dt, kind="ExternalOutput").ap()
            tap_d[name] = t
            S.barrier(engines=["sp"])
            S.dma("sp", out=t, in_=ap, reads=[], writes=[], key="tap")

        def mod_chunks(ns, wa_bufs, wkeys, scb_t, screp_t, badag_t, pcol, pcol_key, pg, pg_key, n_last):
            for n in ns:
                buf = wa_bufs[n % 2]
                wkey = wkeys[n % 2]
                kind = n // 2
                if kind in (2, 5):
                    for k in range(8):
                        S.op("pe", lambda e, k=k: e.matmul(pg, screp_t[:, k, :], buf[:, k, :], start=(k == 0), stop=(k == 7)),
                             ["screp", wkey], [pg_key])
                        if k % 2 == 1:
                            yield
                    off = (0 if kind == 2 else 1024) + (n % 2) * 512
                    S.op("dve", lambda e: e.tensor_tensor(g12[:, off:off + 512], pg, badag_t[:, off:off + 512], ALU.add),
                         [pg_key, "badag"], ["g12"])
                else:
                    for c4 in range(4):
                        idx = n * 4 + c4
                        for k in range(8):
                            S.op("pe", lambda e, k=k: e.matmul(pcol[:, idx:idx + 1], buf[:, k, c4 * 128:(c4 + 1) * 128],
                                                               scb_t[:, k:k + 1], start=(k == 0), stop=(k == 7)),
                                 ["scb", wkey], [pcol_key])
                        yield
                if n + 2 <= n_last:
                    S.dma("pool", buf[:], wada_d[:, :, (n + 2) * 512:(n + 3) * 512], [], [wkey], wkey)
                for _ in range(3):
                    yield

        with ExitStack() as L1:
            identf = sb(L1, "identf", [128, 128], F32)
            cc = sb(L1, "cc", [128, 8], F32)
            scb = sb(L1, "scb", [128, 8], BF16)
            screp = sb(L1, "screp", [128, 8, 128], BF16)
            wa = [sb(L1, f"wa{i}", [128, 8, 512], BF16) for i in range(2)]
            badac = sb(L1, "badac", [128, 48], F32)
            gcols = sb(L1, "gcols", [128, 16], F32)
            modc = sb(L1, "modc", [128, 48], F32)
            posi = sb(L1, "posi", [128, NT], I32)
            posf = sb(L1, "posf", [128, NT], F32)
            invf = sb(L1, "invf_sb", [128, 48], F32)
            ang = sb(L1, "ang", [128, NT * 48], F32)
            tq = sb(L1, "tq", [128, NT * 48], F32)
            ki = sb(L1, "ki", [128, NT * 48], I32)
            kf = sb(L1, "kf", [128, NT * 48], F32)
            rr = sb(L1, "rr", [128, NT * 48], F32)
            rc = sb(L1, "rc", [128, NT * 48], F32)

            S.dma("sp", cc[:], ccol_d, [], ["cc"], "c0")
            S.dma("sp", badac[:], badac_d, [], ["badac"], "c1")
            S.dma("sp", gcols[:], gcols_d, [], ["gcols"], "c2")
            S.dma("sp", posi[:], pos_d, [], ["posi"], "c3")
            S.dma("sp", invf[:], invf_d.partition_broadcast(128), [], ["invf"], "c4")
            S.dma("sp", gains[:], gains_d.partition_broadcast(128), [], ["gains"], "c5")
            S.dma("sp", convc[:], convc_d, [], ["convc"], "c6")
            S.dma("pool", wmask[:], masks_d, [], ["wmask"], "c8")
            for n in range(2):
                S.dma("pool", wa[n][:], wada_d[:, :, n * 512:(n + 1) * 512], [], [f"wa{n}"], f"wa{n}")

            S.op("pool", lambda e: e.memset(identf[:], 1.0), [], ["identf"])
            S.op("pool", lambda e: e.affine_select(out=identf[:], in_=identf[:], pattern=[[-1, 128]],
                                                   compare_op=ALU.is_equal, fill=0.0, base=0,
                                                   channel_multiplier=1), ["identf"], ["identf"])
            S.op("dve", lambda e: e.tensor_copy(ident[:], identf[:]), ["identf"], ["ident"])
            S.op("dve", lambda e: e.memset(ones32[:], 1.0), [], ["ones32"])
            S.op("dve", lambda e: e.memset(epsb[:], EPS), [], ["epsb"])

            S.op("act", lambda e: e.activation(out=scb[:], in_=cc[:], func=AF.Silu), ["cc"], ["scb"])
            S.op("dve", lambda e: e.tensor_copy(screp[:], scb[:].unsqueeze(2).broadcast_to([128, 8, 128])),
                 ["scb"], ["screp"])

            for _ in mod_chunks(range(4), wa, ["wa0", "wa1"], scb, screp, None, psf[0], "psf0", None, None, 3):
                pass
            S.op("dve", lambda e: e.tensor_tensor(modc[:, 0:16], psf[0][:, 0:16], badac[:, 0:16], ALU.add),
                 ["psf0", "badac"], ["modc"])
            S.op("dve", lambda e: e.scalar_tensor_tensor(out=gv1[:], in0=modc[:, 8:16], scalar=1.0, in1=gcols[:, 0:8],
                                                        op0=ALU.add, op1=ALU.mult), ["modc", "gcols"], ["gv1"])
            S.op("dve", lambda e: e.tensor_copy(sh1[:], modc[:, 0:8]), ["modc"], ["sh1"])

            S.op("dve", lambda e: e.tensor_copy(posf[:], posi[:]), ["posi"], ["posf"])
            angv = ang[:].rearrange("p (t f) -> p t f", f=48)
            S.op("dve", lambda e: e.tensor_tensor(angv, posf[:].unsqueeze(2).broadcast_to([128, NT, 48]),
                                                  invf[:].unsqueeze(1).broadcast_to([128, NT, 48]), ALU.mult),
                 ["posf", "invf"], ["ang"])
            S.op("dve", lambda e: e.tensor_scalar(tq[:], ang[:], 1.0 / TWO_PI, None, ALU.mult), ["ang"], ["tq"])
            S.op("dve", lambda e: e.tensor_copy(ki[:], tq[:]), ["tq"], ["ki"])
            S.op("dve", lambda e: e.tensor_copy(kf[:], ki[:]), ["ki"], ["kf"])
            S.op("dve", lambda e: e.scalar_tensor_tensor(out=rr[:], in0=kf[:], scalar=-C_HI, in1=ang[:],
                                                        op0=ALU.mult, op1=ALU.add), ["kf", "ang"], ["rr"])
            S.op("dve", lambda e: e.scalar_tensor_tensor(out=rr[:], in0=kf[:], scalar=-C_LO, in1=rr[:],
                                                        op0=ALU.mult, op1=ALU.add), ["kf", "rr"], ["rr"])
            S.op("dve", lambda e: e.tensor_scalar(rc[:], rr[:], PI / 2, -TWO_PI, ALU.is_gt, ALU.mult), ["rr"], ["rc"])
            S.op("dve", lambda e: e.scalar_tensor_tensor(out=rc[:], in0=rr[:], scalar=PI / 2, in1=rc[:],
                                                        op0=ALU.add, op1=ALU.add), ["rr", "rc"], ["rc"])
            for buf, key in ((rr, "rr"), (rc, "rc")):
                S.op("dve", lambda e, buf=buf: e.tensor_scalar(buf[:], buf[:], PI, -PI, ALU.min, ALU.max), [key], [key])
            S.op("act", lambda e: e.activation(out=sinT[:].rearrange("p t f -> p (t f)"), in_=rr[:], func=AF.Sin),
                 ["rr"], ["sinT"])
            S.op("act", lambda e: e.activation(out=cosT[:].rearrange("p t f -> p (t f)"), in_=rc[:], func=AF.Sin),
                 ["rc"], ["cosT"])
            tap("gv1", gv1[:], [128, 8])
            tap("sh1", sh1[:], [128, 8])
            tap("cosT", cosT[:].rearrange("p t f -> p (t f)"), [128, NT * 48])
            tap("sinT", sinT[:].rearrange("p t f -> p (t f)"), [128, NT * 48])
            S.barrier()

        def rstd_from_ssq(st, ssq_ap, out_ap, n, key_in, key_out, scr_ap, key_scr):
            S.op("act", lambda e: e.activation(out=scr_ap, in_=ssq_ap, func=AF.Sqrt, scale=1.0 / n, bias=epsb[:, 0:1]),
                 [key_in], [key_scr])
            S.op("dve", lambda e: e.reciprocal(out_ap, scr_ap), [key_scr], [key_out])

        def norm_transpose(xtile, key_x, rstd_ap, key_r, gv, sh, xn, hdst, key_h, pb, key_pb):
            S.op("dve", lambda e: e.tensor_scalar(xn[:], xtile, rstd_ap, None, ALU.mult), [key_x, key_r], ["xn"])
            for c in range(8):
                bk = psb[c // 4]
                S.op("pe", lambda e, c=c, bk=bk: e.transpose(bk[:, (c % 4) * 128:(c % 4 + 1) * 128], xn[:, c * 128:(c + 1) * 128], ident[:]),
                     ["xn", "ident"], [f"psb{c // 4}"])
            for cc in range(4):
                c = cc
                S.op("act", lambda e, c=c: e.activation(out=hdst[:, c, :], in_=psb[0][:, (c % 4) * 128:(c % 4 + 1) * 128],
                                                        func=AF.Identity, scale=gv[:, c:c + 1], bias=sh[:, c:c + 1]),
                     ["psb0", "gv", "sh"], [f"{key_h}_{c}"])
                c = 4 + cc
                S.op("dve", lambda e, c=c: e.tensor_scalar(hdst[:, c, :], psb[1][:, (c % 4) * 128:(c % 4 + 1) * 128],
                                                           gv[:, c:c + 1], sh[:, c:c + 1], ALU.mult, ALU.add),
                     ["psb1", "gv", "sh"], [f"{key_h}_{c}"])

        def group_norm(src3, nh, dh, gain_ap, dst3, st, tagp, src_keys=None, sfx=""):
            sq, ss, rs, rs2 = st
            sqv = sq[:, 0:nh * dh].rearrange("p (h d) -> p h d", d=dh)
            src_keys = src_keys or [tagp + "src"]
            ksq, kss, krs, krs2 = "sq" + sfx, "ss" + sfx, "rsg" + sfx, "rs2g" + sfx
            S.op("act", lambda e: e.activation(out=sqv, in_=src3, func=AF.Square), src_keys, [ksq])
            yield
            S.op("dve", lambda e: e.tensor_reduce(out=ss[:, 0:nh], in_=sqv, axis=AX.X, op=ALU.add), [ksq], [kss])
            yield
            S.op("act", lambda e: e.activation(out=rs[:, 0:nh], in_=ss[:, 0:nh], func=AF.Sqrt, scale=1.0 / dh, bias=epsb[:, 0:1]),
                 [kss], [krs])
            yield
            S.op("dve", lambda e: e.reciprocal(rs2[:, 0:nh], rs[:, 0:nh]), [krs], [krs2])
            yield
            S.op("dve", lambda e: e.tensor_tensor(sqv, src3, gain_ap.unsqueeze(1).broadcast_to([128, nh, dh]), ALU.mult),
                 src_keys + ["gains"], [ksq])
            yield
            S.op("dve", lambda e: e.tensor_tensor(dst3, sqv, rs2[:, 0:nh].unsqueeze(2).broadcast_to([128, nh, dh]), ALU.mult),
                 [ksq, krs2], [tagp + "dst"])
            yield

        def rope(src3, nh, half, cos_ap, sin_ap, dst3, tmp, key_src, key_dst, sfx=""):
            x1 = src3[:, :, 0:half]
            x2 = src3[:, :, half:2 * half]
            cb = cos_ap.unsqueeze(1).broadcast_to([128, nh, half])
            sbb = sin_ap.unsqueeze(1).broadcast_to([128, nh, half])
            ta = tmp[0][:, 0:nh * half].rearrange("p (h d) -> p h d", d=half)
            tb = tmp[1][:, 0:nh * half].rearrange("p (h d) -> p h d", d=half)
            ka, kb = "ropa" + sfx, "ropb" + sfx
            S.op("dve", lambda e: e.tensor_tensor(ta, x1, cb, ALU.mult), [key_src, "cosT"], [ka])
            yield
            S.op("dve", lambda e: e.tensor_tensor(tb, x2, sbb, ALU.mult), [key_src, "sinT"], [kb])
            yield
            S.op("dve", lambda e: e.tensor_tensor(dst3[:, :, 0:half], ta, tb, ALU.subtract), [ka, kb], [key_dst])
            yield
            S.op("dve", lambda e: e.tensor_tensor(ta, x1, sbb, ALU.mult), [key_src, "sinT"], [ka])
            yield
            S.op("dve", lambda e: e.tensor_tensor(tb, x2, cb, ALU.mult), [key_src, "cosT"], [kb])
            yield
            S.op("dve", lambda e: e.tensor_tensor(dst3[:, :, half:2 * half], ta, tb, ALU.add), [ka, kb], [key_dst])
            yield

        def run_interleaved(gen_fns, max_active=2):
            locks = {}
            pending = list(gen_fns)
            active = []

            def maybe_start():
                if pending and len(active) < max_active and (not active or active[-1]["spawned"]):
                    active.append({"g": pending.pop(0)(), "req": None, "spawned": False})

            maybe_start()
            while active:
                progressed = False
                for ent in list(active):
                    g = ent["g"]
                    if ent["req"] is not None:
                        if locks.get(ent["req"]) is None:
                            locks[ent["req"]] = g
                            ent["req"] = None
                        else:
                            continue
                    try:
                        r = next(g)
                    except StopIteration:
                        active.remove(ent)
                        assert not [k for k, v in locks.items() if v is g], locks
                        progressed = True
                        maybe_start()
                        continue
                    progressed = True
                    if r is None:
                        pass
                    elif r == "spawn":
                        ent["spawned"] = True
                        maybe_start()
                    elif r[0] == "acq":
                        assert locks.get(r[1]) is not g, r
                        if locks.get(r[1]) is None:
                            locks[r[1]] = g
                        else:
                            ent["req"] = r[1]
                    elif r[0] == "rel":
                        assert locks.get(r[1]) is g, r
                        locks[r[1]] = None
                assert progressed, ("interleave deadlock", locks)

        def attention(qT_of, kT_of, v_of, scale, dil, chunk0, Pt, rec, bcs, side=None, LA=2, sbanks=None):
            chunks = []
            gidx = 0
            for h in range(8):
                for QB in range(4):
                    nkt = 4 * (QB + 1)
                    for c in range(nkt):
                        i0 = max(0, c - 4 * QB)
                        chunks.append(dict(h=h, QB=QB, c=c, i0=i0, q0=i0 * 128, n=512 - i0 * 128,
                                           first=(c == 0), last=(c == nkt - 1), g=gidx))
                    gidx += 1
            if sbanks is None:
                sbanks = [(psf[0], "psf0"), (psf[1], "psf1"), (psf[2], "psf2")]
            nS = len(sbanks)
            assert len(Pt) >= nS and LA < nS
            deferred = []

            def emit_S(i):
                ck = chunks[i]
                h, QB, c, q0, n = ck["h"], ck["QB"], ck["c"], ck["q0"], ck["n"]
                qT, kT = qT_of(h), kT_of(h)
                sp_, skey = sbanks[i % nS]
                P = Pt[i % nS]; pkey = f"P{i % nS}"
                S.op("pe", lambda e: e.matmul(sp_[:, 0:n], kT[:, c * 128:(c + 1) * 128],
                                              qT[:, QB * 512 + q0:(QB + 1) * 512], start=True, stop=True),
                     ["KT", "QT"], [skey])
                S.op("act", lambda e: e.activation(out=P[:, 0:n], in_=sp_[:, 0:n], func=AF.Exp, scale=scale),
                     [skey], [pkey])
                if dil:
                    d0 = 4 * QB + ck["i0"] - c
                    S.op("dve", lambda e: e.tensor_tensor(P[:, 0:n], P[:, 0:n], wmask[:, d0 * 128:d0 * 128 + n], ALU.mult),
                         [pkey, "wmask"], [pkey])
                elif c >= 4 * QB:
                    S.op("dve", lambda e: e.tensor_tensor(P[:, 0:128], P[:, 0:128], wmask[:, 16 * 128:17 * 128], ALU.mult),
                         [pkey, "wmask"], [pkey])

            def emit_PV(i, it):
                ck = chunks[i]
                h, QB, c, q0, n, g = ck["h"], ck["QB"], ck["c"], ck["q0"], ck["n"], ck["g"]
                ot = psf[3 + (g % 2)]; okey = f"psf{3 + (g % 2)}"
                P = Pt[i % nS]; pkey = f"P{i % nS}"
                S.op("pe", lambda e: e.matmul(ot[:, q0:512], v_of(h, c), P[:, 0:n], start=ck["first"], stop=ck["last"]),
                     [pkey, "V"], [okey])
                if not ck["last"]:
                    return
                nlo, dlo = (0, 64) if h % 2 == 0 else (64, 0)
                rk = f"rec{g % 2}"
                rr_ = rec[g % 2]
                S.op("act", lambda e: e.activation(out=rr_[dlo:dlo + 1, :], in_=ot[dlo:dlo + 1, :], func=AF.Ln), [okey], [rk])
                S.op("act", lambda e: e.activation(out=rr_[dlo:dlo + 1, :], in_=rr_[dlo:dlo + 1, :], func=AF.Exp, scale=-1.0), [rk], [rk])
                ch = chunk0 + h // 2

                def fin():
                    S.op("pe", lambda e: e.matmul(psf[5][nlo:nlo + 64, :], ones32[dlo:dlo + 1, 0:64], rr_[dlo:dlo + 1, :],
                                                  start=True, stop=True), [rk, "ones32"], ["psf5"])
                    S.op("dve", lambda e: e.tensor_copy(bcs[nlo:nlo + 64, :], psf[5][nlo:nlo + 64, :]), ["psf5"], ["bcs"])
                    S.op("dve", lambda e: e.tensor_tensor(mixT[nlo:nlo + 64, ch, QB * 512:(QB + 1) * 512], ot[nlo:nlo + 64, :],
                                                          bcs[nlo:nlo + 64, :], ALU.mult), [okey, "bcs"], ["mixT"])
                deferred.append((it + 2, fin))

            nC = len(chunks)
            for it in range(nC + LA + 3):
                if side is not None and it % 3 == 2:
                    next(side, None)
                if it < nC:
                    emit_S(it)
                j = it - LA
                if 0 <= j < nC:
                    emit_PV(j, it)
                while deferred and deferred[0][0] <= it:
                    deferred.pop(0)[1]()
            assert not deferred

        def v_layout_store(Vx, t, src_ps_list, key_src_list, e_act=True):
            pass

        if stop_after != "0":
            QT = R1[:, 0:16384].rearrange("p (h n) -> p h n", n=S_LEN)
            KT = R1[:, 16384:32768].rearrange("p (h n) -> p h n", n=S_LEN)
            Vm = R1[:, 32768:45056].rearrange("p (t j c) -> p t j c", j=4, c=192)
            QdT = R1[:, 0:8192].rearrange("p (j n) -> p j n", n=S_LEN)
            QdT1 = R1[:, 16384:24576].rearrange("p (j n) -> p j n", n=S_LEN)
            KdT = R1[:, 8192:16384].rearrange("p (j n) -> p j n", n=S_LEN)
            Vd = Vm
            S.op("pool", lambda e: e.memset(Vm[:, :, :, 64:128], 1.0), [], ["V"])

            with ExitStack() as L2:
                winM = sb(L2, "winM", [128, 8, 800], BF16)
                wqb = sb(L2, "wqb", [128, 4, 768], BF16)
                wkvb = sb(L2, "wkvb", [128, 2, 1024], BF16)
                xt = [sb(L2, f"xt{i}", [128, D], F32) for i in range(2)]
                xn = sb(L2, "xn", [128, D], BF16)
                hT = [sb(L2, f"hT{i}", [128, 8, 128], BF16) for i in range(2)]
                st_ssq = sb(L2, "st_ssq", [128, 8], F32)
                st_rs = sb(L2, "st_rs", [128, 8], F32)
                st_rs2 = sb(L2, "st_rs2", [128, 8], F32)
                sqs = [sb(L2, f"sq{i}", [128, 1280], F32) for i in range(2)]
                sss = [sb(L2, f"ss{i}", [128, 24], F32) for i in range(2)]
                rsgs = [sb(L2, f"rsg{i}", [128, 24], F32) for i in range(2)]
                rs2gs = [sb(L2, f"rs2g{i}", [128, 24], F32) for i in range(2)]
                qln = sb(L2, "qln", [128, 512], BF16)
                kvn = sb(L2, "kvn", [128, 256], BF16)
                latT = sb(L2, "latT", [128, 6, 128], BF16)
                kpe = sb(L2, "kpe", [128, 32], F32)
                kpe2 = sb(L2, "kpe2", [128, 32], F32)
                kper = [sb(L2, f"kper{i}", [128, 32], BF16) for i in range(2)]
                ropts = [[sb(L2, f"ropt{p}{i}", [128, 256], F32) for i in range(2)] for p in range(2)]

                S.dma("pool", winM[:], winM_d, [], ["winM"], "winM")
                S.dma("pool", wqb[:], wqb_d, [], ["wqb"], "wqb")
                S.dma("pool", wkvb[:], wkvb_d, [], ["wkvb"], "wkvb")
                S.dma("sp", xt[0][:], x_d[0:128, :], [], ["xt0"], "xt0")
                def tileM(t):
                    b = t % 2
                    sq = sqs[b]
                    gst = (sqs[b], sss[b], rsgs[b], rs2gs[b])
                    ropt = ropts[b]
                    junk = sqs[b][:, 0:512].bitcast(BF16)
                    yield ("acq", "X")
                    if t + 1 < NT:
                        S.dma("sp", xt[1 - b][:], x_d[(t + 1) * 128:(t + 2) * 128, :], [], [f"xt{1 - b}"], f"xt{1 - b}")
                    S.op("act", lambda e, b=b: e.activation(out=junk, in_=xt[b][:], func=AF.Square,
                                                            accum_out=st_ssq[:, 0:1]), [f"xt{b}"], [f"sq{b}", "ssq0"])
                    rstd_from_ssq(None, st_ssq[:, 0:1], st_rs[:, 0:1], D, "ssq0", "rs0", st_ssq[:, 1:2], "ssq0b")
                    yield
                    yield ("acq", "psb")
                    norm_transpose(xt[b][:], f"xt{b}", st_rs[:, 0:1], "rs0", gv1, sh1, xn, hT[b], f"hT{b}", psb[0], "psb0")
                    yield ("rel", "psb")
                    yield ("rel", "X")
                    yield "spawn"
                    yield ("acq", "P")
                    for k in range(8):
                        S.op("pe", lambda e, k=k, b=b: e.matmul(psf[0][:, :], hT[b][:, k, :], winM[:, k, 0:512],
                                                                start=(k == 0), stop=(k == 7)), [f"hT{b}_{k}", "winM"], ["psf0"])
                    for k in range(8):
                        S.op("pe", lambda e, k=k, b=b: e.matmul(psf[1][:, 0:288], hT[b][:, k, :], winM[:, k, 512:800],
                                                                start=(k == 0), stop=(k == 7)), [f"hT{b}_{k}", "winM"], ["psf1"])
                    yield
                    S.op("act", lambda e: e.activation(out=sq[:, 0:512], in_=psf[0][:, :], func=AF.Square,
                                                       accum_out=st_ssq[:, 4:5]), ["psf0"], [f"sq{b}", "ssqP0"])
                    yield
                    S.op("act", lambda e: e.activation(out=sq[:, 512:768], in_=psf[1][:, 0:256], func=AF.Square, scale=2.0 ** 0.5,
                                                       accum_out=st_ssq[:, 5:6]), ["psf1"], [f"sq{b}", "ssqP1"])
                    yield
                    S.op("act", lambda e: e.activation(out=sq[:, 768:800], in_=psf[1][:, 256:288], func=AF.Square, scale=4.0,
                                                       accum_out=st_ssq[:, 6:7]), ["psf1"], [f"sq{b}", "ssqP2"])
                    yield
                    S.op("act", lambda e: e.activation(out=st_rs[:, 4:7], in_=st_ssq[:, 4:7], func=AF.Sqrt, scale=1.0 / 512, bias=epsb[:, 0:1]),
                         ["ssqP0", "ssqP1", "ssqP2"], ["rsP"])
                    yield
                    S.op("dve", lambda e: e.reciprocal(st_rs2[:, 4:7], st_rs[:, 4:7]), ["rsP"], ["rs2P"])
                    yield
                    S.op("dve", lambda e: e.scalar_tensor_tensor(out=qln[:], in0=psf[0][:, :], scalar=st_rs2[:, 4:5],
                                                                in1=gains[:, G_QLAT:G_QLAT + 512], op0=ALU.mult, op1=ALU.mult),
                         ["psf0", "rs2P", "gains"], ["qln"])
                    yield
                    S.op("dve", lambda e: e.scalar_tensor_tensor(out=kvn[:], in0=psf[1][:, 0:256], scalar=st_rs2[:, 5:6],
                                                                in1=gains[:, G_KVLAT:G_KVLAT + 256], op0=ALU.mult, op1=ALU.mult),
                         ["psf1", "rs2P", "gains"], ["kvn"])
                    yield
                    S.op("dve", lambda e: e.scalar_tensor_tensor(out=kpe2[:], in0=psf[1][:, 256:288], scalar=st_rs2[:, 6:7],
                                                                in1=gains[:, G_KP:G_KP + 32], op0=ALU.mult, op1=ALU.mult),
                         ["psf1", "rs2P", "gains"], ["kpedst"])
                    yield
                    yield from rope(kpe2[:].rearrange("p (h d) -> p h d", d=32), 1, 16, cosT[:, t, 0:16], sinT[:, t, 0:16],
                         kper[b][:].rearrange("p (h d) -> p h d", d=32), ropt, "kpedst", f"kper{b}", sfx=str(b))
                    yield
                    yield ("acq", "Q")
                    yield ("acq", "psb")
                    for c in range(4):
                        S.op("pe", lambda e, c=c: e.transpose(psb[1][:, c * 128:(c + 1) * 128], qln[:, c * 128:(c + 1) * 128], ident[:]),
                             ["qln", "ident"], ["psb1"])
                    for c in range(2):
                        S.op("pe", lambda e, c=c: e.transpose(psb[1][:, (4 + c) * 128:(5 + c) * 128], kvn[:, c * 128:(c + 1) * 128], ident[:]),
                             ["kvn", "ident"], ["psb1"])
                    S.op("act", lambda e: e.activation(out=latT[:].rearrange("p c n -> p (c n)"), in_=psb[1][:, 0:768], func=AF.Copy),
                         ["psb1"], ["latT"])
                    yield ("rel", "psb")
                    yield ("rel", "P")
                    for (pp, key, c0, cn) in ((psf[2], "psf2", 0, 512), (psf[3], "psf3", 512, 256)):
                        for k in range(4):
                            S.op("pe", lambda e, pp=pp, k=k, c0=c0, cn=cn: e.matmul(pp[:, 0:cn], latT[:, k, :], wqb[:, k, c0:c0 + cn],
                                                                                    start=(k == 0), stop=(k == 3)), ["latT", "wqb"], [key])
                    for (pp, key, c0) in ((psf[4], "psf4", 0), (psf[5], "psf5", 512)):
                        for k in range(2):
                            S.op("pe", lambda e, pp=pp, k=k, c0=c0: e.matmul(pp[:, :], latT[:, 4 + k, :], wkvb[:, k, c0:c0 + 512],
                                                                             start=(k == 0), stop=(k == 1)), ["latT", "wkvb"], [key])
                    yield
                    qf_b, kvf_b, qpn_b, Qtok_b, Ktok_b = qfs[b], kvfs[b], qpns[b], Qtoks[b], Ktoks[b]
                    pb = str(b)
                    S.op("act", lambda e: e.activation(out=qf_b[:, 0:512], in_=psf[2][:, :], func=AF.Copy), ["psf2"], ["qsrcA" + pb])
                    S.op("dve", lambda e: e.tensor_copy(qf_b[:, 512:768], psf[3][:, 0:256]), ["psf3"], ["qsrcB" + pb])
                    S.op("act", lambda e: e.activation(out=kvf_b[:, 0:512], in_=psf[4][:, :], func=AF.Copy), ["psf4"], ["knsrcA" + pb])
                    S.op("dve", lambda e: e.tensor_copy(kvf_b[:, 512:1024], psf[5][:, :]), ["psf5"], ["knsrcB" + pb])
                    yield ("rel", "Q")
                    q3 = qf_b.rearrange("p (h d) -> p h d", d=96)
                    kv3 = kvf_b.rearrange("p (h d) -> p h d", d=128)
                    qsk = ["qsrcA" + pb, "qsrcB" + pb]
                    ksk = ["knsrcA" + pb, "knsrcB" + pb]
                    sqb, ssb, rsb, rs2b = gst
                    sqA = sqb[:, 0:512].rearrange("p (h d) -> p h d", d=64)
                    sqB = sqb[:, 512:768].rearrange("p (h d) -> p h d", d=32)
                    sqC = sqb[:, 768:1280].rearrange("p (h d) -> p h d", d=64)
                    ksq = "sq" + pb
                    S.op("act", lambda e: e.activation(out=sqA, in_=q3[:, :, 0:64], func=AF.Square), qsk, [ksq + "A"])
                    yield
                    S.op("act", lambda e: e.activation(out=sqB, in_=q3[:, :, 64:96], func=AF.Square, scale=2.0 ** 0.5), qsk, [ksq + "B"])
                    yield
                    S.op("act", lambda e: e.activation(out=sqC, in_=kv3[:, :, 0:64], func=AF.Square), ksk, [ksq + "C"])
                    yield
                    S.op("dve", lambda e: e.tensor_reduce(out=ssb[:, 0:8], in_=sqA, axis=AX.X, op=ALU.add), [ksq + "A"], ["ss" + pb])
                    yield
                    S.op("dve", lambda e: e.tensor_reduce(out=ssb[:, 8:16], in_=sqB, axis=AX.X, op=ALU.add), [ksq + "B"], ["ss" + pb])
                    yield
                    S.op("dve", lambda e: e.tensor_reduce(out=ssb[:, 16:24], in_=sqC, axis=AX.X, op=ALU.add), [ksq + "C"], ["ss" + pb])
                    yield
                    S.op("act", lambda e: e.activation(out=rsb[:, 0:24], in_=ssb[:, 0:24], func=AF.Sqrt, scale=1.0 / 64, bias=epsb[:, 0:1]),
                         ["ss" + pb], ["rsg" + pb])
                    yield
                    S.op("dve", lambda e: e.tensor_tensor(sqA, q3[:, :, 0:64], gains[:, G_QN:G_QN + 64].unsqueeze(1).broadcast_to([128, 8, 64]), ALU.mult),
                         qsk + ["gains"], [ksq + "A"])
                    yield
                    S.op("dve", lambda e: e.tensor_tensor(sqB, q3[:, :, 64:96], gains[:, G_QP:G_QP + 32].unsqueeze(1).broadcast_to([128, 8, 32]), ALU.mult),
                         qsk + ["gains"], [ksq + "B"])
                    yield
                    S.op("dve", lambda e: e.tensor_tensor(sqC, kv3[:, :, 0:64], gains[:, G_KN:G_KN + 64].unsqueeze(1).broadcast_to([128, 8, 64]), ALU.mult),
                         ksk + ["gains"], [ksq + "C"])
                    yield
                    S.op("dve", lambda e: e.reciprocal(rs2b[:, 0:24], rsb[:, 0:24]), ["rsg" + pb], ["rs2g" + pb])
                    yield
                    S.op("dve", lambda e: e.tensor_tensor(qpn_b, sqB, rs2b[:, 8:16].unsqueeze(2).broadcast_to([128, 8, 32]), ALU.mult),
                         [ksq + "B", "rs2g" + pb, ksq], ["qp" + pb + "dst"])
                    yield
                    S.op("dve", lambda e: e.tensor_tensor(Qtok_b[:, :, 0:64], sqA, rs2b[:, 0:8].unsqueeze(2).broadcast_to([128, 8, 64]), ALU.mult),
                         [ksq + "A", "rs2g" + pb, ksq], ["q" + pb + "dst"])
                    yield
                    S.op("dve", lambda e: e.tensor_tensor(Ktok_b[:, :, 0:64], sqC, rs2b[:, 16:24].unsqueeze(2).broadcast_to([128, 8, 64]), ALU.mult),
                         [ksq + "C", "rs2g" + pb, ksq], ["kn" + pb + "dst"])
                    yield
                    yield from rope(qpn_b, 8, 16, cosT[:, t, 0:16], sinT[:, t, 0:16], Qtok_b[:, :, 64:96], ropt, "qp" + pb + "dst", "q" + pb + "dst", sfx=pb)
                    yield
                    S.op("dve", lambda e: e.tensor_copy(Ktok_b[:, :, 64:96], kper[b][:].unsqueeze(1).broadcast_to([128, 8, 32])),
                         [f"kper{b}"], ["kn" + pb + "dst"])
                    vsrc = kvf_b.rearrange("p (j e d) -> p j e d", e=2, d=128)[:, :, :, 64:128]
                    vdst = Vm[:, t, :, :].rearrange("p j (e d) -> p j e d", d=64)[:, :, 0:3:2, :]
                    S.op("pool", lambda e, vsrc=vsrc, vdst=vdst: e.tensor_copy(vdst, vsrc), ksk, ["V"])
                    yield
                    yield ("acq", "psb")
                    for h in range(8):
                        S.op("pe", lambda e, h=h: e.transpose(psb[0][0:96, h * 128:(h + 1) * 128], Qtok_b[:, h, :], ident[:]),
                             ["q" + pb + "dst", "ident"], ["psb0"])
                    S.op("act", lambda e, t=t: e.activation(out=QT[0:96, :, t * 128:(t + 1) * 128],
                                                            in_=psb[0][0:96, :].rearrange("p (h n) -> p h n", n=128), func=AF.Copy),
                         ["psb0"], ["QT"])
                    yield
                    for h in range(8):
                        S.op("pe", lambda e, h=h: e.transpose(psb[1][0:96, h * 128:(h + 1) * 128], Ktok_b[:, h, :], ident[:]),
                             ["kn" + pb + "dst", "ident"], ["psb1"])
                    S.op("dve", lambda e, t=t: e.tensor_copy(KT[0:96, :, t * 128:(t + 1) * 128],
                                                             psb[1][0:96, :].rearrange("p (h n) -> p h n", n=128)),
                         ["psb1"], ["KT"])
                    yield ("rel", "psb")

                R2f = R2[:, :].bitcast(F32)
                qfs = [R2f[:, p * 2048 + 0:p * 2048 + 768] for p in range(2)]
                kvfs = [R2f[:, p * 2048 + 768:p * 2048 + 1792] for p in range(2)]
                qpns = [R2f[:, p * 2048 + 1792:p * 2048 + 2048].rearrange("p (h d) -> p h d", d=32) for p in range(2)]
                Qtoks = [R2[:, 8192 + p * 1536:8192 + p * 1536 + 768].rearrange("p (h d) -> p h d", d=96) for p in range(2)]
                Ktoks = [R2[:, 8192 + p * 1536 + 768:8192 + (p + 1) * 1536].rearrange("p (h d) -> p h d", d=96) for p in range(2)]
                run_interleaved([(lambda t=t: tileM(t)) for t in range(NT)])
                tap("QT", QT[0:96, :, :].rearrange("p h n -> p (h n)"), [96, 8 * S_LEN], BF16)
                tap("KT", KT[0:96, :, :].rearrange("p h n -> p (h n)"), [96, 8 * S_LEN], BF16)
                tap("Vm", Vm.rearrange("p t j c -> p (t j c)"), [128, 16 * 768], BF16)
                S.barrier()

        if stop_after not in ("0", "AM"):
            with ExitStack() as L2:
                winD = sb(L2, "winD", [128, 8, 1536], BF16)
                Pt = [sb(L2, f"Pt{i}", [128, 512], BF16) for i in range(4)]
                rec = [sb(L2, f"rec{i}", [128, 512], F32) for i in range(2)]
                bcs = sb(L2, "bcs", [128, 512], F32)
                S.dma("pool", winD[:], winD_d, [], ["winD"], "winD")
                with ExitStack() as Lmod:
                    wa2 = [sb(Lmod, f"wa2_{i}", [128, 8, 512], BF16) for i in range(2)]
                    badag2 = sb(Lmod, "badag2", [128, 2048], F32)
                    cc2 = sb(Lmod, "cc2", [128, 8], F32)
                    scb2 = sb(Lmod, "scb2", [128, 8], BF16)
                    screp2 = sb(Lmod, "screp2", [128, 8, 128], BF16)
                    badac2 = sb(Lmod, "badac2", [128, 48], F32)
                    gcols2 = sb(Lmod, "gcols2", [128, 16], F32)
                    modc2 = sb(Lmod, "modc2", [128, 48], F32)
                    pcol2 = psb[0][:, :].bitcast(F32)
                    pg2 = psb[1][:, :].bitcast(F32)

                    def side_mod():
                        S.dma("sp", cc2[:], ccol_d, [], ["cc2"], "m0")
                        S.dma("sp", badac2[:], badac_d, [], ["badac2"], "m1")
                        S.dma("sp", gcols2[:], gcols_d, [], ["gcols2"], "m2")
                        S.dma("sp", badag2[:], badag_d.partition_broadcast(128), [], ["badag"], "m3")
                        for n in (4, 5):
                            S.dma("pool", wa2[n % 2][:], wada_d[:, :, n * 512:(n + 1) * 512], [], [f"wa2{n % 2}"], f"wa2{n % 2}")
                        for _ in range(12):
                            yield
                        S.op("act", lambda e: e.activation(out=scb2[:], in_=cc2[:], func=AF.Silu), ["cc2"], ["scb"])
                        yield
                        S.op("dve", lambda e: e.tensor_copy(screp2[:], scb2[:].unsqueeze(2).broadcast_to([128, 8, 128])),
                             ["scb"], ["screp"])
                        for _ in range(4):
                            yield
                        yield from mod_chunks(range(4, 12), wa2, ["wa20", "wa21"], scb2, screp2, badag2, pcol2, "psb0", pg2, "psb1", 11)
                        S.op("dve", lambda e: e.tensor_tensor(modc2[:, 24:40], pcol2[:, 24:40], badac2[:, 24:40], ALU.add),
                             ["psb0", "badac2"], ["modc2"])
                        yield
                        S.op("dve", lambda e: e.scalar_tensor_tensor(out=gv2[:], in0=modc2[:, 32:40], scalar=1.0, in1=gcols2[:, 8:16],
                                                                    op0=ALU.add, op1=ALU.mult), ["modc2", "gcols2"], ["gv2"])
                        S.op("dve", lambda e: e.tensor_copy(sh2[:], modc2[:, 24:32]), ["modc2"], ["sh2"])

                    sidegen = side_mod()
                    attention(lambda h: QT[0:96, h, :], lambda h: KT[0:96, h, :],
                              lambda h, c: Vm[:, c, h // 2, (h % 2) * 64:(h % 2) * 64 + 128],
                              96 ** -0.5, False, 0, Pt, rec, bcs, side=sidegen)
                    for _ in sidegen:
                        pass
                    tap("g12", g12[:], [128, 2048])
                tap("mixM", mixT[:, 0:4, :].rearrange("p c n -> p (c n)"), [128, 4 * S_LEN], BF16)
                S.barrier()
                if stop_after != "M":
                    with ExitStack() as L3:
                        xt = [sb(L3, f"xtd{i}", [128, D], F32) for i in range(2)]
                        xn = sb(L3, "xnd", [128, D], BF16)
                        hT = [sb(L3, f"hTd{i}", [128, 8, 128], BF16) for i in range(2)]
                        st_ssq = sb(L3, "std_ssq", [128, 4], F32)
                        st_rs = sb(L3, "std_rs", [128, 4], F32)
                        junkd = sb(L3, "junkd", [128, D], BF16)
                        sqA = sb(L3, "sqA", [128, 1024], F32)
                        ssA = sb(L3, "ssA", [128, 16], F32)
                        rsA = sb(L3, "rsA", [128, 16], F32)
                        rs2A = sb(L3, "rs2A", [128, 16], F32)
                        ropt2 = [sb(L3, f"ropt2{i}", [128, 512], F32) for i in range(2)]
                        tokA = sb(L3, "tokA", [128, 1024], BF16)
                        R2f = R2[:, :].bitcast(F32)
                        qk_f = R2f[:, 4096:5120]
                        qk_n = R2f[:, 5120:6144]
                        S.op("pool", lambda e: e.memset(QdT[64:128, :, :], 0.0), [], ["QT"])
                        S.op("pool", lambda e: e.memset(QdT1[0:64, :, :], 0.0), [], ["QT"])
                        S.dma("sp", xt[0][:], x_d[0:128, :], [], ["xt0"], "xt0")

                        def tileD(t):
                            b = t % 2
                            yield ("acq", "X")
                            if t + 1 < NT:
                                S.dma("sp", xt[1 - b][:], x_d[(t + 1) * 128:(t + 2) * 128, :], [], [f"xt{1 - b}"], f"xt{1 - b}")
                            S.op("act", lambda e: e.activation(out=junkd[:], in_=xt[b][:], func=AF.Square,
                                                               accum_out=st_ssq[:, 0:1]), [f"xt{b}"], ["junkd", "ssq0"])
                            rstd_from_ssq(None, st_ssq[:, 0:1], st_rs[:, 0:1], D, "ssq0", "rs0", st_ssq[:, 1:2], "ssq0b")
                            yield
                            yield ("acq", "psb")
                            norm_transpose(xt[b][:], f"xt{b}", st_rs[:, 0:1], "rs0", gv1, sh1, xn, hT[b], f"hT{b}", psb[0], "psb0")
                            yield ("rel", "psb")
                            yield ("rel", "X")
                            yield "spawn"
                            yield ("acq", "P")
                            for (pp, key, c0) in ((psf[0], "psf0", 0), (psf[1], "psf1", 512), (psf[2], "psf2", 1024)):
                                for k in range(8):
                                    S.op("pe", lambda e, pp=pp, k=k, c0=c0: e.matmul(pp[:, :], hT[b][:, k, :], winD[:, k, c0:c0 + 512],
                                                                                   start=(k == 0), stop=(k == 7)),
                                         [f"hT{b}_{k}", "winD"], [key])
                                yield
                            yield ("acq", "N")
                            S.op("act", lambda e: e.activation(out=qk_f[:, 0:512], in_=psf[0][:, :], func=AF.Copy), ["psf0"], ["qkfA"])
                            S.op("dve", lambda e: e.tensor_copy(qk_f[:, 512:1024], psf[1][:, :]), ["psf1"], ["qkfB"])
                            vsrc = psf[2][:, :].rearrange("p (j e d) -> p j e d", e=2, d=64)
                            vdst = Vd[:, t, :, :].rearrange("p j (e d) -> p j e d", d=64)[:, :, 0:3:2, :]
                            S.op("act", lambda e: e.activation(out=vdst, in_=vsrc, func=AF.Copy), ["psf2"], ["V"])
                            yield ("rel", "P")
                            src3 = qk_f.rearrange("p (h d) -> p h d", d=64)
                            src4 = qk_f.rearrange("p (s h d) -> p s h d", s=2, h=8)
                            sq3 = sqA[:, :].rearrange("p (h d) -> p h d", d=64)
                            sq4 = sqA[:, :].rearrange("p (s h d) -> p s h d", s=2, h=8)
                            dst3 = qk_n.rearrange("p (h d) -> p h d", d=64)
                            g4 = gains[:, G_DQ:G_DQ + 128].rearrange("p (s d) -> p s d", d=64).unsqueeze(2).broadcast_to([128, 2, 8, 64])
                            S.op("act", lambda e: e.activation(out=sq3, in_=src3, func=AF.Square), ["qkfA", "qkfB"], ["sqA"])
                            yield
                            S.op("dve", lambda e: e.tensor_reduce(out=ssA[:, :], in_=sq3, axis=AX.X, op=ALU.add), ["sqA"], ["ssA"])
                            yield
                            S.op("act", lambda e: e.activation(out=rsA[:, :], in_=ssA[:, :], func=AF.Sqrt, scale=1.0 / 64, bias=epsb[:, 0:1]), ["ssA"], ["rsA"])
                            yield
                            S.op("dve", lambda e: e.reciprocal(rs2A[:, :], rsA[:, :]), ["rsA"], ["rs2A"])
                            yield
                            S.op("dve", lambda e: e.tensor_tensor(sq4, src4, g4, ALU.mult), ["qkfA", "qkfB", "gains"], ["sqA"])
                            yield
                            S.op("dve", lambda e: e.tensor_tensor(dst3, sq3, rs2A[:, :].unsqueeze(2).broadcast_to([128, 16, 64]), ALU.mult),
                                 ["sqA", "rs2A"], ["qkn"])
                            yield
                            yield from rope(dst3, 16, 32, cos_ap=cosT[:, t, 16:48], sin_ap=sinT[:, t, 16:48],
                                            dst3=tokA[:, :].rearrange("p (h d) -> p h d", d=64), tmp=ropt2, key_src="qkn", key_dst="tokA", sfx="D")
                            yield
                            yield ("acq", "psb")
                            for j in range(4):
                                S.op("pe", lambda e, j=j: e.transpose(psb[0][:, j * 128:(j + 1) * 128], tokA[:, j * 128:(j + 1) * 128], ident[:]),
                                     ["tokA", "ident"], ["psb0"])
                            for j in range(4):
                                S.op("pe", lambda e, j=j: e.transpose(psb[1][:, j * 128:(j + 1) * 128], tokA[:, 512 + j * 128:512 + (j + 1) * 128], ident[:]),
                                     ["tokA", "ident"], ["psb1"])
                            yield
                            S.op("act", lambda e: e.activation(
                                out=QdT[0:64, :, t * 128:(t + 1) * 128],
                                in_=psb[0][0:64, 0:512].rearrange("p (j n) -> p j n", n=128), func=AF.Copy), ["psb0"], ["QT"])
                            S.op("act", lambda e: e.activation(
                                out=QdT1[64:128, :, t * 128:(t + 1) * 128],
                                in_=psb[0][64:128, 0:512].rearrange("p (j n) -> p j n", n=128), func=AF.Copy), ["psb0"], ["QT"])
                            S.op("dve", lambda e: e.tensor_copy(
                                KdT[:, :, t * 128:(t + 1) * 128],
                                psb[1][:, 0:512].rearrange("p (j n) -> p j n", n=128)), ["psb1"], ["KT"])
                            yield ("rel", "psb")
                            yield ("rel", "N")

                        run_interleaved([(lambda t=t: tileD(t)) for t in range(NT)])
                        tap("QdT", QdT.rearrange("p j n -> p (j n)"), [128, 4 * S_LEN], BF16)
                        tap("KdT", KdT.rearrange("p j n -> p (j n)"), [128, 4 * S_LEN], BF16)
                        tap("Vd", Vd.rearrange("p t j c -> p (t j c)"), [128, 16 * 768], BF16)
                        S.barrier()
                    attention(lambda h: (QdT if h % 2 == 0 else QdT1)[:, h // 2, :],
                              lambda h: KdT[:, h // 2, :],
                              lambda h, c: Vd[:, c, h // 2, (h % 2) * 64:(h % 2) * 64 + 128],
                              64 ** -0.5, True, 4, Pt, rec, bcs, LA=3,
                              sbanks=[(psf[0], "psf0"), (psf[1], "psf1"), (psf[2], "psf2"), (psb[0][:, :].bitcast(F32), "psb0")])
                    tap("mixT", mixT.rearrange("p c n -> p (c n)"), [128, 8 * S_LEN], BF16)
                    S.barrier()

        if stop_after in ("O", "F"):
            x1 = R1[:, 0:32768].bitcast(F32).rearrange("p (t n) -> p t n", n=D)
            h2T = R1[:, 32768:40960].rearrange("p (c n) -> p c n", n=1024)
            wub = [R1[:, 40960 + i * 2048:40960 + (i + 1) * 2048].rearrange("p (k g c) -> p k g c", g=2, c=128) for i in range(2)]
            wub += [R2[:, 11264 + i * 2048:11264 + (i + 1) * 2048].rearrange("p (k g c) -> p k g c", g=2, c=128) for i in range(2)]
            NWB = 4

            def h2_norm(t, junk_ap, xnb, xkey, ssq, rs):
                S.op("act", lambda e: e.activation(out=junk_ap, in_=x1[:, t, :], func=AF.Square, accum_out=ssq[:, 0:1]),
                     [f"x1_{t}"], ["junkh", "sg", "ssq0"])
                rstd_from_ssq(None, ssq[:, 0:1], rs[:, 0:1], D, "ssq0", "rs0", ssq[:, 1:2], "ssq0b")
                S.op("dve", lambda e: e.tensor_scalar(xnb[:], x1[:, t, :], rs[:, 0:1], None, ALU.mult), [f"x1_{t}", "rs0"], [xkey])

            def h2_trans(t, xnb, xkey):
                tt = t % 8
                hdst = h2T[:, :, tt * 128:(tt + 1) * 128]
                for c in range(8):
                    S.op("pe", lambda e, c=c: e.transpose(psb[c // 4][:, (c % 4) * 128:(c % 4 + 1) * 128],
                                                          xnb[:, c * 128:(c + 1) * 128], ident[:]), [xkey, "ident"], [f"psb{c // 4}"])
                for cc in range(4):
                    S.op("act", lambda e, c=cc: e.activation(out=hdst[:, c, :], in_=psb[0][:, (c % 4) * 128:(c % 4 + 1) * 128],
                                                             func=AF.Identity, scale=gv2[:, c:c + 1], bias=sh2[:, c:c + 1]),
                         ["psb0"], [f"h2T_{cc}"])
                    S.op("dve", lambda e, c=4 + cc: e.tensor_scalar(hdst[:, c, :], psb[1][:, (c % 4) * 128:(c % 4 + 1) * 128],
                                                                    gv2[:, c:c + 1], sh2[:, c:c + 1], ALU.mult, ALU.add),
                         ["psb1"], [f"h2T_{4 + cc}"])
            with ExitStack() as L2:
                wo = sb(L2, "wo", [128, 8, 1024], BF16)
                xt = [sb(L2, f"xto{i}", [128, D], F32) for i in range(2)]
                tmpo = sb(L2, "tmpo", [128, 512], F32)
                junk_o = sb(L2, "junko", [128, D], BF16)
                xn_o = [sb(L2, f"xno{i}", [128, D], BF16) for i in range(3)]
                sso = sb(L2, "sso", [128, 4], F32)
                rso = sb(L2, "rso", [128, 4], F32)
                S.dma("pool", wo[:], wo_d, [], ["wo"], "wo")
                S.dma("sp", xt[0][:], x_d[0:128, :], [], ["xt0"], "xt0")
                for t in range(NT):
                    b = t % 2
                    if t + 1 < NT:
                        S.dma("sp", xt[1 - b][:], x_d[(t + 1) * 128:(t + 2) * 128, :], [], [f"xt{1 - b}"], f"xt{1 - b}")
                    for nh in range(2):
                        pp = psf[(2 * t + nh) % 4]
                        key = f"psf{(2 * t + nh) % 4}"
                        for k in range(8):
                            S.op("pe", lambda e, pp=pp, k=k, t=t, nh=nh: e.matmul(pp[:, :], mixT[:, k, t * 128:(t + 1) * 128],
                                                                                  wo[:, k, nh * 512:(nh + 1) * 512],
                                                                                  start=(k == 0), stop=(k == 7)), ["mixT", "wo"], [key])
                        S.op("dve", lambda e, pp=pp, nh=nh: e.tensor_tensor(tmpo[:], pp[:, :], g12[:, nh * 512:(nh + 1) * 512], ALU.mult),
                             [key, "g12"], ["tmpo"])
                        S.op("pool", lambda e, t=t, nh=nh, b=b: e.tensor_tensor(x1[:, t, nh * 512:(nh + 1) * 512], tmpo[:],
                                                                               xt[b][:, nh * 512:(nh + 1) * 512], ALU.add),
                             ["tmpo", f"xt{b}"], [f"x1_{t}"])
                    if t < 8:
                        h2_norm(t, junk_o[:], xn_o[t % 3], f"xno{t % 3}", sso, rso)
                    if 2 <= t <= 9:
                        h2_trans(t - 2, xn_o[(t - 2) % 3], f"xno{(t - 2) % 3}")
                    if t == 10:
                        for i in range(2):
                            S.dma("pool", wub[i], wup_d[i], [], [f"wub{i}"], f"wub{i}")
                tap("x1", R1[:, 0:32768].bitcast(F32), [128, 16 * D])
                S.barrier()

        if stop_after == "F":
            aT = R2[:, 0:11264].rearrange("p (j n) -> p j n", n=1024)
            with ExitStack() as L2:
                wdn = sb(L2, "wdn", [128, 11, 1024], BF16)
                xn_f = [sb(L2, f"xnf{i}", [128, D], BF16) for i in range(2)]
                st_ssq = sb(L2, "stf_ssq", [128, 4], F32)
                st_rs = sb(L2, "stf_rs", [128, 4], F32)
                ug = sb(L2, "ug", [128, 1026], F32)
                uv = sb(L2, "uv", [128, 1026], F32)
                yg = sb(L2, "yg", [128, 1024], F32)
                yv = sb(L2, "yv", [128, 1024], F32)
                sg = sb(L2, "sg", [128, 1024], F32)
                halo = sb(L2, "halo", [128, 2 * NJ, 2], F32)
                tmpf = sb(L2, "tmpf", [128, 512], F32)
                S.dma("pool", wub[2], wup_d[2], [], ["wub2"], "wub2")
                S.dma("pool", wdn[:], wdn_d[:, 0:11, :], [], ["wdn"], "wdn")
                junk_f = sg[:, 0:512].bitcast(BF16)
                for H in range(2):
                    for JG in range(2):
                        for jj in range(11):
                            j = JG * 11 + jj
                            seq = (H * 2 + JG) * 11 + jj
                            wb = wub[seq % NWB]
                            wkey = f"wub{seq % NWB}"
                            if seq + NWB - 1 < 44:
                                nk = f"wub{(seq + NWB - 1) % NWB}"
                                S.dma("pool", wub[(seq + NWB - 1) % NWB], wup_d[(seq + NWB - 1) % 22], [], [nk], nk)
                            banks = {}
                            for tb in range(2):
                                for g_ in range(2):
                                    bi = (4 * seq + 2 * tb + g_) % 6
                                    banks[(tb, g_)] = bi
                                    for k in range(8):
                                        S.op("pe", lambda e, k=k, g_=g_, tb=tb, bi=bi: e.matmul(
                                            psf[bi][:, :], wb[:, k, g_, :], h2T[:, k, tb * 512:(tb + 1) * 512],
                                            start=(k == 0), stop=(k == 7)), [wkey, f"h2T_{k}"], [f"psf{bi}"])
                            for (g_, usb, ukey, ysb, ykey) in ((0, ug, "ug", yg, "yg"), (1, uv, "uv", yv, "yv")):
                                fc = j + g_ * NJ
                                if H == 0:
                                    S.op("pool", lambda e: e.memset(usb[:, 0:2], 0.0), [], [ukey])
                                else:
                                    S.op("pool", lambda e: e.tensor_copy(usb[:, 0:2], halo[:, fc, :]), [f"halo{fc}"], [ukey])
                                for tb in range(2):
                                    bi = banks[(tb, g_)]
                                    S.op("act", lambda e, tb=tb, bi=bi: e.activation(out=usb[:, 2 + tb * 512:514 + tb * 512], in_=psf[bi][:, :],
                                                                                   func=AF.Copy), [f"psf{bi}"], [ukey])
                                    S.op("act", lambda e, tb=tb, bi=bi: e.activation(
                                        out=ysb[:, tb * 512:(tb + 1) * 512], in_=psf[bi][:, :], func=AF.Identity,
                                        scale=convc[:, 2, fc:fc + 1], bias=convc[:, 3, fc:fc + 1]), [f"psf{bi}", "convc"], [ykey])
                                if H == 0:
                                    S.op("pool", lambda e: e.tensor_copy(halo[:, fc, :], usb[:, 1024:1026]), [ukey], [f"halo{fc}"])
                                S.op("dve", lambda e: e.scalar_tensor_tensor(
                                    out=ysb[:], in0=usb[:, 1:1025], scalar=convc[:, 1, fc:fc + 1], in1=ysb[:], op0=ALU.mult, op1=ALU.add),
                                    [ukey, ykey, "convc"], [ykey])
                                S.op("dve", lambda e: e.scalar_tensor_tensor(
                                    out=ysb[:], in0=usb[:, 0:1024], scalar=convc[:, 0, fc:fc + 1], in1=ysb[:], op0=ALU.mult, op1=ALU.add),
                                    [ukey, ykey, "convc"], [ykey])
                            S.op("act", lambda e: e.activation(out=sg[:], in_=yg[:], func=AF.Silu), ["yg"], ["sg"])
                            S.op("dve", lambda e: e.tensor_tensor(aT[:, jj, :], sg[:], yv[:], ALU.mult), ["sg", "yv"], ["aT"])
                        for tt in range(8):
                            t = H * 8 + tt
                            for nh in range(2):
                                pp = psf[4 + ((2 * tt + nh) % 2)]
                                pkey = f"psf{4 + ((2 * tt + nh) % 2)}"
                                for jj in range(11):
                                    S.op("pe", lambda e, pp=pp, jj=jj, tt=tt, nh=nh, JG=JG: e.matmul(
                                        pp[:, :], aT[:, jj, tt * 128:(tt + 1) * 128], wdn[:, jj, nh * 512:(nh + 1) * 512],
                                        start=(jj == 0), stop=(jj == 10)), ["aT", "wdn"], [pkey])
                                S.op("dve", lambda e, pp=pp, nh=nh: e.tensor_tensor(tmpf[:], pp[:, :], g12[:, 1024 + nh * 512:1024 + (nh + 1) * 512],
                                                                                  ALU.mult), [pkey, "g12"], ["tmpf"])
                                S.op("pool", lambda e, t=t, nh=nh: e.tensor_tensor(x1[:, t, nh * 512:(nh + 1) * 512], tmpf[:],
                                                                                 x1[:, t, nh * 512:(nh + 1) * 512], ALU.add),
                                     ["tmpf", f"x1_{t}"], [f"x1_{t}"])
                            if JG == 1:
                                S.dma("sp", out_d[t * 128:(t + 1) * 128, :], x1[:, t, :], [f"x1_{t}"], [], "outd")
                            if H == 0 and JG == 1:
                                h2_norm(8 + tt, junk_f, xn_f[tt % 2], f"xnf{tt % 2}", st_ssq, st_rs)
                                if tt >= 1:
                                    h2_trans(8 + tt - 1, xn_f[(tt - 1) % 2], f"xnf{(tt - 1) % 2}")
                        if H == 0 and JG == 1:
                            h2_trans(15, xn_f[1], "xnf1")
                        if not (H == 1 and JG == 1):
                            nJG = 1 - JG
                            S.dma("pool", wdn[:], wdn_d[:, nJG * 11:(nJG + 1) * 11, :], ["dummy"], ["wdn"], "wdn")
        else:
            with ExitStack() as L2:
                z = sb(L2, "zout", [128, D], F32)
                S.op("dve", lambda e: e.memset(z[:], 0.0), [], ["z"])
                for t in range(NT):
                    S.dma("sp", out_d[t * 128:(t + 1) * 128, :], z[:], ["z"], [], "outd")
        S.finish()
    return nc, list(tap_d.keys())


def _host_constants():
    ki = np.arange(128)[:, None]
    col = np.arange(16 * 128)[None, :]
    dist = col - ki
    cnt = ((dist >= 0) & (dist <= 128)).astype(np.float32)
    cnt += ((dist >= 0) & (dist <= 512) & (dist % 4 == 0)).astype(np.float32)
    cnt += ((dist >= 0) & (dist <= 2048) & (dist % 16 == 0)).astype(np.float32)
    caus = (np.arange(128)[None, :] >= ki).astype(np.float32)
    masks = np.concatenate([cnt, caus], axis=1).astype(np.float32)
    inv_m = np.power(np.float32(10000.0), (-2.0 * np.arange(16, dtype=np.float32) / np.float32(32))).astype(np.float32)
    inv_d = np.power(np.float32(10000.0), (-2.0 * np.arange(32, dtype=np.float32) / np.float32(64))).astype(np.float32)
    invf = np.concatenate([inv_m, inv_d])[None, :].astype(np.float32)
    return masks, invf


def _prep_inputs(inp):
    f = lambda a: np.ascontiguousarray(np.asarray(a))
    masks, invf = _host_constants()
    w_ada = f(inp["w_ada"])[0]
    b_ada = f(inp["b_ada"])[0]
    w_in = f(inp["w_in"])[0]
    shared = {
        "w_ada": f(w_ada.reshape(8, 128, 6 * D).transpose(1, 0, 2)),
        "b_ada_col": f(b_ada.reshape(48, 128).T),
        "b_ada_g": f(np.concatenate([b_ada[2048:3072], b_ada[5120:6144]])[None, :]),
        "gcols": f(np.concatenate([f(inp["g_mix_norm"])[0].reshape(8, 128).T, f(inp["g_ffn_norm"])[0].reshape(8, 128).T], axis=1)),
        "gains": f(np.concatenate([f(inp["g_q_lat"])[0], f(inp["g_kv_lat"])[0], f(inp["g_mla_q_nope"])[0], f(inp["g_mla_q_pe"])[0],
                                   f(inp["g_mla_k_nope"])[0], f(inp["g_mla_k_pe"])[0], f(inp["g_dil_q"])[0], f(inp["g_dil_k"])[0]])[None, :]),
        "invf": invf,
        "w_inM": f(w_in[:, 0:800].reshape(8, 128, 800).transpose(1, 0, 2)),
        "w_inD": f(w_in[:, 800:2336].reshape(8, 128, 1536).transpose(1, 0, 2)),
        "w_qb": f(f(inp["w_q_b"])[0].reshape(4, 128, 768).transpose(1, 0, 2)),
        "w_kvb": f(f(inp["w_kv_b"])[0].reshape(2, 128, 1024).transpose(1, 0, 2)),
        "w_o": f(f(inp["w_o"])[0].reshape(8, 128, 1024).transpose(1, 0, 2)),
        "w_up": f(f(inp["w_up"])[0].reshape(8, 128, 2, NJ, 128).transpose(3, 1, 0, 2, 4)),
        "w_down": f(f(inp["w_down"])[0].reshape(NJ, 128, 1024).transpose(1, 0, 2)),
        "convcol": f(np.concatenate([f(inp["w_conv"])[0], f(inp["b_conv"])], axis=0).reshape(4, 2 * NJ, 128).transpose(2, 0, 1)),
        "masks": masks,
    }
    shared = {k: v.astype(np.float32) for k, v in shared.items()}
    x = f(inp["x"]); c = f(inp["c"]); pos = f(inp["positions"])
    maps = []
    for b in range(8):
        m = dict(shared)
        m["x"] = f(x[b]).astype(np.float32)
        m["ccol"] = f(c[b].reshape(8, 128).T).astype(np.float32)
        m["pos"] = f(pos[b].reshape(NT, 128).T).astype(np.int32)
        maps.append(m)
    return maps


_CACHE = {}


def kernel(**inputs):
    maps = _prep_inputs(inputs)
    if "nc" not in _CACHE:
        _CACHE["nc"] = build_program("F")[0]
    res = run_bass_kernel_spmd(_CACHE["nc"], maps, core_ids=list(range(8)))
    out = np.stack([np.asarray(r["out"]).reshape(S_LEN, D) for r in res.results], axis=0)
    return out.astype(np.float32)
```

```python
import numpy as np
from contextlib import ExitStack

import concourse.bass as bass
import concourse.mybir as mybir
from concourse.bass_utils import run_bass_kernel_spmd

F32 = mybir.dt.float32
BF16 = mybir.dt.bfloat16
I32 = mybir.dt.int32
AF = mybir.ActivationFunctionType
ALU = mybir.AluOpType
AX = mybir.AxisListType

D = 1024
S_LEN = 2048
NT = 16
EPS = 1e-6
DFF = 2816
NJ = 22
TWO_PI = 6.283185307179586
PI = 3.141592653589793
C_HI = 6.28125
C_LO = TWO_PI - C_HI

G_QLAT, G_KVLAT, G_QN, G_QP, G_KN, G_KP, G_DQ, G_DK = 0, 512, 768, 832, 864, 928, 960, 1024
G_TOT = 1088


class Sched:
    CE = ("pe", "act", "dve", "pool")
    STRICT = True

    def __init__(self, nc):
        self.nc = nc
        self.E = {"pe": nc.tensor, "act": nc.scalar, "dve": nc.vector, "pool": nc.gpsimd, "sp": nc.sync}
        self.sem = {e: nc.alloc_semaphore(name=f"s_{e}") for e in self.CE}
        self.cnt = {e: 0 for e in self.CE}
        self.known = {e: {} for e in self.E}
        self.res = {}
        self.dsem = {}
        self.nwaits = 0

    def _wait(self, eng, tok):
        src, val = tok
        if self.known[eng].get(src, 0) >= val:
            return
        sem = self.sem[src] if src in self.sem else self.dsem[src[2:]][0]
        self.E[eng].wait_ge(sem, val)
        self.known[eng][src] = val
        self.nwaits += 1

    def _deps(self, eng, reads, writes):
        for r in reads:
            st = self.res.get(r)
            if st and st[0]:
                self._wait(eng, st[0])
            if st and r.startswith("ps"):
                for t in st[1].values():
                    if t[0] != eng:
                        self._wait(eng, t)
        strict = self.STRICT and eng != "pe"
        for w in writes:
            st = self.res.get(w)
            if st:
                if st[0] and (st[0][0] != eng or strict):
                    self._wait(eng, st[0])
                for t in st[1].values():
                    if t[0] != eng or strict:
                        self._wait(eng, t)

    def _commit(self, tok, reads, writes):
        for r in reads:
            st = self.res.setdefault(r, [None, {}])
            st[1][tok[0]] = tok
        for w in writes:
            self.res[w] = [tok, {}]

    def op(self, eng, fn, reads=(), writes=()):
        self._deps(eng, reads, writes)
        ins = fn(self.E[eng])
        self.cnt[eng] += 1
        ins.then_inc(self.sem[eng], 1)
        self._commit((eng, self.cnt[eng]), reads, writes)

    def dma(self, q, out, in_, reads, writes, key):
        self._deps(q, reads, writes)
        if key not in self.dsem:
            self.dsem[key] = [self.nc.alloc_semaphore(name="d_" + key), 0]
        ins = self.E[q].dma_start(out=out, in_=in_)
        self.dsem[key][1] += 16
        ins.then_inc(self.dsem[key][0], 16)
        self._commit(("d:" + key, self.dsem[key][1]), reads, writes)

    def barrier(self, engines=None):
        for e in (engines or self.E):
            for f in self.CE:
                if f != e and self.cnt[f] > 0:
                    self._wait(e, (f, self.cnt[f]))
            for key, (sem, c) in self.dsem.items():
                if c > 0:
                    self._wait(e, ("d:" + key, c))
        if engines is None:
            self.res = {}

    def finish(self):
        self.barrier(engines=["sp"])


def build_program(stop_after="F", taps=()):
    nc = bass.Bass("TRN2", target_bir_lowering=False)
    dr = {}

    def din(name, shape, dt=F32):
        dr[name] = nc.dram_tensor(name, list(shape), dt, kind="ExternalInput").ap()
        return dr[name]

    x_d = din("x", [S_LEN, D])
    ccol_d = din("ccol", [128, 8])
    pos_d = din("pos", [128, NT], I32)
    wada_d = din("w_ada", [128, 8, 6 * D])
    badac_d = din("b_ada_col", [128, 48])
    badag_d = din("b_ada_g", [1, 2048])
    gcols_d = din("gcols", [128, 16])
    gains_d = din("gains", [1, G_TOT])
    invf_d = din("invf", [1, 48])
    winM_d = din("w_inM", [128, 8, 800])
    winD_d = din("w_inD", [128, 8, 1536])
    wqb_d = din("w_qb", [128, 4, 768])
    wkvb_d = din("w_kvb", [128, 2, 1024])
    wo_d = din("w_o", [128, 8, 1024])
    wup_d = din("w_up", [NJ, 128, 8, 2, 128])
    wdn_d = din("w_down", [128, NJ, 1024])
    convc_d = din("convcol", [128, 4, 2 * NJ])
    masks_d = din("masks", [128, 17 * 128])
    out_d = nc.dram_tensor("out", [S_LEN, D], F32, kind="ExternalOutput").ap()
    tap_d = {}

    S = Sched(nc)
    L0 = ExitStack()

    uid = [0]

    def sb(stack, name, shape, dt=F32):
        uid[0] += 1
        return stack.enter_context(nc.sbuf_tensor(f"sb{uid[0]}_{name}", list(shape), dt))

    with L0:
        ident = sb(L0, "ident", [128, 128], BF16)
        ones32 = sb(L0, "ones32", [128, 64], F32)
        gains = sb(L0, "gains_sb", [128, G_TOT], F32)
        cosT = sb(L0, "cosT", [128, NT, 48], F32)
        sinT = sb(L0, "sinT", [128, NT, 48], F32)
        gv1 = sb(L0, "gv1", [128, 8], F32)
        sh1 = sb(L0, "sh1", [128, 8], F32)
        gv2 = sb(L0, "gv2", [128, 8], F32)
        sh2 = sb(L0, "sh2", [128, 8], F32)
        convc = sb(L0, "convc", [128, 4, 2 * NJ], F32)
        g12 = sb(L0, "g12", [128, 2048], F32)
        wmask = sb(L0, "wmask", [128, 17 * 128], BF16)
        epsb = sb(L0, "epsb", [128, 1], F32)
        R1 = sb(L0, "R1", [128, 45056], BF16)
        R2 = sb(L0, "R2", [128, 16384], BF16)
        psb = [L0.enter_context(nc.psum_tensor(f"psb{i}", [128, 1024], BF16)) for i in range(2)]
        psf = [L0.enter_context(nc.psum_tensor(f"psf{i}", [128, 512], F32)) for i in range(6)]

        mixT = R2[:, :].rearrange("p (c n) -> p c n", n=S_LEN)

        def tap(name, ap, shape, dt=F32, key=None):
            if name not in taps:
                return
            t = nc.dram_tensor("tap_" + name, list(shape), dt, kind="ExternalOutput").ap()
            tap_d[name] = t
            S.barrier(engines=["sp"])
            S.dma("sp", out=t, in_=ap, reads=[], writes=[], key="tap")

        def mod_chunks(ns, wa_bufs, wkeys, scb_t, screp_t, badag_t, pcol, pcol_key, pg, pg_key, n_last):
            ns = list(ns)
            for i_n, n in enumerate(ns):
                buf = wa_bufs[n % 2]
                wkey = wkeys[n % 2]
                kind = n // 2
                if kind in (2, 5):
                    for k in range(8):
                        S.op("pe", lambda e, k=k: e.matmul(pg, screp_t[:, k, :], buf[:, k, :], start=(k == 0), stop=(k == 7)),
                             ["screp", wkey], [pg_key])
                        if k % 2 == 1:
                            yield
                    off = (0 if kind == 2 else 1024) + (n % 2) * 512
                    S.op("dve", lambda e: e.tensor_tensor(g12[:, off:off + 512], pg, badag_t[:, off:off + 512], ALU.add),
                         [pg_key, "badag"], ["g12"])
                else:
                    for c4 in range(4):
                        idx = n * 4 + c4
                        for k in range(8):
                            S.op("pe", lambda e, k=k: e.matmul(pcol[:, idx:idx + 1], buf[:, k, c4 * 128:(c4 + 1) * 128],
                                                               scb_t[:, k:k + 1], start=(k == 0), stop=(k == 7)),
                                 ["scb", wkey], [pcol_key])
                        yield
                if i_n + 2 < len(ns):
                    n2 = ns[i_n + 2]
                    assert n2 % 2 == n % 2
                    S.dma("pool", buf[:], wada_d[:, :, n2 * 512:(n2 + 1) * 512], [], [wkey], wkey)
                for _ in range(3):
                    yield

        with ExitStack() as L1:
            identf = sb(L1, "identf", [128, 128], F32)
            cc = sb(L1, "cc", [128, 8], F32)
            scb = sb(L1, "scb", [128, 8], BF16)
            screp = sb(L1, "screp", [128, 8, 128], BF16)
            wa = [sb(L1, f"wa{i}", [128, 8, 512], BF16) for i in range(2)]
            badac = sb(L1, "badac", [128, 48], F32)
            gcols = sb(L1, "gcols", [128, 16], F32)
            modc = sb(L1, "modc", [128, 48], F32)
            posi = sb(L1, "posi", [128, NT], I32)
            posf = sb(L1, "posf", [128, NT], F32)
            invf = sb(L1, "invf_sb", [128, 48], F32)
            ang = sb(L1, "ang", [128, NT * 48], F32)
            tq = sb(L1, "tq", [128, NT * 48], F32)
            ki = sb(L1, "ki", [128, NT * 48], I32)
            kf = sb(L1, "kf", [128, NT * 48], F32)
            rr = sb(L1, "rr", [128, NT * 48], F32)
            rc = sb(L1, "rc", [128, NT * 48], F32)

            S.dma("sp", cc[:], ccol_d, [], ["cc"], "c0")
            S.dma("sp", badac[:], badac_d, [], ["badac"], "c1")
            S.dma("sp", gcols[:], gcols_d, [], ["gcols"], "c2")
            S.dma("sp", posi[:], pos_d, [], ["posi"], "c3")
            S.dma("sp", invf[:], invf_d.partition_broadcast(128), [], ["invf"], "c4")
            S.dma("sp", gains[:], gains_d.partition_broadcast(128), [], ["gains"], "c5")
            S.dma("sp", convc[:], convc_d, [], ["convc"], "c6")
            S.dma("pool", wmask[:], masks_d, [], ["wmask"], "c8")
            for n in range(2):
                S.dma("pool", wa[n][:], wada_d[:, :, n * 512:(n + 1) * 512], [], [f"wa{n}"], f"wa{n}")

            S.op("pool", lambda e: e.memset(identf[:], 1.0), [], ["identf"])
            S.op("pool", lambda e: e.affine_select(out=identf[:], in_=identf[:], pattern=[[-1, 128]],
                                                   compare_op=ALU.is_equal, fill=0.0, base=0,
                                                   channel_multiplier=1), ["identf"], ["identf"])
            S.op("dve", lambda e: e.tensor_copy(ident[:], identf[:]), ["identf"], ["ident"])
            S.op("dve", lambda e: e.memset(ones32[:], 1.0), [], ["ones32"])
            S.op("dve", lambda e: e.memset(epsb[:], EPS), [], ["epsb"])

            S.op("act", lambda e: e.activation(out=scb[:], in_=cc[:], func=AF.Silu), ["cc"], ["scb"])
            S.op("dve", lambda e: e.tensor_copy(screp[:], scb[:].unsqueeze(2).broadcast_to([128, 8, 128])),
                 ["scb"], ["screp"])

            for _ in mod_chunks(range(4), wa, ["wa0", "wa1"], scb, screp, None, psf[0], "psf0", None, None, 3):
                pass
            S.op("dve", lambda e: e.tensor_tensor(modc[:, 0:16], psf[0][:, 0:16], badac[:, 0:16], ALU.add),
                 ["psf0", "badac"], ["modc"])
            S.op("dve", lambda e: e.scalar_tensor_tensor(out=gv1[:], in0=modc[:, 8:16], scalar=1.0, in1=gcols[:, 0:8],
                                                        op0=ALU.add, op1=ALU.mult), ["modc", "gcols"], ["gv1"])
            S.op("dve", lambda e: e.tensor_copy(sh1[:], modc[:, 0:8]), ["modc"], ["sh1"])

            S.op("dve", lambda e: e.tensor_copy(posf[:], posi[:]), ["posi"], ["posf"])
            angv = ang[:].rearrange("p (t f) -> p t f", f=48)
            S.op("dve", lambda e: e.tensor_tensor(angv, posf[:].unsqueeze(2).broadcast_to([128, NT, 48]),
                                                  invf[:].unsqueeze(1).broadcast_to([128, NT, 48]), ALU.mult),
                 ["posf", "invf"], ["ang"])
            S.op("dve", lambda e: e.tensor_scalar(tq[:], ang[:], 1.0 / TWO_PI, None, ALU.mult), ["ang"], ["tq"])
            S.op("dve", lambda e: e.tensor_copy(ki[:], tq[:]), ["tq"], ["ki"])
            S.op("dve", lambda e: e.tensor_copy(kf[:], ki[:]), ["ki"], ["kf"])
            S.op("dve", lambda e: e.scalar_tensor_tensor(out=rr[:], in0=kf[:], scalar=-C_HI, in1=ang[:],
                                                        op0=ALU.mult, op1=ALU.add), ["kf", "ang"], ["rr"])
            S.op("dve", lambda e: e.scalar_tensor_tensor(out=rr[:], in0=kf[:], scalar=-C_LO, in1=rr[:],
                                                        op0=ALU.mult, op1=ALU.add), ["kf", "rr"], ["rr"])
            S.op("dve", lambda e: e.tensor_scalar(rc[:], rr[:], PI / 2, -TWO_PI, ALU.is_gt, ALU.mult), ["rr"], ["rc"])
            S.op("dve", lambda e: e.scalar_tensor_tensor(out=rc[:], in0=rr[:], scalar=PI / 2, in1=rc[:],
                                                        op0=ALU.add, op1=ALU.add), ["rr", "rc"], ["rc"])
            for buf, key in ((rr, "rr"), (rc, "rc")):
                S.op("dve", lambda e, buf=buf: e.tensor_scalar(buf[:], buf[:], PI, -PI, ALU.min, ALU.max), [key], [key])
            S.op("act", lambda e: e.activation(out=sinT[:].rearrange("p t f -> p (t f)"), in_=rr[:], func=AF.Sin),
                 ["rr"], ["sinT"])
            S.op("act", lambda e: e.activation(out=cosT[:].rearrange("p t f -> p (t f)"), in_=rc[:], func=AF.Sin),
                 ["rc"], ["cosT"])
            tap("gv1", gv1[:], [128, 8])
            tap("sh1", sh1[:], [128, 8])
            tap("cosT", cosT[:].rearrange("p t f -> p (t f)"), [128, NT * 48])
            tap("sinT", sinT[:].rearrange("p t f -> p (t f)"), [128, NT * 48])
            S.barrier()

        def rstd_from_ssq(st, ssq_ap, out_ap, n, key_in, key_out, scr_ap, key_scr):
            S.op("act", lambda e: e.activation(out=scr_ap, in_=ssq_ap, func=AF.Sqrt, scale=1.0 / n, bias=epsb[:, 0:1]),
                 [key_in], [key_scr])
            S.op("dve", lambda e: e.reciprocal(out_ap, scr_ap), [key_scr], [key_out])

        def norm_transpose(xtile, key_x, rstd_ap, key_r, gv, sh, xn, hdst, key_h, pb, key_pb):
            S.op("dve", lambda e: e.tensor_scalar(xn[:], xtile, rstd_ap, None, ALU.mult), [key_x, key_r], ["xn"])
            for c in range(8):
                bk = psb[c // 4]
                S.op("pe", lambda e, c=c, bk=bk: e.transpose(bk[:, (c % 4) * 128:(c % 4 + 1) * 128], xn[:, c * 128:(c + 1) * 128], ident[:]),
                     ["xn", "ident"], [f"psb{c // 4}"])
            for cc in range(4):
                c = cc
                S.op("act", lambda e, c=c: e.activation(out=hdst[:, c, :], in_=psb[0][:, (c % 4) * 128:(c % 4 + 1) * 128],
                                                        func=AF.Identity, scale=gv[:, c:c + 1], bias=sh[:, c:c + 1]),
                     ["psb0", "gv", "sh"], [f"{key_h}_{c}"])
                c = 4 + cc
                S.op("dve", lambda e, c=c: e.tensor_scalar(hdst[:, c, :], psb[1][:, (c % 4) * 128:(c % 4 + 1) * 128],
                                                           gv[:, c:c + 1], sh[:, c:c + 1], ALU.mult, ALU.add),
                     ["psb1", "gv", "sh"], [f"{key_h}_{c}"])

        def group_norm(src3, nh, dh, gain_ap, dst3, st, tagp, src_keys=None, sfx=""):
            sq, ss, rs, rs2 = st
            sqv = sq[:, 0:nh * dh].rearrange("p (h d) -> p h d", d=dh)
            src_keys = src_keys or [tagp + "src"]
            ksq, kss, krs, krs2 = "sq" + sfx, "ss" + sfx, "rsg" + sfx, "rs2g" + sfx
            S.op("act", lambda e: e.activation(out=sqv, in_=src3, func=AF.Square), src_keys, [ksq])
            yield
            S.op("dve", lambda e: e.tensor_reduce(out=ss[:, 0:nh], in_=sqv, axis=AX.X, op=ALU.add), [ksq], [kss])
            yield
            S.op("act", lambda e: e.activation(out=rs[:, 0:nh], in_=ss[:, 0:nh], func=AF.Sqrt, scale=1.0 / dh, bias=epsb[:, 0:1]),
                 [kss], [krs])
            yield
            S.op("dve", lambda e: e.reciprocal(rs2[:, 0:nh], rs[:, 0:nh]), [krs], [krs2])
            yield
            S.op("dve", lambda e: e.tensor_tensor(sqv, src3, gain_ap.unsqueeze(1).broadcast_to([128, nh, dh]), ALU.mult),
                 src_keys + ["gains"], [ksq])
            yield
            S.op("dve", lambda e: e.tensor_tensor(dst3, sqv, rs2[:, 0:nh].unsqueeze(2).broadcast_to([128, nh, dh]), ALU.mult),
                 [ksq, krs2], [tagp + "dst"])
            yield

        def rope(src3, nh, half, cos_ap, sin_ap, dst3, tmp, key_src, key_dst, sfx=""):
            x1 = src3[:, :, 0:half]
            x2 = src3[:, :, half:2 * half]
            cb = cos_ap.unsqueeze(1).broadcast_to([128, nh, half])
            sbb = sin_ap.unsqueeze(1).broadcast_to([128, nh, half])
            ta = tmp[0][:, 0:nh * half].rearrange("p (h d) -> p h d", d=half)
            tb = tmp[1][:, 0:nh * half].rearrange("p (h d) -> p h d", d=half)
            ka, kb = "ropa" + sfx, "ropb" + sfx
            S.op("dve", lambda e: e.tensor_tensor(ta, x1, cb, ALU.mult), [key_src, "cosT"], [ka])
            yield
            S.op("dve", lambda e: e.tensor_tensor(tb, x2, sbb, ALU.mult), [key_src, "sinT"], [kb])
            yield
            S.op("dve", lambda e: e.tensor_tensor(dst3[:, :, 0:half], ta, tb, ALU.subtract), [ka, kb], [key_dst])
            yield
            S.op("dve", lambda e: e.tensor_tensor(ta, x1, sbb, ALU.mult), [key_src, "sinT"], [ka])
            yield
            S.op("dve", lambda e: e.tensor_tensor(tb, x2, cb, ALU.mult), [key_src, "cosT"], [kb])
            yield
            S.op("dve", lambda e: e.tensor_tensor(dst3[:, :, half:2 * half], ta, tb, ALU.add), [ka, kb], [key_dst])
            yield

        def run_interleaved(gen_fns, max_active=2):
            locks = {}
            pending = list(gen_fns)
            active = []

            def maybe_start():
                if pending and len(active) < max_active and (not active or active[-1]["spawned"]):
                    active.append({"g": pending.pop(0)(), "req": None, "spawned": False})

            maybe_start()
            while active:
                progressed = False
                for ent in list(active):
                    g = ent["g"]
                    if ent["req"] is not None:
                        if locks.get(ent["req"]) is None:
                            locks[ent["req"]] = g
                            ent["req"] = None
                        else:
                            continue
                    try:
                        r = next(g)
                    except StopIteration:
                        active.remove(ent)
                        assert not [k for k, v in locks.items() if v is g], locks
                        progressed = True
                        maybe_start()
                        continue
                    progressed = True
                    if r is None:
                        pass
                    elif r == "spawn":
                        ent["spawned"] = True
                        maybe_start()
                    elif r[0] == "acq":
                        assert locks.get(r[1]) is not g, r
                        if locks.get(r[1]) is None:
                            locks[r[1]] = g
                        else:
                            ent["req"] = r[1]
                    elif r[0] == "rel":
                        assert locks.get(r[1]) is g, r
                        locks[r[1]] = None
                assert progressed, ("interleave deadlock", locks)

        def attention(qT_of, kT_of, v_of, scale, dil, chunk0, Pt, rec, bcs, side=None, LA=2, sbanks=None):
            chunks = []
            gidx = 0
            for h in range(8):
                for QB in range(4):
                    nkt = 4 * (QB + 1)
                    for c in range(nkt):
                        i0 = max(0, c - 4 * QB)
                        chunks.append(dict(h=h, QB=QB, c=c, i0=i0, q0=i0 * 128, n=512 - i0 * 128,
                                           first=(c == 0), last=(c == nkt - 1), g=gidx))
                    gidx += 1
            if sbanks is None:
                sbanks = [(psf[0], "psf0"), (psf[1], "psf1"), (psf[2], "psf2")]
            nS = len(sbanks)
            assert len(Pt) >= nS and LA < nS
            deferred = []

            def emit_S(i):
                ck = chunks[i]
                h, QB, c, q0, n = ck["h"], ck["QB"], ck["c"], ck["q0"], ck["n"]
                qT, kT = qT_of(h), kT_of(h)
                sp_, skey = sbanks[i % nS]
                P = Pt[i % nS]; pkey = f"P{i % nS}"
                S.op("pe", lambda e: e.matmul(sp_[:, 0:n], kT[:, c * 128:(c + 1) * 128],
                                              qT[:, QB * 512 + q0:(QB + 1) * 512], start=True, stop=True),
                     ["KT", "QT"], [skey])
                S.op("act", lambda e: e.activation(out=P[:, 0:n], in_=sp_[:, 0:n], func=AF.Exp, scale=scale),
                     [skey], [pkey])
                if dil:
                    d0 = 4 * QB + ck["i0"] - c
                    S.op("dve", lambda e: e.tensor_tensor(P[:, 0:n], P[:, 0:n], wmask[:, d0 * 128:d0 * 128 + n], ALU.mult),
                         [pkey, "wmask"], [pkey])
                elif c >= 4 * QB:
                    S.op("dve", lambda e: e.tensor_tensor(P[:, 0:128], P[:, 0:128], wmask[:, 16 * 128:17 * 128], ALU.mult),
                         [pkey, "wmask"], [pkey])

            def emit_PV(i, it):
                ck = chunks[i]
                h, QB, c, q0, n, g = ck["h"], ck["QB"], ck["c"], ck["q0"], ck["n"], ck["g"]
                ot = psf[3 + (g % 2)]; okey = f"psf{3 + (g % 2)}"
                P = Pt[i % nS]; pkey = f"P{i % nS}"
                S.op("pe", lambda e: e.matmul(ot[:, q0:512], v_of(h, c), P[:, 0:n], start=ck["first"], stop=ck["last"]),
                     [pkey, "V"], [okey])
                if not ck["last"]:
                    return
                nlo, dlo = (0, 64) if h % 2 == 0 else (64, 0)
                rk = f"rec{g % 2}"
                rr_ = rec[g % 2]
                S.op("act", lambda e: e.activation(out=rr_[dlo:dlo + 1, :], in_=ot[dlo:dlo + 1, :], func=AF.Ln), [okey], [rk])
                S.op("act", lambda e: e.activation(out=rr_[dlo:dlo + 1, :], in_=rr_[dlo:dlo + 1, :], func=AF.Exp, scale=-1.0), [rk], [rk])
                ch = chunk0 + h // 2

                def fin():
                    S.op("pe", lambda e: e.matmul(psf[5][nlo:nlo + 64, :], ones32[dlo:dlo + 1, 0:64], rr_[dlo:dlo + 1, :],
                                                  start=True, stop=True), [rk, "ones32"], ["psf5"])
                    S.op("dve", lambda e: e.tensor_copy(bcs[nlo:nlo + 64, :], psf[5][nlo:nlo + 64, :]), ["psf5"], ["bcs"])
                    S.op("dve", lambda e: e.tensor_tensor(mixT[nlo:nlo + 64, ch, QB * 512:(QB + 1) * 512], ot[nlo:nlo + 64, :],
                                                          bcs[nlo:nlo + 64, :], ALU.mult), [okey, "bcs"], ["mixT"])
                deferred.append((it + 2, fin))

            nC = len(chunks)
            for it in range(nC + LA + 3):
                if side is not None and it % 3 == 2:
                    next(side, None)
                if it < nC:
                    emit_S(it)
                j = it - LA
                if 0 <= j < nC:
                    emit_PV(j, it)
                while deferred and deferred[0][0] <= it:
                    deferred.pop(0)[1]()
            assert not deferred

        def v_layout_store(Vx, t, src_ps_list, key_src_list, e_act=True):
            pass

        if stop_after != "0":
            QT = R1[:, 0:16384].rearrange("p (h n) -> p h n", n=S_LEN)
            KT = R1[:, 16384:32768].rearrange("p (h n) -> p h n", n=S_LEN)
            Vm = R1[:, 32768:45056].rearrange("p (t j c) -> p t j c", j=4, c=192)
            QdT = R1[:, 0:8192].rearrange("p (j n) -> p j n", n=S_LEN)
            QdT1 = R1[:, 16384:24576].rearrange("p (j n) -> p j n", n=S_LEN)
            KdT = R1[:, 8192:16384].rearrange("p (j n) -> p j n", n=S_LEN)
            Vd = Vm
            S.op("pool", lambda e: e.memset(Vm[:, :, :, 64:128], 1.0), [], ["V"])

            with ExitStack() as L2:
                winM = sb(L2, "winM", [128, 8, 800], BF16)
                wqb = sb(L2, "wqb", [128, 4, 768], BF16)
                wkvb = sb(L2, "wkvb", [128, 2, 1024], BF16)
                xt = [sb(L2, f"xt{i}", [128, D], F32) for i in range(2)]
                xn = sb(L2, "xn", [128, D], BF16)
                hT = [sb(L2, f"hT{i}", [128, 8, 128], BF16) for i in range(2)]
                st_ssq = sb(L2, "st_ssq", [128, 8], F32)
                st_rs = sb(L2, "st_rs", [128, 8], F32)
                st_rs2 = sb(L2, "st_rs2", [128, 8], F32)
                sqs = [sb(L2, f"sq{i}", [128, 1280], F32) for i in range(2)]
                sss = [sb(L2, f"ss{i}", [128, 24], F32) for i in range(2)]
                rsgs = [sb(L2, f"rsg{i}", [128, 24], F32) for i in range(2)]
                rs2gs = [sb(L2, f"rs2g{i}", [128, 24], F32) for i in range(2)]
                qln = sb(L2, "qln", [128, 512], BF16)
                kvn = sb(L2, "kvn", [128, 256], BF16)
                latT = sb(L2, "latT", [128, 6, 128], BF16)
                kpe = sb(L2, "kpe", [128, 32], F32)
                kpe2 = sb(L2, "kpe2", [128, 32], F32)
                kper = [sb(L2, f"kper{i}", [128, 32], BF16) for i in range(2)]
                ropts = [[sb(L2, f"ropt{p}{i}", [128, 256], F32) for i in range(2)] for p in range(2)]

                S.dma("pool", winM[:], winM_d, [], ["winM"], "winM")
                S.dma("pool", wqb[:], wqb_d, [], ["wqb"], "wqb")
                S.dma("pool", wkvb[:], wkvb_d, [], ["wkvb"], "wkvb")
                S.dma("sp", xt[0][:], x_d[0:128, :], [], ["xt0"], "xt0")
                def tileM(t):
                    b = t % 2
                    sq = sqs[b]
                    gst = (sqs[b], sss[b], rsgs[b], rs2gs[b])
                    ropt = ropts[b]
                    junk = sqs[b][:, 0:512].bitcast(BF16)
                    yield ("acq", "X")
                    if t + 1 < NT:
                        S.dma("sp", xt[1 - b][:], x_d[(t + 1) * 128:(t + 2) * 128, :], [], [f"xt{1 - b}"], f"xt{1 - b}")
                    S.op("act", lambda e, b=b: e.activation(out=junk, in_=xt[b][:], func=AF.Square,
                                                            accum_out=st_ssq[:, 0:1]), [f"xt{b}"], [f"sq{b}", "ssq0"])
                    rstd_from_ssq(None, st_ssq[:, 0:1], st_rs[:, 0:1], D, "ssq0", "rs0", st_ssq[:, 1:2], "ssq0b")
                    yield
                    yield ("acq", "psb")
                    norm_transpose(xt[b][:], f"xt{b}", st_rs[:, 0:1], "rs0", gv1, sh1, xn, hT[b], f"hT{b}", psb[0], "psb0")
                    yield ("rel", "psb")
                    yield ("rel", "X")
                    yield "spawn"
                    yield ("acq", "P")
                    for k in range(8):
                        S.op("pe", lambda e, k=k, b=b: e.matmul(psf[0][:, :], hT[b][:, k, :], winM[:, k, 0:512],
                                                                start=(k == 0), stop=(k == 7)), [f"hT{b}_{k}", "winM"], ["psf0"])
                    for k in range(8):
                        S.op("pe", lambda e, k=k, b=b: e.matmul(psf[1][:, 0:288], hT[b][:, k, :], winM[:, k, 512:800],
                                                                start=(k == 0), stop=(k == 7)), [f"hT{b}_{k}", "winM"], ["psf1"])
                    yield
                    S.op("act", lambda e: e.activation(out=sq[:, 0:512], in_=psf[0][:, :], func=AF.Square,
                                                       accum_out=st_ssq[:, 4:5]), ["psf0"], [f"sq{b}", "ssqP0"])
                    yield
                    S.op("act", lambda e: e.activation(out=sq[:, 512:768], in_=psf[1][:, 0:256], func=AF.Square, scale=2.0 ** 0.5,
                                                       accum_out=st_ssq[:, 5:6]), ["psf1"], [f"sq{b}", "ssqP1"])
                    yield
                    S.op("act", lambda e: e.activation(out=sq[:, 768:800], in_=psf[1][:, 256:288], func=AF.Square, scale=4.0,
                                                       accum_out=st_ssq[:, 6:7]), ["psf1"], [f"sq{b}", "ssqP2"])
                    yield
                    S.op("act", lambda e: e.activation(out=st_rs[:, 4:7], in_=st_ssq[:, 4:7], func=AF.Sqrt, scale=1.0 / 512, bias=epsb[:, 0:1]),
                         ["ssqP0", "ssqP1", "ssqP2"], ["rsP"])
                    yield
                    S.op("dve", lambda e: e.reciprocal(st_rs2[:, 4:7], st_rs[:, 4:7]), ["rsP"], ["rs2P"])
                    yield
                    S.op("dve", lambda e: e.scalar_tensor_tensor(out=qln[:], in0=psf[0][:, :], scalar=st_rs2[:, 4:5],
                                                                in1=gains[:, G_QLAT:G_QLAT + 512], op0=ALU.mult, op1=ALU.mult),
                         ["psf0", "rs2P", "gains"], ["qln"])
                    yield
                    S.op("dve", lambda e: e.scalar_tensor_tensor(out=kvn[:], in0=psf[1][:, 0:256], scalar=st_rs2[:, 5:6],
                                                                in1=gains[:, G_KVLAT:G_KVLAT + 256], op0=ALU.mult, op1=ALU.mult),
                         ["psf1", "rs2P", "gains"], ["kvn"])
                    yield
                    S.op("dve", lambda e: e.scalar_tensor_tensor(out=kpe2[:], in0=psf[1][:, 256:288], scalar=st_rs2[:, 6:7],
                                                                in1=gains[:, G_KP:G_KP + 32], op0=ALU.mult, op1=ALU.mult),
                         ["psf1", "rs2P", "gains"], ["kpedst"])
                    yield
                    yield from rope(kpe2[:].rearrange("p (h d) -> p h d", d=32), 1, 16, cosT[:, t, 0:16], sinT[:, t, 0:16],
                         kper[b][:].rearrange("p (h d) -> p h d", d=32), ropt, "kpedst", f"kper{b}", sfx=str(b))
                    yield
                    yield ("acq", "Q")
                    yield ("acq", "psb")
                    for c in range(4):
                        S.op("pe", lambda e, c=c: e.transpose(psb[1][:, c * 128:(c + 1) * 128], qln[:, c * 128:(c + 1) * 128], ident[:]),
                             ["qln", "ident"], ["psb1"])
                    for c in range(2):
                        S.op("pe", lambda e, c=c: e.transpose(psb[1][:, (4 + c) * 128:(5 + c) * 128], kvn[:, c * 128:(c + 1) * 128], ident[:]),
                             ["kvn", "ident"], ["psb1"])
                    S.op("act", lambda e: e.activation(out=latT[:].rearrange("p c n -> p (c n)"), in_=psb[1][:, 0:768], func=AF.Copy),
                         ["psb1"], ["latT"])
                    yield ("rel", "psb")
                    yield ("rel", "P")
                    for (pp, key, c0, cn) in ((psf[2], "psf2", 0, 512), (psf[3], "psf3", 512, 256)):
                        for k in range(4):
                            S.op("pe", lambda e, pp=pp, k=k, c0=c0, cn=cn: e.matmul(pp[:, 0:cn], latT[:, k, :], wqb[:, k, c0:c0 + cn],
                                                                                    start=(k == 0), stop=(k == 3)), ["latT", "wqb"], [key])
                    for (pp, key, c0) in ((psf[4], "psf4", 0), (psf[5], "psf5", 512)):
                        for k in range(2):
                            S.op("pe", lambda e, pp=pp, k=k, c0=c0: e.matmul(pp[:, :], latT[:, 4 + k, :], wkvb[:, k, c0:c0 + 512],
                                                                             start=(k == 0), stop=(k == 1)), ["latT", "wkvb"], [key])
                    yield
                    qf_b, kvf_b, qpn_b, Qtok_b, Ktok_b = qfs[b], kvfs[b], qpns[b], Qtoks[b], Ktoks[b]
                    pb = str(b)
                    S.op("act", lambda e: e.activation(out=qf_b[:, 0:512], in_=psf[2][:, :], func=AF.Copy), ["psf2"], ["qsrcA" + pb])
                    S.op("dve", lambda e: e.tensor_copy(qf_b[:, 512:768], psf[3][:, 0:256]), ["psf3"], ["qsrcB" + pb])
                    S.op("act", lambda e: e.activation(out=kvf_b[:, 0:512], in_=psf[4][:, :], func=AF.Copy), ["psf4"], ["knsrcA" + pb])
                    S.op("dve", lambda e: e.tensor_copy(kvf_b[:, 512:1024], psf[5][:, :]), ["psf5"], ["knsrcB" + pb])
                    yield ("rel", "Q")
                    q3 = qf_b.rearrange("p (h d) -> p h d", d=96)
                    kv3 = kvf_b.rearrange("p (h d) -> p h d", d=128)
                    qsk = ["qsrcA" + pb, "qsrcB" + pb]
                    ksk = ["knsrcA" + pb, "knsrcB" + pb]
                    sqb, ssb, rsb, rs2b = gst
                    sqA = sqb[:, 0:512].rearrange("p (h d) -> p h d", d=64)
                    sqB = sqb[:, 512:768].rearrange("p (h d) -> p h d", d=32)
                    sqC = sqb[:, 768:1280].rearrange("p (h d) -> p h d", d=64)
                    ksq = "sq" + pb
                    S.op("act", lambda e: e.activation(out=sqA, in_=q3[:, :, 0:64], func=AF.Square), qsk, [ksq + "A"])
                    yield
                    S.op("act", lambda e: e.activation(out=sqB, in_=q3[:, :, 64:96], func=AF.Square, scale=2.0 ** 0.5), qsk, [ksq + "B"])
                    yield
                    S.op("act", lambda e: e.activation(out=sqC, in_=kv3[:, :, 0:64], func=AF.Square), ksk, [ksq + "C"])
                    yield
                    S.op("dve", lambda e: e.tensor_reduce(out=ssb[:, 0:8], in_=sqA, axis=AX.X, op=ALU.add), [ksq + "A"], ["ss" + pb])
                    yield
                    S.op("dve", lambda e: e.tensor_reduce(out=ssb[:, 8:16], in_=sqB, axis=AX.X, op=ALU.add), [ksq + "B"], ["ss" + pb])
                    yield
                    S.op("dve", lambda e: e.tensor_reduce(out=ssb[:, 16:24], in_=sqC, axis=AX.X, op=ALU.add), [ksq + "C"], ["ss" + pb])
                    yield
                    S.op("act", lambda e: e.activation(out=rsb[:, 0:24], in_=ssb[:, 0:24], func=AF.Sqrt, scale=1.0 / 64, bias=epsb[:, 0:1]),
                         ["ss" + pb], ["rsg" + pb])
                    yield
                    S.op("dve", lambda e: e.tensor_tensor(sqA, q3[:, :, 0:64], gains[:, G_QN:G_QN + 64].unsqueeze(1).broadcast_to([128, 8, 64]), ALU.mult),
                         qsk + ["gains"], [ksq + "A"])
                    yield
                    S.op("dve", lambda e: e.tensor_tensor(sqB, q3[:, :, 64:96], gains[:, G_QP:G_QP + 32].unsqueeze(1).broadcast_to([128, 8, 32]), ALU.mult),
                         qsk + ["gains"], [ksq + "B"])
                    yield
                    S.op("dve", lambda e: e.tensor_tensor(sqC, kv3[:, :, 0:64], gains[:, G_KN:G_KN + 64].unsqueeze(1).broadcast_to([128, 8, 64]), ALU.mult),
                         ksk + ["gains"], [ksq + "C"])
                    yield
                    S.op("dve", lambda e: e.reciprocal(rs2b[:, 0:24], rsb[:, 0:24]), ["rsg" + pb], ["rs2g" + pb])
                    yield
                    S.op("dve", lambda e: e.tensor_tensor(qpn_b, sqB, rs2b[:, 8:16].unsqueeze(2).broadcast_to([128, 8, 32]), ALU.mult),
                         [ksq + "B", "rs2g" + pb, ksq], ["qp" + pb + "dst"])
                    yield
                    S.op("dve", lambda e: e.tensor_tensor(Qtok_b[:, :, 0:64], sqA, rs2b[:, 0:8].unsqueeze(2).broadcast_to([128, 8, 64]), ALU.mult),
                         [ksq + "A", "rs2g" + pb, ksq], ["q" + pb + "dst"])
                    yield
                    S.op("dve", lambda e: e.tensor_tensor(Ktok_b[:, :, 0:64], sqC, rs2b[:, 16:24].unsqueeze(2).broadcast_to([128, 8, 64]), ALU.mult),
                         [ksq + "C", "rs2g" + pb, ksq], ["kn" + pb + "dst"])
                    yield
                    yield from rope(qpn_b, 8, 16, cosT[:, t, 0:16], sinT[:, t, 0:16], Qtok_b[:, :, 64:96], ropt, "qp" + pb + "dst", "q" + pb + "dst", sfx=pb)
                    yield
                    S.op("dve", lambda e: e.tensor_copy(Ktok_b[:, :, 64:96], kper[b][:].unsqueeze(1).broadcast_to([128, 8, 32])),
                         [f"kper{b}"], ["kn" + pb + "dst"])
                    vsrc = kvf_b.rearrange("p (j e d) -> p j e d", e=2, d=128)[:, :, :, 64:128]
                    vdst = Vm[:, t, :, :].rearrange("p j (e d) -> p j e d", d=64)[:, :, 0:3:2, :]
                    S.op("pool", lambda e, vsrc=vsrc, vdst=vdst: e.tensor_copy(vdst, vsrc), ksk, ["V"])
                    yield
                    yield ("acq", "psb")
                    for h in range(8):
                        S.op("pe", lambda e, h=h: e.transpose(psb[0][0:96, h * 128:(h + 1) * 128], Qtok_b[:, h, :], ident[:]),
                             ["q" + pb + "dst", "ident"], ["psb0"])
                    S.op("act", lambda e, t=t: e.activation(out=QT[0:96, :, t * 128:(t + 1) * 128],
                                                            in_=psb[0][0:96, :].rearrange("p (h n) -> p h n", n=128), func=AF.Copy),
                         ["psb0"], ["QT"])
                    yield
                    for h in range(8):
                        S.op("pe", lambda e, h=h: e.transpose(psb[1][0:96, h * 128:(h + 1) * 128], Ktok_b[:, h, :], ident[:]),
                             ["kn" + pb + "dst", "ident"], ["psb1"])
                    S.op("dve", lambda e, t=t: e.tensor_copy(KT[0:96, :, t * 128:(t + 1) * 128],
                                                             psb[1][0:96, :].rearrange("p (h n) -> p h n", n=128)),
                         ["psb1"], ["KT"])
                    yield ("rel", "psb")

                R2f = R2[:, :].bitcast(F32)
                qfs = [R2f[:, p * 2048 + 0:p * 2048 + 768] for p in range(2)]
                kvfs = [R2f[:, p * 2048 + 768:p * 2048 + 1792] for p in range(2)]
                qpns = [R2f[:, p * 2048 + 1792:p * 2048 + 2048].rearrange("p (h d) -> p h d", d=32) for p in range(2)]
                Qtoks = [R2[:, 8192 + p * 1536:8192 + p * 1536 + 768].rearrange("p (h d) -> p h d", d=96) for p in range(2)]
                Ktoks = [R2[:, 8192 + p * 1536 + 768:8192 + (p + 1) * 1536].rearrange("p (h d) -> p h d", d=96) for p in range(2)]
                run_interleaved([(lambda t=t: tileM(t)) for t in range(NT)])
                tap("QT", QT[0:96, :, :].rearrange("p h n -> p (h n)"), [96, 8 * S_LEN], BF16)
                tap("KT", KT[0:96, :, :].rearrange("p h n -> p (h n)"), [96, 8 * S_LEN], BF16)
                tap("Vm", Vm.rearrange("p t j c -> p (t j c)"), [128, 16 * 768], BF16)
                S.barrier()

        if stop_after not in ("0", "AM"):
            with ExitStack() as L2:
                winD = sb(L2, "winD", [128, 8, 1536], BF16)
                Pt = [sb(L2, f"Pt{i}", [128, 512], BF16) for i in range(4)]
                rec = [sb(L2, f"rec{i}", [128, 512], F32) for i in range(2)]
                bcs = sb(L2, "bcs", [128, 512], F32)
                S.dma("pool", winD[:], winD_d, [], ["winD"], "winD")
                with ExitStack() as Lmod:
                    wa2 = [sb(Lmod, f"wa2_{i}", [128, 8, 512], BF16) for i in range(2)]
                    badag2 = sb(Lmod, "badag2", [128, 2048], F32)
                    cc2 = sb(Lmod, "cc2", [128, 8], F32)
                    scb2 = sb(Lmod, "scb2", [128, 8], BF16)
                    screp2 = sb(Lmod, "screp2", [128, 8, 128], BF16)
                    badac2 = sb(Lmod, "badac2", [128, 48], F32)
                    gcols2 = sb(Lmod, "gcols2", [128, 16], F32)
                    modc2 = sb(Lmod, "modc2", [128, 48], F32)
                    pcol2 = psb[0][:, :].bitcast(F32)
                    pg2 = psb[1][:, :].bitcast(F32)

                    def side_mod():
                        S.dma("sp", cc2[:], ccol_d, [], ["cc2"], "m0")
                        S.dma("sp", badac2[:], badac_d, [], ["badac2"], "m1")
                        S.dma("sp", gcols2[:], gcols_d, [], ["gcols2"], "m2")
                        S.dma("sp", badag2[:], badag_d.partition_broadcast(128), [], ["badag"], "m3")
                        for n in (4, 5):
                            S.dma("pool", wa2[n % 2][:], wada_d[:, :, n * 512:(n + 1) * 512], [], [f"wa2{n % 2}"], f"wa2{n % 2}")
                        for _ in range(12):
                            yield
                        S.op("act", lambda e: e.activation(out=scb2[:], in_=cc2[:], func=AF.Silu), ["cc2"], ["scb"])
                        yield
                        S.op("dve", lambda e: e.tensor_copy(screp2[:], scb2[:].unsqueeze(2).broadcast_to([128, 8, 128])),
                             ["scb"], ["screp"])
                        for _ in range(4):
                            yield
                        yield from mod_chunks([4, 5, 10, 11, 6, 7, 8, 9], wa2, ["wa20", "wa21"], scb2, screp2, badag2, pg2, "psb1", pg2, "psb1", 11)
                        S.op("dve", lambda e: e.tensor_tensor(modc2[:, 24:40], pg2[:, 24:40], badac2[:, 24:40], ALU.add),
                             ["psb1", "badac2"], ["modc2"])
                        yield
                        S.op("dve", lambda e: e.scalar_tensor_tensor(out=gv2[:], in0=modc2[:, 32:40], scalar=1.0, in1=gcols2[:, 8:16],
                                                                    op0=ALU.add, op1=ALU.mult), ["modc2", "gcols2"], ["gv2"])
                        S.op("dve", lambda e: e.tensor_copy(sh2[:], modc2[:, 24:32]), ["modc2"], ["sh2"])

                    sidegen = side_mod()
                    attention(lambda h: QT[0:96, h, :], lambda h: KT[0:96, h, :],
                              lambda h, c: Vm[:, c, h // 2, (h % 2) * 64:(h % 2) * 64 + 128],
                              96 ** -0.5, False, 0, Pt, rec, bcs, side=sidegen, LA=3,
                              sbanks=[(psf[0], "psf0"), (psf[1], "psf1"), (psf[2], "psf2"), (psb[0][:, :].bitcast(F32), "psb0")])
                    for _ in sidegen:
                        pass
                    tap("g12", g12[:], [128, 2048])
                tap("mixM", mixT[:, 0:4, :].rearrange("p c n -> p (c n)"), [128, 4 * S_LEN], BF16)
                S.barrier()
                if stop_after != "M":
                    with ExitStack() as L3:
                        xt = [sb(L3, f"xtd{i}", [128, D], F32) for i in range(2)]
                        xn = sb(L3, "xnd", [128, D], BF16)
                        hT = [sb(L3, f"hTd{i}", [128, 8, 128], BF16) for i in range(2)]
                        st_ssq = sb(L3, "std_ssq", [128, 4], F32)
                        st_rs = sb(L3, "std_rs", [128, 4], F32)
                        junkd = sb(L3, "junkd", [128, D], BF16)
                        sqA = sb(L3, "sqA", [128, 1024], F32)
                        ssA = sb(L3, "ssA", [128, 16], F32)
                        rsA = sb(L3, "rsA", [128, 16], F32)
                        rs2A = sb(L3, "rs2A", [128, 16], F32)
                        ropt2 = [sb(L3, f"ropt2{i}", [128, 512], F32) for i in range(2)]
                        tokA = sb(L3, "tokA", [128, 1024], BF16)
                        R2f = R2[:, :].bitcast(F32)
                        qk_f = R2f[:, 4096:5120]
                        qk_n = R2f[:, 5120:6144]
                        S.op("pool", lambda e: e.memset(QdT[64:128, :, :], 0.0), [], ["QT"])
                        S.op("pool", lambda e: e.memset(QdT1[0:64, :, :], 0.0), [], ["QT"])
                        S.dma("sp", xt[0][:], x_d[0:128, :], [], ["xt0"], "xt0")

                        def tileD(t):
                            b = t % 2
                            yield ("acq", "X")
                            if t + 1 < NT:
                                S.dma("sp", xt[1 - b][:], x_d[(t + 1) * 128:(t + 2) * 128, :], [], [f"xt{1 - b}"], f"xt{1 - b}")
                            S.op("act", lambda e: e.activation(out=junkd[:], in_=xt[b][:], func=AF.Square,
                                                               accum_out=st_ssq[:, 0:1]), [f"xt{b}"], ["junkd", "ssq0"])
                            rstd_from_ssq(None, st_ssq[:, 0:1], st_rs[:, 0:1], D, "ssq0", "rs0", st_ssq[:, 1:2], "ssq0b")
                            yield
                            yield ("acq", "psb")
                            norm_transpose(xt[b][:], f"xt{b}", st_rs[:, 0:1], "rs0", gv1, sh1, xn, hT[b], f"hT{b}", psb[0], "psb0")
                            yield ("rel", "psb")
                            yield ("rel", "X")
                            yield "spawn"
                            yield ("acq", "P")
                            for (pp, key, c0) in ((psf[0], "psf0", 0), (psf[1], "psf1", 512), (psf[2], "psf2", 1024)):
                                for k in range(8):
                                    S.op("pe", lambda e, pp=pp, k=k, c0=c0: e.matmul(pp[:, :], hT[b][:, k, :], winD[:, k, c0:c0 + 512],
                                                                                   start=(k == 0), stop=(k == 7)),
                                         [f"hT{b}_{k}", "winD"], [key])
                                yield
                            yield ("acq", "N")
                            S.op("act", lambda e: e.activation(out=qk_f[:, 0:512], in_=psf[0][:, :], func=AF.Copy), ["psf0"], ["qkfA"])
                            S.op("dve", lambda e: e.tensor_copy(qk_f[:, 512:1024], psf[1][:, :]), ["psf1"], ["qkfB"])
                            vsrc = psf[2][:, :].rearrange("p (j e d) -> p j e d", e=2, d=64)
                            vdst = Vd[:, t, :, :].rearrange("p j (e d) -> p j e d", d=64)[:, :, 0:3:2, :]
                            S.op("act", lambda e: e.activation(out=vdst, in_=vsrc, func=AF.Copy), ["psf2"], ["V"])
                            yield ("rel", "P")
                            src3 = qk_f.rearrange("p (h d) -> p h d", d=64)
                            src4 = qk_f.rearrange("p (s h d) -> p s h d", s=2, h=8)
                            sq3 = sqA[:, :].rearrange("p (h d) -> p h d", d=64)
                            sq4 = sqA[:, :].rearrange("p (s h d) -> p s h d", s=2, h=8)
                            dst3 = qk_n.rearrange("p (h d) -> p h d", d=64)
                            g4 = gains[:, G_DQ:G_DQ + 128].rearrange("p (s d) -> p s d", d=64).unsqueeze(2).broadcast_to([128, 2, 8, 64])
                            S.op("act", lambda e: e.activation(out=sq3, in_=src3, func=AF.Square), ["qkfA", "qkfB"], ["sqA"])
                            yield
                            S.op("dve", lambda e: e.tensor_reduce(out=ssA[:, :], in_=sq3, axis=AX.X, op=ALU.add), ["sqA"], ["ssA"])
                            yield
                            S.op("act", lambda e: e.activation(out=rsA[:, :], in_=ssA[:, :], func=AF.Sqrt, scale=1.0 / 64, bias=epsb[:, 0:1]), ["ssA"], ["rsA"])
                            yield
                            S.op("dve", lambda e: e.reciprocal(rs2A[:, :], rsA[:, :]), ["rsA"], ["rs2A"])
                            yield
                            S.op("dve", lambda e: e.tensor_tensor(sq4, src4, g4, ALU.mult), ["qkfA", "qkfB", "gains"], ["sqA"])
                            yield
                            S.op("dve", lambda e: e.tensor_tensor(dst3, sq3, rs2A[:, :].unsqueeze(2).broadcast_to([128, 16, 64]), ALU.mult),
                                 ["sqA", "rs2A"], ["qkn"])
                            yield
                            yield from rope(dst3, 16, 32, cos_ap=cosT[:, t, 16:48], sin_ap=sinT[:, t, 16:48],
                                            dst3=tokA[:, :].rearrange("p (h d) -> p h d", d=64), tmp=ropt2, key_src="qkn", key_dst="tokA", sfx="D")
                            yield
                            yield ("acq", "psb")
                            for j in range(4):
                                S.op("pe", lambda e, j=j: e.transpose(psb[0][:, j * 128:(j + 1) * 128], tokA[:, j * 128:(j + 1) * 128], ident[:]),
                                     ["tokA", "ident"], ["psb0"])
                            for j in range(4):
                                S.op("pe", lambda e, j=j: e.transpose(psb[1][:, j * 128:(j + 1) * 128], tokA[:, 512 + j * 128:512 + (j + 1) * 128], ident[:]),
                                     ["tokA", "ident"], ["psb1"])
                            yield
                            S.op("act", lambda e: e.activation(
                                out=QdT[0:64, :, t * 128:(t + 1) * 128],
                                in_=psb[0][0:64, 0:512].rearrange("p (j n) -> p j n", n=128), func=AF.Copy), ["psb0"], ["QT"])
                            S.op("act", lambda e: e.activation(
                                out=QdT1[64:128, :, t * 128:(t + 1) * 128],
                                in_=psb[0][64:128, 0:512].rearrange("p (j n) -> p j n", n=128), func=AF.Copy), ["psb0"], ["QT"])
                            S.op("dve", lambda e: e.tensor_copy(
                                KdT[:, :, t * 128:(t + 1) * 128],
                                psb[1][:, 0:512].rearrange("p (j n) -> p j n", n=128)), ["psb1"], ["KT"])
                            yield ("rel", "psb")
                            yield ("rel", "N")

                        run_interleaved([(lambda t=t: tileD(t)) for t in range(NT)])
                        tap("QdT", QdT.rearrange("p j n -> p (j n)"), [128, 4 * S_LEN], BF16)
                        tap("KdT", KdT.rearrange("p j n -> p (j n)"), [128, 4 * S_LEN], BF16)
                        tap("Vd", Vd.rearrange("p t j c -> p (t j c)"), [128, 16 * 768], BF16)
                        S.barrier()
                    attention(lambda h: (QdT if h % 2 == 0 else QdT1)[:, h // 2, :],
                              lambda h: KdT[:, h // 2, :],
                              lambda h, c: Vd[:, c, h // 2, (h % 2) * 64:(h % 2) * 64 + 128],
                              64 ** -0.5, True, 4, Pt, rec, bcs, LA=3,
                              sbanks=[(psf[0], "psf0"), (psf[1], "psf1"), (psf[2], "psf2"), (psb[0][:, :].bitcast(F32), "psb0")])
                    tap("mixT", mixT.rearrange("p c n -> p (c n)"), [128, 8 * S_LEN], BF16)
                    S.barrier()

        if stop_after in ("O", "F"):
            x1 = R1[:, 0:32768].bitcast(F32).rearrange("p (t n) -> p t n", n=D)
            h2T = R1[:, 32768:40960].rearrange("p (c n) -> p c n", n=1024)
            wub = [R1[:, 40960 + i * 2048:40960 + (i + 1) * 2048].rearrange("p (k g c) -> p k g c", g=2, c=128) for i in range(2)]
            wub += [R2[:, 11264 + i * 2048:11264 + (i + 1) * 2048].rearrange("p (k g c) -> p k g c", g=2, c=128) for i in range(2)]
            NWB = 4

            def h2_norm(t, junk_ap, xnb, xkey, ssq, rs):
                S.op("act", lambda e: e.activation(out=junk_ap, in_=x1[:, t, :], func=AF.Square, accum_out=ssq[:, 0:1]),
                     [f"x1_{t}"], ["junkh", "sg", "ssq0"])
                rstd_from_ssq(None, ssq[:, 0:1], rs[:, 0:1], D, "ssq0", "rs0", ssq[:, 1:2], "ssq0b")
                S.op("dve", lambda e: e.tensor_scalar(xnb[:], x1[:, t, :], rs[:, 0:1], None, ALU.mult), [f"x1_{t}", "rs0"], [xkey])

            def h2_trans(t, xnb, xkey):
                tt = t % 8
                hdst = h2T[:, :, tt * 128:(tt + 1) * 128]
                for c in range(8):
                    S.op("pe", lambda e, c=c: e.transpose(psb[c // 4][:, (c % 4) * 128:(c % 4 + 1) * 128],
                                                          xnb[:, c * 128:(c + 1) * 128], ident[:]), [xkey, "ident"], [f"psb{c // 4}"])
                for cc in range(4):
                    S.op("act", lambda e, c=cc: e.activation(out=hdst[:, c, :], in_=psb[0][:, (c % 4) * 128:(c % 4 + 1) * 128],
                                                             func=AF.Identity, scale=gv2[:, c:c + 1], bias=sh2[:, c:c + 1]),
                         ["psb0"], [f"h2T_{cc}"])
                    S.op("dve", lambda e, c=4 + cc: e.tensor_scalar(hdst[:, c, :], psb[1][:, (c % 4) * 128:(c % 4 + 1) * 128],
                                                                    gv2[:, c:c + 1], sh2[:, c:c + 1], ALU.mult, ALU.add),
                         ["psb1"], [f"h2T_{4 + cc}"])
            with ExitStack() as L2:
                wo = sb(L2, "wo", [128, 8, 1024], BF16)
                xt = [sb(L2, f"xto{i}", [128, D], F32) for i in range(2)]
                tmpo = sb(L2, "tmpo", [128, 512], F32)
                junk_o = sb(L2, "junko", [128, D], BF16)
                xn_o = [sb(L2, f"xno{i}", [128, D], BF16) for i in range(3)]
                sso = sb(L2, "sso", [128, 4], F32)
                rso = sb(L2, "rso", [128, 4], F32)
                S.dma("pool", wo[:], wo_d, [], ["wo"], "wo")
                S.dma("sp", xt[0][:], x_d[0:128, :], [], ["xt0"], "xt0")
                for t in range(NT):
                    b = t % 2
                    if t + 1 < NT:
                        S.dma("sp", xt[1 - b][:], x_d[(t + 1) * 128:(t + 2) * 128, :], [], [f"xt{1 - b}"], f"xt{1 - b}")
                    for nh in range(2):
                        pp = psf[(2 * t + nh) % 4]
                        key = f"psf{(2 * t + nh) % 4}"
                        for k in range(8):
                            S.op("pe", lambda e, pp=pp, k=k, t=t, nh=nh: e.matmul(pp[:, :], mixT[:, k, t * 128:(t + 1) * 128],
                                                                                  wo[:, k, nh * 512:(nh + 1) * 512],
                                                                                  start=(k == 0), stop=(k == 7)), ["mixT", "wo"], [key])
                        S.op("dve", lambda e, pp=pp, nh=nh: e.tensor_tensor(tmpo[:], pp[:, :], g12[:, nh * 512:(nh + 1) * 512], ALU.mult),
                             [key, "g12"], ["tmpo"])
                        S.op("pool", lambda e, t=t, nh=nh, b=b: e.tensor_tensor(x1[:, t, nh * 512:(nh + 1) * 512], tmpo[:],
                                                                               xt[b][:, nh * 512:(nh + 1) * 512], ALU.add),
                             ["tmpo", f"xt{b}"], [f"x1_{t}"])
                    if t < 8:
                        h2_norm(t, junk_o[:], xn_o[t % 3], f"xno{t % 3}", sso, rso)
                    if 2 <= t <= 9:
                        h2_trans(t - 2, xn_o[(t - 2) % 3], f"xno{(t - 2) % 3}")
                    if t == 10:
                        for i in range(2):
                            S.dma("pool", wub[i], wup_d[i], [], [f"wub{i}"], f"wub{i}")
                tap("x1", R1[:, 0:32768].bitcast(F32), [128, 16 * D])
                S.barrier()

        if stop_after == "F":
            aT = R2[:, 0:11264].rearrange("p (j n) -> p j n", n=1024)
            with ExitStack() as L2:
                wdn = sb(L2, "wdn", [128, 11, 1024], BF16)
                xn_f = [sb(L2, f"xnf{i}", [128, D], BF16) for i in range(2)]
                st_ssq = sb(L2, "stf_ssq", [128, 4], F32)
                st_rs = sb(L2, "stf_rs", [128, 4], F32)
                ug = sb(L2, "ug", [128, 1026], F32)
                uv = sb(L2, "uv", [128, 1026], F32)
                yg = sb(L2, "yg", [128, 1024], F32)
                yv = sb(L2, "yv", [128, 1024], F32)
                sg = sb(L2, "sg", [128, 1024], F32)
                halo = sb(L2, "halo", [128, 2 * NJ, 2], F32)
                tmpf = sb(L2, "tmpf", [128, 512], F32)
                S.dma("pool", wub[2], wup_d[2], [], ["wub2"], "wub2")
                S.dma("pool", wdn[:], wdn_d[:, 0:11, :], [], ["wdn"], "wdn")
                junk_f = sg[:, 0:512].bitcast(BF16)
                for H in range(2):
                    for JG in range(2):
                        for jj in range(11):
                            j = JG * 11 + jj
                            seq = (H * 2 + JG) * 11 + jj
                            wb = wub[seq % NWB]
                            wkey = f"wub{seq % NWB}"
                            if seq + NWB - 1 < 44:
                                nk = f"wub{(seq + NWB - 1) % NWB}"
                                S.dma("pool", wub[(seq + NWB - 1) % NWB], wup_d[(seq + NWB - 1) % 22], [], [nk], nk)
                            banks = {}
                            for tb in range(2):
                                for g_ in range(2):
                                    bi = (4 * seq + 2 * tb + g_) % 6
                                    banks[(tb, g_)] = bi
                                    for k in range(8):
                                        S.op("pe", lambda e, k=k, g_=g_, tb=tb, bi=bi: e.matmul(
                                            psf[bi][:, :], wb[:, k, g_, :], h2T[:, k, tb * 512:(tb + 1) * 512],
                                            start=(k == 0), stop=(k == 7)), [wkey, f"h2T_{k}"], [f"psf{bi}"])
                            for (g_, usb, ukey, ysb, ykey) in ((0, ug, "ug", yg, "yg"), (1, uv, "uv", yv, "yv")):
                                fc = j + g_ * NJ
                                if H == 0:
                                    S.op("pool", lambda e: e.memset(usb[:, 0:2], 0.0), [], [ukey])
                                else:
                                    S.op("pool", lambda e: e.tensor_copy(usb[:, 0:2], halo[:, fc, :]), [f"halo{fc}"], [ukey])
                                for tb in range(2):
                                    bi = banks[(tb, g_)]
                                    S.op("act", lambda e, tb=tb, bi=bi: e.activation(out=usb[:, 2 + tb * 512:514 + tb * 512], in_=psf[bi][:, :],
                                                                                   func=AF.Copy), [f"psf{bi}"], [ukey])
                                    S.op("act", lambda e, tb=tb, bi=bi: e.activation(
                                        out=ysb[:, tb * 512:(tb + 1) * 512], in_=psf[bi][:, :], func=AF.Identity,
                                        scale=convc[:, 2, fc:fc + 1], bias=convc[:, 3, fc:fc + 1]), [f"psf{bi}", "convc"], [ykey])
                                if H == 0:
                                    S.op("pool", lambda e: e.tensor_copy(halo[:, fc, :], usb[:, 1024:1026]), [ukey], [f"halo{fc}"])
                                S.op("dve", lambda e: e.scalar_tensor_tensor(
                                    out=ysb[:], in0=usb[:, 1:1025], scalar=convc[:, 1, fc:fc + 1], in1=ysb[:], op0=ALU.mult, op1=ALU.add),
                                    [ukey, ykey, "convc"], [ykey])
                                S.op("dve", lambda e: e.scalar_tensor_tensor(
                                    out=ysb[:], in0=usb[:, 0:1024], scalar=convc[:, 0, fc:fc + 1], in1=ysb[:], op0=ALU.mult, op1=ALU.add),
                                    [ukey, ykey, "convc"], [ykey])
                            S.op("act", lambda e: e.activation(out=sg[:], in_=yg[:], func=AF.Silu), ["yg"], ["sg"])
                            S.op("dve", lambda e: e.tensor_tensor(aT[:, jj, :], sg[:], yv[:], ALU.mult), ["sg", "yv"], ["aT"])
                        for tt in range(8):
                            t = H * 8 + tt
                            for nh in range(2):
                                pp = psf[4 + ((2 * tt + nh) % 2)]
                                pkey = f"psf{4 + ((2 * tt + nh) % 2)}"
                                for jj in range(11):
                                    S.op("pe", lambda e, pp=pp, jj=jj, tt=tt, nh=nh, JG=JG: e.matmul(
                                        pp[:, :], aT[:, jj, tt * 128:(tt + 1) * 128], wdn[:, jj, nh * 512:(nh + 1) * 512],
                                        start=(jj == 0), stop=(jj == 10)), ["aT", "wdn"], [pkey])
                                S.op("dve", lambda e, pp=pp, nh=nh: e.tensor_tensor(tmpf[:], pp[:, :], g12[:, 1024 + nh * 512:1024 + (nh + 1) * 512],
                                                                                  ALU.mult), [pkey, "g12"], ["tmpf"])
                                S.op("pool", lambda e, t=t, nh=nh: e.tensor_tensor(x1[:, t, nh * 512:(nh + 1) * 512], tmpf[:],
                                                                                 x1[:, t, nh * 512:(nh + 1) * 512], ALU.add),
                                     ["tmpf", f"x1_{t}"], [f"x1_{t}"])
                            if JG == 1:
                                S.dma("sp", out_d[t * 128:(t + 1) * 128, :], x1[:, t, :], [f"x1_{t}"], [], "outd")
                            if H == 0 and JG == 1:
                                h2_norm(8 + tt, junk_f, xn_f[tt % 2], f"xnf{tt % 2}", st_ssq, st_rs)
                                if tt >= 1:
                                    h2_trans(8 + tt - 1, xn_f[(tt - 1) % 2], f"xnf{(tt - 1) % 2}")
                        if H == 0 and JG == 1:
                            h2_trans(15, xn_f[1], "xnf1")
                        if not (H == 1 and JG == 1):
                            nJG = 1 - JG
                            S.dma("pool", wdn[:], wdn_d[:, nJG * 11:(nJG + 1) * 11, :], ["dummy"], ["wdn"], "wdn")
        else:
            with ExitStack() as L2:
                z = sb(L2, "zout", [128, D], F32)
                S.op("dve", lambda e: e.memset(z[:], 0.0), [], ["z"])
                for t in range(NT):
                    S.dma("sp", out_d[t * 128:(t + 1) * 128, :], z[:], ["z"], [], "outd")
        S.finish()
    return nc, list(tap_d.keys())


def _host_constants():
    ki = np.arange(128)[:, None]
    col = np.arange(16 * 128)[None, :]
    dist = col - ki
    cnt = ((dist >= 0) & (dist <= 128)).astype(np.float32)
    cnt += ((dist >= 0) & (dist <= 512) & (dist % 4 == 0)).astype(np.float32)
    cnt += ((dist >= 0) & (dist <= 2048) & (dist % 16 == 0)).astype(np.float32)
    caus = (np.arange(128)[None, :] >= ki).astype(np.float32)
    masks = np.concatenate([cnt, caus], axis=1).astype(np.float32)
    inv_m = np.power(np.float32(10000.0), (-2.0 * np.arange(16, dtype=np.float32) / np.float32(32))).astype(np.float32)
    inv_d = np.power(np.float32(10000.0), (-2.0 * np.arange(32, dtype=np.float32) / np.float32(64))).astype(np.float32)
    invf = np.concatenate([inv_m, inv_d])[None, :].astype(np.float32)
    return masks, invf


def _prep_inputs(inp):
    f = lambda a: np.ascontiguousarray(np.asarray(a))
    masks, invf = _host_constants()
    w_ada = f(inp["w_ada"])[0]
    b_ada = f(inp["b_ada"])[0]
    w_in = f(inp["w_in"])[0]
    shared = {
        "w_ada": f(w_ada.reshape(8, 128, 6 * D).transpose(1, 0, 2)),
        "b_ada_col": f(b_ada.reshape(48, 128).T),
        "b_ada_g": f(np.concatenate([b_ada[2048:3072], b_ada[5120:6144]])[None, :]),
        "gcols": f(np.concatenate([f(inp["g_mix_norm"])[0].reshape(8, 128).T, f(inp["g_ffn_norm"])[0].reshape(8, 128).T], axis=1)),
        "gains": f(np.concatenate([f(inp["g_q_lat"])[0], f(inp["g_kv_lat"])[0], f(inp["g_mla_q_nope"])[0], f(inp["g_mla_q_pe"])[0],
                                   f(inp["g_mla_k_nope"])[0], f(inp["g_mla_k_pe"])[0], f(inp["g_dil_q"])[0], f(inp["g_dil_k"])[0]])[None, :]),
        "invf": invf,
        "w_inM": f(w_in[:, 0:800].reshape(8, 128, 800).transpose(1, 0, 2)),
        "w_inD": f(w_in[:, 800:2336].reshape(8, 128, 1536).transpose(1, 0, 2)),
        "w_qb": f(f(inp["w_q_b"])[0].reshape(4, 128, 768).transpose(1, 0, 2)),
        "w_kvb": f(f(inp["w_kv_b"])[0].reshape(2, 128, 1024).transpose(1, 0, 2)),
        "w_o": f(f(inp["w_o"])[0].reshape(8, 128, 1024).transpose(1, 0, 2)),
        "w_up": f(f(inp["w_up"])[0].reshape(8, 128, 2, NJ, 128).transpose(3, 1, 0, 2, 4)),
        "w_down": f(f(inp["w_down"])[0].reshape(NJ, 128, 1024).transpose(1, 0, 2)),
        "convcol": f(np.concatenate([f(inp["w_conv"])[0], f(inp["b_conv"])], axis=0).reshape(4, 2 * NJ, 128).transpose(2, 0, 1)),
        "masks": masks,
    }
    shared = {k: v.astype(np.float32) for k, v in shared.items()}
    x = f(inp["x"]); c = f(inp["c"]); pos = f(inp["positions"])
    maps = []
    for b in range(8):
        m = dict(shared)
        m["x"] = f(x[b]).astype(np.float32)
        m["ccol"] = f(c[b].reshape(8, 128).T).astype(np.float32)
        m["pos"] = f(pos[b].reshape(NT, 128).T).astype(np.int32)
        maps.append(m)
    return maps


_CACHE = {}


def kernel(**inputs):
    maps = _prep_inputs(inputs)
    if "nc" not in _CACHE:
        _CACHE["nc"] = build_program("F")[0]
    res = run_bass_kernel_spmd(_CACHE["nc"], maps, core_ids=list(range(8)))
    out = np.stack([np.asarray(r["out"]).reshape(S_LEN, D) for r in res.results], axis=0)
    return out.astype(np.float32)
```

```python
import numpy as np
from contextlib import ExitStack

import concourse.bass as bass
import concourse.mybir as mybir
from concourse.bass_utils import run_bass_kernel_spmd

F32 = mybir.dt.float32
BF16 = mybir.dt.bfloat16
I32 = mybir.dt.int32
AF = mybir.ActivationFunctionType
ALU = mybir.AluOpType
AX = mybir.AxisListType

D = 1024
S_LEN = 2048
NT = 16
EPS = 1e-6
DFF = 2816
NJ = 22
TWO_PI = 6.283185307179586
PI = 3.141592653589793
C_HI = 6.28125
C_LO = TWO_PI - C_HI

G_QLAT, G_KVLAT, G_QN, G_QP, G_KN, G_KP, G_DQ, G_DK = 0, 512, 768, 832, 864, 928, 960, 1024
G_TOT = 1088


class Sched:
    CE = ("pe", "act", "dve", "pool")
    STRICT = True

    def __init__(self, nc):
        self.nc = nc
        self.E = {"pe": nc.tensor, "act": nc.scalar, "dve": nc.vector, "pool": nc.gpsimd, "sp": nc.sync}
        self.sem = {e: nc.alloc_semaphore(name=f"s_{e}") for e in self.CE}
        self.cnt = {e: 0 for e in self.CE}
        self.known = {e: {} for e in self.E}
        self.res = {}
        self.dsem = {}
        self.nwaits = 0

    def _wait(self, eng, tok):
        src, val = tok
        if self.known[eng].get(src, 0) >= val:
            return
        sem = self.sem[src] if src in self.sem else self.dsem[src[2:]][0]
        self.E[eng].wait_ge(sem, val)
        self.known[eng][src] = val
        self.nwaits += 1

    def _deps(self, eng, reads, writes):
        for r in reads:
            st = self.res.get(r)
            if st and st[0]:
                self._wait(eng, st[0])
            if st and r.startswith("ps"):
                for t in st[1].values():
                    if t[0] != eng:
                        self._wait(eng, t)
        strict = self.STRICT and eng != "pe"
        for w in writes:
            st = self.res.get(w)
            if st:
                if st[0] and (st[0][0] != eng or strict):
                    self._wait(eng, st[0])
                for t in st[1].values():
                    if t[0] != eng or strict:
                        self._wait(eng, t)

    def _commit(self, tok, reads, writes):
        for r in reads:
            st = self.res.setdefault(r, [None, {}])
            st[1][tok[0]] = tok
        for w in writes:
            self.res[w] = [tok, {}]

    def op(self, eng, fn, reads=(), writes=()):
        self._deps(eng, reads, writes)
        ins = fn(self.E[eng])
        self.cnt[eng] += 1
        ins.then_inc(self.sem[eng], 1)
        self._commit((eng, self.cnt[eng]), reads, writes)

    def dma(self, q, out, in_, reads, writes, key):
        self._deps(q, reads, writes)
        if key not in self.dsem:
            self.dsem[key] = [self.nc.alloc_semaphore(name="d_" + key), 0]
        ins = self.E[q].dma_start(out=out, in_=in_)
        self.dsem[key][1] += 16
        ins.then_inc(self.dsem[key][0], 16)
        self._commit(("d:" + key, self.dsem[key][1]), reads, writes)

    def barrier(self, engines=None):
        for e in (engines or self.E):
            for f in self.CE:
                if f != e and self.cnt[f] > 0:
                    self._wait(e, (f, self.cnt[f]))
            for key, (sem, c) in self.dsem.items():
                if c > 0:
                    self._wait(e, ("d:" + key, c))
        if engines is None:
            self.res = {}

    def finish(self):
        self.barrier(engines=["sp"])


def build_program(stop_after="F", taps=()):
    nc = bass.Bass("TRN2", target_bir_lowering=False)
    dr = {}

    def din(name, shape, dt=F32):
        dr[name] = nc.dram_tensor(name, list(shape), dt, kind="ExternalInput").ap()
        return dr[name]

    x_d = din("x", [S_LEN, D])
    ccol_d = din("ccol", [128, 8])
    pos_d = din("pos", [128, NT], I32)
    wada_d = din("w_ada", [128, 8, 6 * D])
    badac_d = din("b_ada_col", [128, 48])
    badag_d = din("b_ada_g", [1, 2048])
    gcols_d = din("gcols", [128, 16])
    gains_d = din("gains", [1, G_TOT])
    invf_d = din("invf", [1, 48])
    winM_d = din("w_inM", [128, 8, 800])
    winD_d = din("w_inD", [128, 8, 1536])
    wqb_d = din("w_qb", [128, 4, 768])
    wkvb_d = din("w_kvb", [128, 2, 1024])
    wo_d = din("w_o", [128, 8, 1024])
    wup_d = din("w_up", [NJ, 128, 8, 2, 128])
    wdn_d = din("w_down", [128, NJ, 1024])
    convc_d = din("convcol", [128, 4, 2 * NJ])
    masks_d = din("masks", [128, 17 * 128])
    out_d = nc.dram_tensor("out", [S_LEN, D], F32, kind="ExternalOutput").ap()
    tap_d = {}

    S = Sched(nc)
    L0 = ExitStack()

    uid = [0]

    def sb(stack, name, shape, dt=F32):
        uid[0] += 1
        return stack.enter_context(nc.sbuf_tensor(f"sb{uid[0]}_{name}", list(shape), dt))

    with L0:
        ident = sb(L0, "ident", [128, 128], BF16)
        ones32 = sb(L0, "ones32", [128, 64], F32)
        gains = sb(L0, "gains_sb", [128, G_TOT], F32)
        cosT = sb(L0, "cosT", [128, NT, 48], F32)
        sinT = sb(L0, "sinT", [128, NT, 48], F32)
        gv1 = sb(L0, "gv1", [128, 8], F32)
        sh1 = sb(L0, "sh1", [128, 8], F32)
        gv2 = sb(L0, "gv2", [128, 8], F32)
        sh2 = sb(L0, "sh2", [128, 8], F32)
        convc = sb(L0, "convc", [128, 4, 2 * NJ], F32)
        g12 = sb(L0, "g12", [128, 2048], F32)
        wmask = sb(L0, "wmask", [128, 17 * 128], BF16)
        epsb = sb(L0, "epsb", [128, 1], F32)
        R1 = sb(L0, "R1", [128, 45056], BF16)
        R2 = sb(L0, "R2", [128, 16384], BF16)
        psb = [L0.enter_context(nc.psum_tensor(f"psb{i}", [128, 1024], BF16)) for i in range(2)]
        psf = [L0.enter_context(nc.psum_tensor(f"psf{i}", [128, 512], F32)) for i in range(6)]

        mixT = R2[:, :].rearrange("p (c n) -> p c n", n=S_LEN)

        def tap(name, ap, shape, dt=F32, key=None):
            if name not in taps:
                return
            t = nc.dram_tensor("tap_" + name, list(shape), dt, kind="ExternalOutput").ap()
            tap_d[name] = t
            S.barrier(engines=["sp"])
            S.dma("sp", out=t, in_=ap, reads=[], writes=[], key="tap")

        def mod_chunks(ns, wa_bufs, wkeys, scb_t, screp_t, badag_t, pcol, pcol_key, pg, pg_key, n_last):
            ns = list(ns)
            for i_n, n in enumerate(ns):
                buf = wa_bufs[n % 2]
                wkey = wkeys[n % 2]
                kind = n // 2
                if kind in (2, 5):
                    for k in range(8):
                        S.op("pe", lambda e, k=k: e.matmul(pg, screp_t[:, k, :], buf[:, k, :], start=(k == 0), stop=(k == 7)),
                             ["screp", wkey], [pg_key])
                        if k % 2 == 1:
                            yield
                    off = (0 if kind == 2 else 1024) + (n % 2) * 512
                    S.op("dve", lambda e: e.tensor_tensor(g12[:, off:off + 512], pg, badag_t[:, off:off + 512], ALU.add),
                         [pg_key, "badag"], ["g12"])
                else:
                    for c4 in range(4):
                        idx = n * 4 + c4
                        for k in range(8):
                            S.op("pe", lambda e, k=k: e.matmul(pcol[:, idx:idx + 1], buf[:, k, c4 * 128:(c4 + 1) * 128],
                                                               scb_t[:, k:k + 1], start=(k == 0), stop=(k == 7)),
                                 ["scb", wkey], [pcol_key])
                        yield
                if i_n + 2 < len(ns):
                    n2 = ns[i_n + 2]
                    assert n2 % 2 == n % 2
                    S.dma("pool", buf[:], wada_d[:, :, n2 * 512:(n2 + 1) * 512], [], [wkey], wkey)
                for _ in range(3):
                    yield

        with ExitStack() as L1:
            identf = sb(L1, "identf", [128, 128], F32)
            cc = sb(L1, "cc", [128, 8], F32)
            scb = sb(L1, "scb", [128, 8], BF16)
            screp = sb(L1, "screp", [128, 8, 128], BF16)
            wa = [sb(L1, f"wa{i}", [128, 8, 512], BF16) for i in range(2)]
            badac = sb(L1, "badac", [128, 48], F32)
            gcols = sb(L1, "gcols", [128, 16], F32)
            modc = sb(L1, "modc", [128, 48], F32)
            posi = sb(L1, "posi", [128, NT], I32)
            posf = sb(L1, "posf", [128, NT], F32)
            invf = sb(L1, "invf_sb", [128, 48], F32)
            ang = sb(L1, "ang", [128, NT * 48], F32)
            tq = sb(L1, "tq", [128, NT * 48], F32)
            ki = sb(L1, "ki", [128, NT * 48], I32)
            kf = sb(L1, "kf", [128, NT * 48], F32)
            rr = sb(L1, "rr", [128, NT * 48], F32)
            rc = sb(L1, "rc", [128, NT * 48], F32)

            S.dma("sp", cc[:], ccol_d, [], ["cc"], "c0")
            S.dma("sp", badac[:], badac_d, [], ["badac"], "c1")
            S.dma("sp", gcols[:], gcols_d, [], ["gcols"], "c2")
            S.dma("sp", posi[:], pos_d, [], ["posi"], "c3")
            S.dma("sp", invf[:], invf_d.partition_broadcast(128), [], ["invf"], "c4")
            S.dma("sp", gains[:], gains_d.partition_broadcast(128), [], ["gains"], "c5")
            S.dma("sp", convc[:], convc_d, [], ["convc"], "c6")
            S.dma("pool", wmask[:], masks_d, [], ["wmask"], "c8")
            for n in range(2):
                S.dma("pool", wa[n][:], wada_d[:, :, n * 512:(n + 1) * 512], [], [f"wa{n}"], f"wa{n}")

            S.op("pool", lambda e: e.memset(identf[:], 1.0), [], ["identf"])
            S.op("pool", lambda e: e.affine_select(out=identf[:], in_=identf[:], pattern=[[-1, 128]],
                                                   compare_op=ALU.is_equal, fill=0.0, base=0,
                                                   channel_multiplier=1), ["identf"], ["identf"])
            S.op("dve", lambda e: e.tensor_copy(ident[:], identf[:]), ["identf"], ["ident"])
            S.op("dve", lambda e: e.memset(ones32[:], 1.0), [], ["ones32"])
            S.op("dve", lambda e: e.memset(epsb[:], EPS), [], ["epsb"])

            S.op("act", lambda e: e.activation(out=scb[:], in_=cc[:], func=AF.Silu), ["cc"], ["scb"])
            S.op("dve", lambda e: e.tensor_copy(screp[:], scb[:].unsqueeze(2).broadcast_to([128, 8, 128])),
                 ["scb"], ["screp"])

            for _ in mod_chunks(range(4), wa, ["wa0", "wa1"], scb, screp, None, psf[0], "psf0", None, None, 3):
                pass
            S.op("dve", lambda e: e.tensor_tensor(modc[:, 0:16], psf[0][:, 0:16], badac[:, 0:16], ALU.add),
                 ["psf0", "badac"], ["modc"])
            S.op("dve", lambda e: e.scalar_tensor_tensor(out=gv1[:], in0=modc[:, 8:16], scalar=1.0, in1=gcols[:, 0:8],
                                                        op0=ALU.add, op1=ALU.mult), ["modc", "gcols"], ["gv1"])
            S.op("dve", lambda e: e.tensor_copy(sh1[:], modc[:, 0:8]), ["modc"], ["sh1"])

            S.op("dve", lambda e: e.tensor_copy(posf[:], posi[:]), ["posi"], ["posf"])
            angv = ang[:].rearrange("p (t f) -> p t f", f=48)
            S.op("dve", lambda e: e.tensor_tensor(angv, posf[:].unsqueeze(2).broadcast_to([128, NT, 48]),
                                                  invf[:].unsqueeze(1).broadcast_to([128, NT, 48]), ALU.mult),
                 ["posf", "invf"], ["ang"])
            S.op("dve", lambda e: e.tensor_scalar(tq[:], ang[:], 1.0 / TWO_PI, None, ALU.mult), ["ang"], ["tq"])
            S.op("dve", lambda e: e.tensor_copy(ki[:], tq[:]), ["tq"], ["ki"])
            S.op("dve", lambda e: e.tensor_copy(kf[:], ki[:]), ["ki"], ["kf"])
            S.op("dve", lambda e: e.scalar_tensor_tensor(out=rr[:], in0=kf[:], scalar=-C_HI, in1=ang[:],
                                                        op0=ALU.mult, op1=ALU.add), ["kf", "ang"], ["rr"])
            S.op("dve", lambda e: e.scalar_tensor_tensor(out=rr[:], in0=kf[:], scalar=-C_LO, in1=rr[:],
                                                        op0=ALU.mult, op1=ALU.add), ["kf", "rr"], ["rr"])
            S.op("dve", lambda e: e.tensor_scalar(rc[:], rr[:], PI / 2, -TWO_PI, ALU.is_gt, ALU.mult), ["rr"], ["rc"])
            S.op("dve", lambda e: e.scalar_tensor_tensor(out=rc[:], in0=rr[:], scalar=PI / 2, in1=rc[:],
                                                        op0=ALU.add, op1=ALU.add), ["rr", "rc"], ["rc"])
            for buf, key in ((rr, "rr"), (rc, "rc")):
                S.op("dve", lambda e, buf=buf: e.tensor_scalar(buf[:], buf[:], PI, -PI, ALU.min, ALU.max), [key], [key])
            S.op("act", lambda e: e.activation(out=sinT[:].rearrange("p t f -> p (t f)"), in_=rr[:], func=AF.Sin),
                 ["rr"], ["sinT"])
            S.op("act", lambda e: e.activation(out=cosT[:].rearrange("p t f -> p (t f)"), in_=rc[:], func=AF.Sin),
                 ["rc"], ["cosT"])
            tap("gv1", gv1[:], [128, 8])
            tap("sh1", sh1[:], [128, 8])
            tap("cosT", cosT[:].rearrange("p t f -> p (t f)"), [128, NT * 48])
            tap("sinT", sinT[:].rearrange("p t f -> p (t f)"), [128, NT * 48])
            S.barrier()

        def rstd_from_ssq(st, ssq_ap, out_ap, n, key_in, key_out, scr_ap, key_scr):
            S.op("act", lambda e: e.activation(out=scr_ap, in_=ssq_ap, func=AF.Sqrt, scale=1.0 / n, bias=epsb[:, 0:1]),
                 [key_in], [key_scr])
            S.op("dve", lambda e: e.reciprocal(out_ap, scr_ap), [key_scr], [key_out])

        def norm_transpose(xtile, key_x, rstd_ap, key_r, gv, sh, xn, hdst, key_h, pb, key_pb):
            S.op("dve", lambda e: e.tensor_scalar(xn[:], xtile, rstd_ap, None, ALU.mult), [key_x, key_r], ["xn"])
            for c in range(8):
                bk = psb[c // 4]
                S.op("pe", lambda e, c=c, bk=bk: e.transpose(bk[:, (c % 4) * 128:(c % 4 + 1) * 128], xn[:, c * 128:(c + 1) * 128], ident[:]),
                     ["xn", "ident"], [f"psb{c // 4}"])
            for cc in range(4):
                c = cc
                S.op("act", lambda e, c=c: e.activation(out=hdst[:, c, :], in_=psb[0][:, (c % 4) * 128:(c % 4 + 1) * 128],
                                                        func=AF.Identity, scale=gv[:, c:c + 1], bias=sh[:, c:c + 1]),
                     ["psb0", "gv", "sh"], [f"{key_h}_{c}"])
                c = 4 + cc
                S.op("dve", lambda e, c=c: e.tensor_scalar(hdst[:, c, :], psb[1][:, (c % 4) * 128:(c % 4 + 1) * 128],
                                                           gv[:, c:c + 1], sh[:, c:c + 1], ALU.mult, ALU.add),
                     ["psb1", "gv", "sh"], [f"{key_h}_{c}"])

        def group_norm(src3, nh, dh, gain_ap, dst3, st, tagp, src_keys=None, sfx=""):
            sq, ss, rs, rs2 = st
            sqv = sq[:, 0:nh * dh].rearrange("p (h d) -> p h d", d=dh)
            src_keys = src_keys or [tagp + "src"]
            ksq, kss, krs, krs2 = "sq" + sfx, "ss" + sfx, "rsg" + sfx, "rs2g" + sfx
            S.op("act", lambda e: e.activation(out=sqv, in_=src3, func=AF.Square), src_keys, [ksq])
            yield
            S.op("dve", lambda e: e.tensor_reduce(out=ss[:, 0:nh], in_=sqv, axis=AX.X, op=ALU.add), [ksq], [kss])
            yield
            S.op("act", lambda e: e.activation(out=rs[:, 0:nh], in_=ss[:, 0:nh], func=AF.Sqrt, scale=1.0 / dh, bias=epsb[:, 0:1]),
                 [kss], [krs])
            yield
            S.op("dve", lambda e: e.reciprocal(rs2[:, 0:nh], rs[:, 0:nh]), [krs], [krs2])
            yield
            S.op("dve", lambda e: e.tensor_tensor(sqv, src3, gain_ap.unsqueeze(1).broadcast_to([128, nh, dh]), ALU.mult),
                 src_keys + ["gains"], [ksq])
            yield
            S.op("dve", lambda e: e.tensor_tensor(dst3, sqv, rs2[:, 0:nh].unsqueeze(2).broadcast_to([128, nh, dh]), ALU.mult),
                 [ksq, krs2], [tagp + "dst"])
            yield

        def rope(src3, nh, half, cos_ap, sin_ap, dst3, tmp, key_src, key_dst, sfx=""):
            x1 = src3[:, :, 0:half]
            x2 = src3[:, :, half:2 * half]
            cb = cos_ap.unsqueeze(1).broadcast_to([128, nh, half])
            sbb = sin_ap.unsqueeze(1).broadcast_to([128, nh, half])
            ta = tmp[0][:, 0:nh * half].rearrange("p (h d) -> p h d", d=half)
            tb = tmp[1][:, 0:nh * half].rearrange("p (h d) -> p h d", d=half)
            ka, kb = "ropa" + sfx, "ropb" + sfx
            S.op("dve", lambda e: e.tensor_tensor(ta, x1, cb, ALU.mult), [key_src, "cosT"], [ka])
            yield
            S.op("dve", lambda e: e.tensor_tensor(tb, x2, sbb, ALU.mult), [key_src, "sinT"], [kb])
            yield
            S.op("dve", lambda e: e.tensor_tensor(dst3[:, :, 0:half], ta, tb, ALU.subtract), [ka, kb], [key_dst])
            yield
            S.op("dve", lambda e: e.tensor_tensor(ta, x1, sbb, ALU.mult), [key_src, "sinT"], [ka])
            yield
            S.op("dve", lambda e: e.tensor_tensor(tb, x2, cb, ALU.mult), [key_src, "cosT"], [kb])
            yield
            S.op("dve", lambda e: e.tensor_tensor(dst3[:, :, half:2 * half], ta, tb, ALU.add), [ka, kb], [key_dst])
            yield

        def run_interleaved(gen_fns, max_active=2):
            locks = {}
            pending = list(gen_fns)
            active = []

            def maybe_start():
                if pending and len(active) < max_active and (not active or active[-1]["spawned"]):
                    active.append({"g": pending.pop(0)(), "req": None, "spawned": False})

            maybe_start()
            while active:
                progressed = False
                for ent in list(active):
                    g = ent["g"]
                    if ent["req"] is not None:
                        if locks.get(ent["req"]) is None:
                            locks[ent["req"]] = g
                            ent["req"] = None
                        else:
                            continue
                    try:
                        r = next(g)
                    except StopIteration:
                        active.remove(ent)
                        assert not [k for k, v in locks.items() if v is g], locks
                        progressed = True
                        maybe_start()
                        continue
                    progressed = True
                    if r is None:
                        pass
                    elif r == "spawn":
                        ent["spawned"] = True
                        maybe_start()
                    elif r[0] == "acq":
                        assert locks.get(r[1]) is not g, r
                        if locks.get(r[1]) is None:
                            locks[r[1]] = g
                        else:
                            ent["req"] = r[1]
                    elif r[0] == "rel":
                        assert locks.get(r[1]) is g, r
                        locks[r[1]] = None
                assert progressed, ("interleave deadlock", locks)

        def attention(qT_of, kT_of, v_of, scale, dil, chunk0, Pt, rec, bcs, side=None, LA=2, sbanks=None):
            chunks = []
            gidx = 0
            for h in range(8):
                for QB in range(4):
                    nkt = 4 * (QB + 1)
                    for c in range(nkt):
                        i0 = max(0, c - 4 * QB)
                        chunks.append(dict(h=h, QB=QB, c=c, i0=i0, q0=i0 * 128, n=512 - i0 * 128,
                                           first=(c == 0), last=(c == nkt - 1), g=gidx))
                    gidx += 1
            if sbanks is None:
                sbanks = [(psf[0], "psf0"), (psf[1], "psf1"), (psf[2], "psf2")]
            nS = len(sbanks)
            assert len(Pt) >= nS and LA < nS
            deferred = []

            def emit_S(i):
                ck = chunks[i]
                h, QB, c, q0, n = ck["h"], ck["QB"], ck["c"], ck["q0"], ck["n"]
                qT, kT = qT_of(h), kT_of(h)
                sp_, skey = sbanks[i % nS]
                P = Pt[i % nS]; pkey = f"P{i % nS}"
                S.op("pe", lambda e: e.matmul(sp_[:, 0:n], kT[:, c * 128:(c + 1) * 128],
                                              qT[:, QB * 512 + q0:(QB + 1) * 512], start=True, stop=True),
                     ["KT", "QT"], [skey])
                S.op("act", lambda e: e.activation(out=P[:, 0:n], in_=sp_[:, 0:n], func=AF.Exp, scale=scale),
                     [skey], [pkey])
                if dil:
                    d0 = 4 * QB + ck["i0"] - c
                    S.op("dve", lambda e: e.tensor_tensor(P[:, 0:n], P[:, 0:n], wmask[:, d0 * 128:d0 * 128 + n], ALU.mult),
                         [pkey, "wmask"], [pkey])
                elif c >= 4 * QB:
                    S.op("dve", lambda e: e.tensor_tensor(P[:, 0:128], P[:, 0:128], wmask[:, 16 * 128:17 * 128], ALU.mult),
                         [pkey, "wmask"], [pkey])

            def emit_PV(i, it):
                ck = chunks[i]
                h, QB, c, q0, n, g = ck["h"], ck["QB"], ck["c"], ck["q0"], ck["n"], ck["g"]
                ot = psf[3 + (g % 2)]; okey = f"psf{3 + (g % 2)}"
                P = Pt[i % nS]; pkey = f"P{i % nS}"
                S.op("pe", lambda e: e.matmul(ot[:, q0:512], v_of(h, c), P[:, 0:n], start=ck["first"], stop=ck["last"]),
                     [pkey, "V"], [okey])
                if not ck["last"]:
                    return
                nlo, dlo = (0, 64) if h % 2 == 0 else (64, 0)
                rk = f"rec{g % 2}"
                rr_ = rec[g % 2]
                S.op("act", lambda e: e.activation(out=rr_[dlo:dlo + 1, :], in_=ot[dlo:dlo + 1, :], func=AF.Ln), [okey], [rk])
                S.op("act", lambda e: e.activation(out=rr_[dlo:dlo + 1, :], in_=rr_[dlo:dlo + 1, :], func=AF.Exp, scale=-1.0), [rk], [rk])
                ch = chunk0 + h // 2

                def fin():
                    S.op("pe", lambda e: e.matmul(psf[5][nlo:nlo + 64, :], ones32[dlo:dlo + 1, 0:64], rr_[dlo:dlo + 1, :],
                                                  start=True, stop=True), [rk, "ones32"], ["psf5"])
                    S.op("dve", lambda e: e.tensor_copy(bcs[nlo:nlo + 64, :], psf[5][nlo:nlo + 64, :]), ["psf5"], ["bcs"])
                    S.op("dve", lambda e: e.tensor_tensor(mixT[nlo:nlo + 64, ch, QB * 512:(QB + 1) * 512], ot[nlo:nlo + 64, :],
                                                          bcs[nlo:nlo + 64, :], ALU.mult), [okey, "bcs"], ["mixT"])
                deferred.append((it + 3, fin))

            nC = len(chunks)
            for it in range(nC + LA + 6):
                if side is not None and it % 3 == 2:
                    next(side, None)
                if it < nC:
                    emit_S(it)
                j = it - LA
                if 0 <= j < nC:
                    emit_PV(j, it)
                while deferred and deferred[0][0] <= it:
                    deferred.pop(0)[1]()
            assert not deferred

        def v_layout_store(Vx, t, src_ps_list, key_src_list, e_act=True):
            pass

        if stop_after != "0":
            QT = R1[:, 0:16384].rearrange("p (h n) -> p h n", n=S_LEN)
            KT = R1[:, 16384:32768].rearrange("p (h n) -> p h n", n=S_LEN)
            Vm = R1[:, 32768:45056].rearrange("p (t j c) -> p t j c", j=4, c=192)
            QdT = R1[:, 0:8192].rearrange("p (j n) -> p j n", n=S_LEN)
            QdT1 = R1[:, 16384:24576].rearrange("p (j n) -> p j n", n=S_LEN)
            KdT = R1[:, 8192:16384].rearrange("p (j n) -> p j n", n=S_LEN)
            Vd = Vm
            S.op("pool", lambda e: e.memset(Vm[:, :, :, 64:128], 1.0), [], ["V"])

            with ExitStack() as L2:
                winM = sb(L2, "winM", [128, 8, 800], BF16)
                wqb = sb(L2, "wqb", [128, 4, 768], BF16)
                wkvb = sb(L2, "wkvb", [128, 2, 1024], BF16)
                xt = [sb(L2, f"xt{i}", [128, D], F32) for i in range(2)]
                xn = sb(L2, "xn", [128, D], BF16)
                hT = [sb(L2, f"hT{i}", [128, 8, 128], BF16) for i in range(2)]
                st_ssq = sb(L2, "st_ssq", [128, 8], F32)
                st_rs = sb(L2, "st_rs", [128, 8], F32)
                st_rs2 = sb(L2, "st_rs2", [128, 8], F32)
                sqs = [sb(L2, f"sq{i}", [128, 1280], F32) for i in range(2)]
                sss = [sb(L2, f"ss{i}", [128, 24], F32) for i in range(2)]
                rsgs = [sb(L2, f"rsg{i}", [128, 24], F32) for i in range(2)]
                rs2gs = [sb(L2, f"rs2g{i}", [128, 24], F32) for i in range(2)]
                qln = sb(L2, "qln", [128, 512], BF16)
                kvn = sb(L2, "kvn", [128, 256], BF16)
                latT = sb(L2, "latT", [128, 6, 128], BF16)
                kpe = sb(L2, "kpe", [128, 32], F32)
                kpe2 = sb(L2, "kpe2", [128, 32], F32)
                kper = [sb(L2, f"kper{i}", [128, 32], BF16) for i in range(2)]
                ropts = [[sb(L2, f"ropt{p}{i}", [128, 256], F32) for i in range(2)] for p in range(2)]

                S.dma("pool", winM[:], winM_d, [], ["winM"], "winM")
                S.dma("pool", wqb[:], wqb_d, [], ["wqb"], "wqb")
                S.dma("pool", wkvb[:], wkvb_d, [], ["wkvb"], "wkvb")
                S.dma("sp", xt[0][:], x_d[0:128, :], [], ["xt0"], "xt0")
                def tileM(t):
                    b = t % 2
                    sq = sqs[b]
                    gst = (sqs[b], sss[b], rsgs[b], rs2gs[b])
                    ropt = ropts[b]
                    junk = sqs[b][:, 0:512].bitcast(BF16)
                    yield ("acq", "X")
                    if t + 1 < NT:
                        S.dma("sp", xt[1 - b][:], x_d[(t + 1) * 128:(t + 2) * 128, :], [], [f"xt{1 - b}"], f"xt{1 - b}")
                    S.op("act", lambda e, b=b: e.activation(out=junk, in_=xt[b][:], func=AF.Square,
                                                            accum_out=st_ssq[:, 0:1]), [f"xt{b}"], [f"sq{b}", "ssq0"])
                    rstd_from_ssq(None, st_ssq[:, 0:1], st_rs[:, 0:1], D, "ssq0", "rs0", st_ssq[:, 1:2], "ssq0b")
                    yield
                    yield ("acq", "psb")
                    norm_transpose(xt[b][:], f"xt{b}", st_rs[:, 0:1], "rs0", gv1, sh1, xn, hT[b], f"hT{b}", psb[0], "psb0")
                    yield ("rel", "psb")
                    yield ("rel", "X")
                    yield "spawn"
                    yield ("acq", "P")
                    for k in range(8):
                        S.op("pe", lambda e, k=k, b=b: e.matmul(psf[0][:, :], hT[b][:, k, :], winM[:, k, 0:512],
                                                                start=(k == 0), stop=(k == 7)), [f"hT{b}_{k}", "winM"], ["psf0"])
                    for k in range(8):
                        S.op("pe", lambda e, k=k, b=b: e.matmul(psf[1][:, 0:288], hT[b][:, k, :], winM[:, k, 512:800],
                                                                start=(k == 0), stop=(k == 7)), [f"hT{b}_{k}", "winM"], ["psf1"])
                    yield
                    S.op("act", lambda e: e.activation(out=sq[:, 0:512], in_=psf[0][:, :], func=AF.Square,
                                                       accum_out=st_ssq[:, 4:5]), ["psf0"], [f"sq{b}", "ssqP0"])
                    yield
                    S.op("act", lambda e: e.activation(out=sq[:, 512:768], in_=psf[1][:, 0:256], func=AF.Square, scale=2.0 ** 0.5,
                                                       accum_out=st_ssq[:, 5:6]), ["psf1"], [f"sq{b}", "ssqP1"])
                    yield
                    S.op("act", lambda e: e.activation(out=sq[:, 768:800], in_=psf[1][:, 256:288], func=AF.Square, scale=4.0,
                                                       accum_out=st_ssq[:, 6:7]), ["psf1"], [f"sq{b}", "ssqP2"])
                    yield
                    S.op("act", lambda e: e.activation(out=st_rs[:, 4:7], in_=st_ssq[:, 4:7], func=AF.Sqrt, scale=1.0 / 512, bias=epsb[:, 0:1]),
                         ["ssqP0", "ssqP1", "ssqP2"], ["rsP"])
                    yield
                    S.op("dve", lambda e: e.reciprocal(st_rs2[:, 4:7], st_rs[:, 4:7]), ["rsP"], ["rs2P"])
                    yield
                    S.op("dve", lambda e: e.scalar_tensor_tensor(out=qln[:], in0=psf[0][:, :], scalar=st_rs2[:, 4:5],
                                                                in1=gains[:, G_QLAT:G_QLAT + 512], op0=ALU.mult, op1=ALU.mult),
                         ["psf0", "rs2P", "gains"], ["qln"])
                    yield
                    S.op("dve", lambda e: e.scalar_tensor_tensor(out=kvn[:], in0=psf[1][:, 0:256], scalar=st_rs2[:, 5:6],
                                                                in1=gains[:, G_KVLAT:G_KVLAT + 256], op0=ALU.mult, op1=ALU.mult),
                         ["psf1", "rs2P", "gains"], ["kvn"])
                    yield
                    S.op("dve", lambda e: e.scalar_tensor_tensor(out=kpe2[:], in0=psf[1][:, 256:288], scalar=st_rs2[:, 6:7],
                                                                in1=gains[:, G_KP:G_KP + 32], op0=ALU.mult, op1=ALU.mult),
                         ["psf1", "rs2P", "gains"], ["kpedst"])
                    yield
                    yield from rope(kpe2[:].rearrange("p (h d) -> p h d", d=32), 1, 16, cosT[:, t, 0:16], sinT[:, t, 0:16],
                         kper[b][:].rearrange("p (h d) -> p h d", d=32), ropt, "kpedst", f"kper{b}", sfx=str(b))
                    yield
                    yield ("acq", "Q")
                    yield ("acq", "psb")
                    for c in range(4):
                        S.op("pe", lambda e, c=c: e.transpose(psb[1][:, c * 128:(c + 1) * 128], qln[:, c * 128:(c + 1) * 128], ident[:]),
                             ["qln", "ident"], ["psb1"])
                    for c in range(2):
                        S.op("pe", lambda e, c=c: e.transpose(psb[1][:, (4 + c) * 128:(5 + c) * 128], kvn[:, c * 128:(c + 1) * 128], ident[:]),
                             ["kvn", "ident"], ["psb1"])
                    S.op("act", lambda e: e.activation(out=latT[:].rearrange("p c n -> p (c n)"), in_=psb[1][:, 0:768], func=AF.Copy),
                         ["psb1"], ["latT"])
                    yield ("rel", "psb")
                    yield ("rel", "P")
                    for (pp, key, c0, cn) in ((psf[2], "psf2", 0, 512), (psf[3], "psf3", 512, 256)):
                        for k in range(4):
                            S.op("pe", lambda e, pp=pp, k=k, c0=c0, cn=cn: e.matmul(pp[:, 0:cn], latT[:, k, :], wqb[:, k, c0:c0 + cn],
                                                                                    start=(k == 0), stop=(k == 3)), ["latT", "wqb"], [key])
                    for (pp, key, c0) in ((psf[4], "psf4", 0), (psf[5], "psf5", 512)):
                        for k in range(2):
                            S.op("pe", lambda e, pp=pp, k=k, c0=c0: e.matmul(pp[:, :], latT[:, 4 + k, :], wkvb[:, k, c0:c0 + 512],
                                                                             start=(k == 0), stop=(k == 1)), ["latT", "wkvb"], [key])
                    yield
                    qf_b, kvf_b, qpn_b, Qtok_b, Ktok_b = qfs[b], kvfs[b], qpns[b], Qtoks[b], Ktoks[b]
                    pb = str(b)
                    S.op("act", lambda e: e.activation(out=qf_b[:, 0:512], in_=psf[2][:, :], func=AF.Copy), ["psf2"], ["qsrcA" + pb])
                    S.op("dve", lambda e: e.tensor_copy(qf_b[:, 512:768], psf[3][:, 0:256]), ["psf3"], ["qsrcB" + pb])
                    S.op("act", lambda e: e.activation(out=kvf_b[:, 0:512], in_=psf[4][:, :], func=AF.Copy), ["psf4"], ["knsrcA" + pb])
                    S.op("dve", lambda e: e.tensor_copy(kvf_b[:, 512:1024], psf[5][:, :]), ["psf5"], ["knsrcB" + pb])
                    yield ("rel", "Q")
                    q3 = qf_b.rearrange("p (h d) -> p h d", d=96)
                    kv3 = kvf_b.rearrange("p (h d) -> p h d", d=128)
                    qsk = ["qsrcA" + pb, "qsrcB" + pb]
                    ksk = ["knsrcA" + pb, "knsrcB" + pb]
                    sqb, ssb, rsb, rs2b = gst
                    sqA = sqb[:, 0:512].rearrange("p (h d) -> p h d", d=64)
                    sqB = sqb[:, 512:768].rearrange("p (h d) -> p h d", d=32)
                    sqC = sqb[:, 768:1280].rearrange("p (h d) -> p h d", d=64)
                    ksq = "sq" + pb
                    S.op("act", lambda e: e.activation(out=sqA, in_=q3[:, :, 0:64], func=AF.Square), qsk, [ksq + "A"])
                    yield
                    S.op("act", lambda e: e.activation(out=sqB, in_=q3[:, :, 64:96], func=AF.Square, scale=2.0 ** 0.5), qsk, [ksq + "B"])
                    yield
                    S.op("act", lambda e: e.activation(out=sqC, in_=kv3[:, :, 0:64], func=AF.Square), ksk, [ksq + "C"])
                    yield
                    S.op("dve", lambda e: e.tensor_reduce(out=ssb[:, 0:8], in_=sqA, axis=AX.X, op=ALU.add), [ksq + "A"], ["ss" + pb])
                    yield
                    S.op("dve", lambda e: e.tensor_reduce(out=ssb[:, 8:16], in_=sqB, axis=AX.X, op=ALU.add), [ksq + "B"], ["ss" + pb])
                    yield
                    S.op("dve", lambda e: e.tensor_reduce(out=ssb[:, 16:24], in_=sqC, axis=AX.X, op=ALU.add), [ksq + "C"], ["ss" + pb])
                    yield
                    S.op("act", lambda e: e.activation(out=rsb[:, 0:24], in_=ssb[:, 0:24], func=AF.Sqrt, scale=1.0 / 64, bias=epsb[:, 0:1]),
                         ["ss" + pb], ["rsg" + pb])
                    yield
                    S.op("dve", lambda e: e.tensor_tensor(sqA, q3[:, :, 0:64], gains[:, G_QN:G_QN + 64].unsqueeze(1).broadcast_to([128, 8, 64]), ALU.mult),
                         qsk + ["gains"], [ksq + "A"])
                    yield
                    S.op("dve", lambda e: e.tensor_tensor(sqB, q3[:, :, 64:96], gains[:, G_QP:G_QP + 32].unsqueeze(1).broadcast_to([128, 8, 32]), ALU.mult),
                         qsk + ["gains"], [ksq + "B"])
                    yield
                    S.op("dve", lambda e: e.tensor_tensor(sqC, kv3[:, :, 0:64], gains[:, G_KN:G_KN + 64].unsqueeze(1).broadcast_to([128, 8, 64]), ALU.mult),
                         ksk + ["gains"], [ksq + "C"])
                    yield
                    S.op("dve", lambda e: e.reciprocal(rs2b[:, 0:24], rsb[:, 0:24]), ["rsg" + pb], ["rs2g" + pb])
                    yield
                    S.op("dve", lambda e: e.tensor_tensor(qpn_b, sqB, rs2b[:, 8:16].unsqueeze(2).broadcast_to([128, 8, 32]), ALU.mult),
                         [ksq + "B", "rs2g" + pb, ksq], ["qp" + pb + "dst"])
                    yield
                    S.op("dve", lambda e: e.tensor_tensor(Qtok_b[:, :, 0:64], sqA, rs2b[:, 0:8].unsqueeze(2).broadcast_to([128, 8, 64]), ALU.mult),
                         [ksq + "A", "rs2g" + pb, ksq], ["q" + pb + "dst"])
                    yield
                    S.op("dve", lambda e: e.tensor_tensor(Ktok_b[:, :, 0:64], sqC, rs2b[:, 16:24].unsqueeze(2).broadcast_to([128, 8, 64]), ALU.mult),
                         [ksq + "C", "rs2g" + pb, ksq], ["kn" + pb + "dst"])
                    yield
                    yield from rope(qpn_b, 8, 16, cosT[:, t, 0:16], sinT[:, t, 0:16], Qtok_b[:, :, 64:96], ropt, "qp" + pb + "dst", "q" + pb + "dst", sfx=pb)
                    yield
                    S.op("dve", lambda e: e.tensor_copy(Ktok_b[:, :, 64:96], kper[b][:].unsqueeze(1).broadcast_to([128, 8, 32])),
                         [f"kper{b}"], ["kn" + pb + "dst"])
                    vsrc = kvf_b.rearrange("p (j e d) -> p j e d", e=2, d=128)[:, :, :, 64:128]
                    vdst = Vm[:, t, :, :].rearrange("p j (e d) -> p j e d", d=64)[:, :, 0:3:2, :]
                    S.op("pool", lambda e, vsrc=vsrc, vdst=vdst: e.tensor_copy(vdst, vsrc), ksk, ["V"])
                    yield
                    yield ("acq", "psb")
                    for h in range(8):
                        S.op("pe", lambda e, h=h: e.transpose(psb[0][0:96, h * 128:(h + 1) * 128], Qtok_b[:, h, :], ident[:]),
                             ["q" + pb + "dst", "ident"], ["psb0"])
                    S.op("act", lambda e, t=t: e.activation(out=QT[0:96, :, t * 128:(t + 1) * 128],
                                                            in_=psb[0][0:96, :].rearrange("p (h n) -> p h n", n=128), func=AF.Copy),
                         ["psb0"], ["QT"])
                    yield
                    for h in range(8):
                        S.op("pe", lambda e, h=h: e.transpose(psb[1][0:96, h * 128:(h + 1) * 128], Ktok_b[:, h, :], ident[:]),
                             ["kn" + pb + "dst", "ident"], ["psb1"])
                    S.op("dve", lambda e, t=t: e.tensor_copy(KT[0:96, :, t * 128:(t + 1) * 128],
                                                             psb[1][0:96, :].rearrange("p (h n) -> p h n", n=128)),
                         ["psb1"], ["KT"])
                    yield ("rel", "psb")

                R2f = R2[:, :].bitcast(F32)
                qfs = [R2f[:, p * 2048 + 0:p * 2048 + 768] for p in range(2)]
                kvfs = [R2f[:, p * 2048 + 768:p * 2048 + 1792] for p in range(2)]
                qpns = [R2f[:, p * 2048 + 1792:p * 2048 + 2048].rearrange("p (h d) -> p h d", d=32) for p in range(2)]
                Qtoks = [R2[:, 8192 + p * 1536:8192 + p * 1536 + 768].rearrange("p (h d) -> p h d", d=96) for p in range(2)]
                Ktoks = [R2[:, 8192 + p * 1536 + 768:8192 + (p + 1) * 1536].rearrange("p (h d) -> p h d", d=96) for p in range(2)]
                run_interleaved([(lambda t=t: tileM(t)) for t in range(NT)])
                tap("QT", QT[0:96, :, :].rearrange("p h n -> p (h n)"), [96, 8 * S_LEN], BF16)
                tap("KT", KT[0:96, :, :].rearrange("p h n -> p (h n)"), [96, 8 * S_LEN], BF16)
                tap("Vm", Vm.rearrange("p t j c -> p (t j c)"), [128, 16 * 768], BF16)
                S.barrier()

        if stop_after not in ("0", "AM"):
            with ExitStack() as L2:
                winD = sb(L2, "winD", [128, 8, 1536], BF16)
                Pt = [sb(L2, f"Pt{i}", [128, 512], BF16) for i in range(4)]
                rec = [sb(L2, f"rec{i}", [128, 512], F32) for i in range(2)]
                bcs = sb(L2, "bcs", [128, 512], F32)
                S.dma("pool", winD[:], winD_d, [], ["winD"], "winD")
                with ExitStack() as Lmod:
                    wa2 = [sb(Lmod, f"wa2_{i}", [128, 8, 512], BF16) for i in range(2)]
                    badag2 = sb(Lmod, "badag2", [128, 2048], F32)
                    cc2 = sb(Lmod, "cc2", [128, 8], F32)
                    scb2 = sb(Lmod, "scb2", [128, 8], BF16)
                    screp2 = sb(Lmod, "screp2", [128, 8, 128], BF16)
                    badac2 = sb(Lmod, "badac2", [128, 48], F32)
                    gcols2 = sb(Lmod, "gcols2", [128, 16], F32)
                    modc2 = sb(Lmod, "modc2", [128, 48], F32)
                    pcol2 = psb[0][:, :].bitcast(F32)
                    pg2 = psb[1][:, :].bitcast(F32)

                    def side_mod():
                        S.dma("sp", cc2[:], ccol_d, [], ["cc2"], "m0")
                        S.dma("sp", badac2[:], badac_d, [], ["badac2"], "m1")
                        S.dma("sp", gcols2[:], gcols_d, [], ["gcols2"], "m2")
                        S.dma("sp", badag2[:], badag_d.partition_broadcast(128), [], ["badag"], "m3")
                        for n in (4, 5):
                            S.dma("pool", wa2[n % 2][:], wada_d[:, :, n * 512:(n + 1) * 512], [], [f"wa2{n % 2}"], f"wa2{n % 2}")
                        for _ in range(12):
                            yield
                        S.op("act", lambda e: e.activation(out=scb2[:], in_=cc2[:], func=AF.Silu), ["cc2"], ["scb"])
                        yield
                        S.op("dve", lambda e: e.tensor_copy(screp2[:], scb2[:].unsqueeze(2).broadcast_to([128, 8, 128])),
                             ["scb"], ["screp"])
                        for _ in range(4):
                            yield
                        yield from mod_chunks([4, 5, 10, 11, 6, 7, 8, 9], wa2, ["wa20", "wa21"], scb2, screp2, badag2, pg2, "psb1", pg2, "psb1", 11)
                        S.op("dve", lambda e: e.tensor_tensor(modc2[:, 24:40], pg2[:, 24:40], badac2[:, 24:40], ALU.add),
                             ["psb1", "badac2"], ["modc2"])
                        yield
                        S.op("dve", lambda e: e.scalar_tensor_tensor(out=gv2[:], in0=modc2[:, 32:40], scalar=1.0, in1=gcols2[:, 8:16],
                                                                    op0=ALU.add, op1=ALU.mult), ["modc2", "gcols2"], ["gv2"])
                        S.op("dve", lambda e: e.tensor_copy(sh2[:], modc2[:, 24:32]), ["modc2"], ["sh2"])

                    sidegen = side_mod()
                    attention(lambda h: QT[0:96, h, :], lambda h: KT[0:96, h, :],
                              lambda h, c: Vm[:, c, h // 2, (h % 2) * 64:(h % 2) * 64 + 128],
                              96 ** -0.5, False, 0, Pt, rec, bcs, side=sidegen, LA=3,
                              sbanks=[(psf[0], "psf0"), (psf[1], "psf1"), (psf[2], "psf2"), (psb[0][:, :].bitcast(F32), "psb0")])
                    for _ in sidegen:
                        pass
                    tap("g12", g12[:], [128, 2048])
                tap("mixM", mixT[:, 0:4, :].rearrange("p c n -> p (c n)"), [128, 4 * S_LEN], BF16)
                S.barrier()
                if stop_after != "M":
                    with ExitStack() as L3:
                        xt = [sb(L3, f"xtd{i}", [128, D], F32) for i in range(2)]
                        xn = sb(L3, "xnd", [128, D], BF16)
                        hT = [sb(L3, f"hTd{i}", [128, 8, 128], BF16) for i in range(2)]
                        st_ssq = sb(L3, "std_ssq", [128, 4], F32)
                        st_rs = sb(L3, "std_rs", [128, 4], F32)
                        junkd = sb(L3, "junkd", [128, D], BF16)
                        sqA = sb(L3, "sqA", [128, 1024], F32)
                        ssA = sb(L3, "ssA", [128, 16], F32)
                        rsA = sb(L3, "rsA", [128, 16], F32)
                        rs2A = sb(L3, "rs2A", [128, 16], F32)
                        ropt2 = [sb(L3, f"ropt2{i}", [128, 512], F32) for i in range(2)]
                        tokA = sb(L3, "tokA", [128, 1024], BF16)
                        R2f = R2[:, :].bitcast(F32)
                        qk_f = R2f[:, 4096:5120]
                        qk_n = R2f[:, 5120:6144]
                        S.op("pool", lambda e: e.memset(QdT[64:128, :, :], 0.0), [], ["QT"])
                        S.op("pool", lambda e: e.memset(QdT1[0:64, :, :], 0.0), [], ["QT"])
                        S.dma("sp", xt[0][:], x_d[0:128, :], [], ["xt0"], "xt0")

                        def tileD(t):
                            b = t % 2
                            yield ("acq", "X")
                            if t + 1 < NT:
                                S.dma("sp", xt[1 - b][:], x_d[(t + 1) * 128:(t + 2) * 128, :], [], [f"xt{1 - b}"], f"xt{1 - b}")
                            S.op("act", lambda e: e.activation(out=junkd[:], in_=xt[b][:], func=AF.Square,
                                                               accum_out=st_ssq[:, 0:1]), [f"xt{b}"], ["junkd", "ssq0"])
                            rstd_from_ssq(None, st_ssq[:, 0:1], st_rs[:, 0:1], D, "ssq0", "rs0", st_ssq[:, 1:2], "ssq0b")
                            yield
                            yield ("acq", "psb")
                            norm_transpose(xt[b][:], f"xt{b}", st_rs[:, 0:1], "rs0", gv1, sh1, xn, hT[b], f"hT{b}", psb[0], "psb0")
                            yield ("rel", "psb")
                            yield ("rel", "X")
                            yield "spawn"
                            yield ("acq", "P")
                            for (pp, key, c0) in ((psf[0], "psf0", 0), (psf[1], "psf1", 512), (psf[2], "psf2", 1024)):
                                for k in range(8):
                                    S.op("pe", lambda e, pp=pp, k=k, c0=c0: e.matmul(pp[:, :], hT[b][:, k, :], winD[:, k, c0:c0 + 512],
                                                                                   start=(k == 0), stop=(k == 7)),
                                         [f"hT{b}_{k}", "winD"], [key])
                                yield
                            yield ("acq", "N")
                            S.op("act", lambda e: e.activation(out=qk_f[:, 0:512], in_=psf[0][:, :], func=AF.Copy), ["psf0"], ["qkfA"])
                            S.op("dve", lambda e: e.tensor_copy(qk_f[:, 512:1024], psf[1][:, :]), ["psf1"], ["qkfB"])
                            vsrc = psf[2][:, :].rearrange("p (j e d) -> p j e d", e=2, d=64)
                            vdst = Vd[:, t, :, :].rearrange("p j (e d) -> p j e d", d=64)[:, :, 0:3:2, :]
                            S.op("act", lambda e: e.activation(out=vdst, in_=vsrc, func=AF.Copy), ["psf2"], ["V"])
                            yield ("rel", "P")
                            src3 = qk_f.rearrange("p (h d) -> p h d", d=64)
                            src4 = qk_f.rearrange("p (s h d) -> p s h d", s=2, h=8)
                            sq3 = sqA[:, :].rearrange("p (h d) -> p h d", d=64)
                            sq4 = sqA[:, :].rearrange("p (s h d) -> p s h d", s=2, h=8)
                            dst3 = qk_n.rearrange("p (h d) -> p h d", d=64)
                            g4 = gains[:, G_DQ:G_DQ + 128].rearrange("p (s d) -> p s d", d=64).unsqueeze(2).broadcast_to([128, 2, 8, 64])
                            S.op("act", lambda e: e.activation(out=sq3, in_=src3, func=AF.Square), ["qkfA", "qkfB"], ["sqA"])
                            yield
                            S.op("dve", lambda e: e.tensor_reduce(out=ssA[:, :], in_=sq3, axis=AX.X, op=ALU.add), ["sqA"], ["ssA"])
                            yield
                            S.op("act", lambda e: e.activation(out=rsA[:, :], in_=ssA[:, :], func=AF.Sqrt, scale=1.0 / 64, bias=epsb[:, 0:1]), ["ssA"], ["rsA"])
                            yield
                            S.op("dve", lambda e: e.reciprocal(rs2A[:, :], rsA[:, :]), ["rsA"], ["rs2A"])
                            yield
                            S.op("dve", lambda e: e.tensor_tensor(sq4, src4, g4, ALU.mult), ["qkfA", "qkfB", "gains"], ["sqA"])
                            yield
                            S.op("dve", lambda e: e.tensor_tensor(dst3, sq3, rs2A[:, :].unsqueeze(2).broadcast_to([128, 16, 64]), ALU.mult),
                                 ["sqA", "rs2A"], ["qkn"])
                            yield
                            yield from rope(dst3, 16, 32, cos_ap=cosT[:, t, 16:48], sin_ap=sinT[:, t, 16:48],
                                            dst3=tokA[:, :].rearrange("p (h d) -> p h d", d=64), tmp=ropt2, key_src="qkn", key_dst="tokA", sfx="D")
                            yield
                            yield ("acq", "psb")
                            for j in range(4):
                                S.op("pe", lambda e, j=j: e.transpose(psb[0][:, j * 128:(j + 1) * 128], tokA[:, j * 128:(j + 1) * 128], ident[:]),
                                     ["tokA", "ident"], ["psb0"])
                            for j in range(4):
                                S.op("pe", lambda e, j=j: e.transpose(psb[1][:, j * 128:(j + 1) * 128], tokA[:, 512 + j * 128:512 + (j + 1) * 128], ident[:]),
                                     ["tokA", "ident"], ["psb1"])
                            yield
                            S.op("act", lambda e: e.activation(
                                out=QdT[0:64, :, t * 128:(t + 1) * 128],
                                in_=psb[0][0:64, 0:512].rearrange("p (j n) -> p j n", n=128), func=AF.Copy), ["psb0"], ["QT"])
                            S.op("act", lambda e: e.activation(
                                out=QdT1[64:128, :, t * 128:(t + 1) * 128],
                                in_=psb[0][64:128, 0:512].rearrange("p (j n) -> p j n", n=128), func=AF.Copy), ["psb0"], ["QT"])
                            S.op("dve", lambda e: e.tensor_copy(
                                KdT[:, :, t * 128:(t + 1) * 128],
                                psb[1][:, 0:512].rearrange("p (j n) -> p j n", n=128)), ["psb1"], ["KT"])
                            yield ("rel", "psb")
                            yield ("rel", "N")

                        run_interleaved([(lambda t=t: tileD(t)) for t in range(NT)])
                        tap("QdT", QdT.rearrange("p j n -> p (j n)"), [128, 4 * S_LEN], BF16)
                        tap("KdT", KdT.rearrange("p j n -> p (j n)"), [128, 4 * S_LEN], BF16)
                        tap("Vd", Vd.rearrange("p t j c -> p (t j c)"), [128, 16 * 768], BF16)
                        S.barrier()
                    attention(lambda h: (QdT if h % 2 == 0 else QdT1)[:, h // 2, :],
                              lambda h: KdT[:, h // 2, :],
                              lambda h, c: Vd[:, c, h // 2, (h % 2) * 64:(h % 2) * 64 + 128],
                              64 ** -0.5, True, 4, Pt, rec, bcs, LA=3,
                              sbanks=[(psf[0], "psf0"), (psf[1], "psf1"), (psf[2], "psf2"), (psb[0][:, :].bitcast(F32), "psb0")])
                    tap("mixT", mixT.rearrange("p c n -> p (c n)"), [128, 8 * S_LEN], BF16)
                    S.barrier()

        if stop_after in ("O", "F"):
            x1 = R1[:, 0:32768].bitcast(F32).rearrange("p (t n) -> p t n", n=D)
            h2T = R1[:, 32768:40960].rearrange("p (c n) -> p c n", n=1024)
            wub = [R1[:, 40960 + i * 2048:40960 + (i + 1) * 2048].rearrange("p (k g c) -> p k g c", g=2, c=128) for i in range(2)]
            wub += [R2[:, 11264 + i * 2048:11264 + (i + 1) * 2048].rearrange("p (k g c) -> p k g c", g=2, c=128) for i in range(2)]
            NWB = 4

            def h2_norm(t, junk_ap, xnb, xkey, ssq, rs):
                S.op("act", lambda e: e.activation(out=junk_ap, in_=x1[:, t, :], func=AF.Square, accum_out=ssq[:, 0:1]),
                     [f"x1_{t}"], ["junkh", "sg", "ssq0"])
                rstd_from_ssq(None, ssq[:, 0:1], rs[:, 0:1], D, "ssq0", "rs0", ssq[:, 1:2], "ssq0b")
                S.op("dve", lambda e: e.tensor_scalar(xnb[:], x1[:, t, :], rs[:, 0:1], None, ALU.mult), [f"x1_{t}", "rs0"], [xkey])

            def h2_trans(t, xnb, xkey):
                tt = t % 8
                hdst = h2T[:, :, tt * 128:(tt + 1) * 128]
                for c in range(8):
                    S.op("pe", lambda e, c=c: e.transpose(psb[c // 4][:, (c % 4) * 128:(c % 4 + 1) * 128],
                                                          xnb[:, c * 128:(c + 1) * 128], ident[:]), [xkey, "ident"], [f"psb{c // 4}"])
                for cc in range(4):
                    S.op("act", lambda e, c=cc: e.activation(out=hdst[:, c, :], in_=psb[0][:, (c % 4) * 128:(c % 4 + 1) * 128],
                                                             func=AF.Identity, scale=gv2[:, c:c + 1], bias=sh2[:, c:c + 1]),
                         ["psb0"], [f"h2T_{cc}"])
                    S.op("dve", lambda e, c=4 + cc: e.tensor_scalar(hdst[:, c, :], psb[1][:, (c % 4) * 128:(c % 4 + 1) * 128],
                                                                    gv2[:, c:c + 1], sh2[:, c:c + 1], ALU.mult, ALU.add),
                         ["psb1"], [f"h2T_{4 + cc}"])
            with ExitStack() as L2:
                wo = sb(L2, "wo", [128, 8, 1024], BF16)
                xt = [sb(L2, f"xto{i}", [128, D], F32) for i in range(2)]
                tmpo = sb(L2, "tmpo", [128, 512], F32)
                junk_o = sb(L2, "junko", [128, D], BF16)
                xn_o = [sb(L2, f"xno{i}", [128, D], BF16) for i in range(3)]
                sso = sb(L2, "sso", [128, 4], F32)
                rso = sb(L2, "rso", [128, 4], F32)
                S.dma("pool", wo[:], wo_d, [], ["wo"], "wo")
                S.dma("sp", xt[0][:], x_d[0:128, :], [], ["xt0"], "xt0")
                for t in range(NT):
                    b = t % 2
                    if t + 1 < NT:
                        S.dma("sp", xt[1 - b][:], x_d[(t + 1) * 128:(t + 2) * 128, :], [], [f"xt{1 - b}"], f"xt{1 - b}")
                    for nh in range(2):
                        pp = psf[(2 * t + nh) % 4]
                        key = f"psf{(2 * t + nh) % 4}"
                        for k in range(8):
                            S.op("pe", lambda e, pp=pp, k=k, t=t, nh=nh: e.matmul(pp[:, :], mixT[:, k, t * 128:(t + 1) * 128],
                                                                                  wo[:, k, nh * 512:(nh + 1) * 512],
                                                                                  start=(k == 0), stop=(k == 7)), ["mixT", "wo"], [key])
                        S.op("dve", lambda e, pp=pp, nh=nh: e.tensor_tensor(tmpo[:], pp[:, :], g12[:, nh * 512:(nh + 1) * 512], ALU.mult),
                             [key, "g12"], ["tmpo"])
                        S.op("pool", lambda e, t=t, nh=nh, b=b: e.tensor_tensor(x1[:, t, nh * 512:(nh + 1) * 512], tmpo[:],
                                                                               xt[b][:, nh * 512:(nh + 1) * 512], ALU.add),
                             ["tmpo", f"xt{b}"], [f"x1_{t}"])
                    if t < 8:
                        h2_norm(t, junk_o[:], xn_o[t % 3], f"xno{t % 3}", sso, rso)
                    if 2 <= t <= 9:
                        h2_trans(t - 2, xn_o[(t - 2) % 3], f"xno{(t - 2) % 3}")
                    if t == 10:
                        for i in range(2):
                            S.dma("pool", wub[i], wup_d[i], [], [f"wub{i}"], f"wub{i}")
                tap("x1", R1[:, 0:32768].bitcast(F32), [128, 16 * D])
                S.barrier()

        if stop_after == "F":
            aT = R2[:, 0:11264].rearrange("p (j n) -> p j n", n=1024)
            with ExitStack() as L2:
                wdn = sb(L2, "wdn", [128, 11, 1024], BF16)
                xn_f = [sb(L2, f"xnf{i}", [128, D], BF16) for i in range(2)]
                st_ssq = sb(L2, "stf_ssq", [128, 4], F32)
                st_rs = sb(L2, "stf_rs", [128, 4], F32)
                ug = sb(L2, "ug", [128, 1026], F32)
                uv = sb(L2, "uv", [128, 1026], F32)
                yg = sb(L2, "yg", [128, 1024], F32)
                yv = sb(L2, "yv", [128, 1024], F32)
                sg = sb(L2, "sg", [128, 1024], F32)
                halo = sb(L2, "halo", [128, 2 * NJ, 2], F32)
                tmpf = sb(L2, "tmpf", [128, 512], F32)
                S.dma("pool", wub[2], wup_d[2], [], ["wub2"], "wub2")
                S.dma("pool", wdn[:], wdn_d[:, 0:11, :], [], ["wdn"], "wdn")
                junk_f = sg[:, 0:512].bitcast(BF16)
                for H in range(2):
                    for JG in range(2):
                        for jj in range(11):
                            j = JG * 11 + jj
                            seq = (H * 2 + JG) * 11 + jj
                            wb = wub[seq % NWB]
                            wkey = f"wub{seq % NWB}"
                            if seq + NWB - 1 < 44:
                                nk = f"wub{(seq + NWB - 1) % NWB}"
                                S.dma("pool", wub[(seq + NWB - 1) % NWB], wup_d[(seq + NWB - 1) % 22], [], [nk], nk)
                            banks = {}
                            for tb in range(2):
                                for g_ in range(2):
                                    bi = (4 * seq + 2 * tb + g_) % 6
                                    banks[(tb, g_)] = bi
                                    for k in range(8):
                                        S.op("pe", lambda e, k=k, g_=g_, tb=tb, bi=bi: e.matmul(
                                            psf[bi][:, :], wb[:, k, g_, :], h2T[:, k, tb * 512:(tb + 1) * 512],
                                            start=(k == 0), stop=(k == 7)), [wkey, f"h2T_{k}"], [f"psf{bi}"])
                            for (g_, usb, ukey, ysb, ykey) in ((0, ug, "ug", yg, "yg"), (1, uv, "uv", yv, "yv")):
                                fc = j + g_ * NJ
                                if H == 0:
                                    S.op("pool", lambda e: e.memset(usb[:, 0:2], 0.0), [], [ukey])
                                else:
                                    S.op("pool", lambda e: e.tensor_copy(usb[:, 0:2], halo[:, fc, :]), [f"halo{fc}"], [ukey])
                                for tb in range(2):
                                    bi = banks[(tb, g_)]
                                    S.op("act", lambda e, tb=tb, bi=bi: e.activation(out=usb[:, 2 + tb * 512:514 + tb * 512], in_=psf[bi][:, :],
                                                                                   func=AF.Copy), [f"psf{bi}"], [ukey])
                                    S.op("act", lambda e, tb=tb, bi=bi: e.activation(
                                        out=ysb[:, tb * 512:(tb + 1) * 512], in_=psf[bi][:, :], func=AF.Identity,
                                        scale=convc[:, 2, fc:fc + 1], bias=convc[:, 3, fc:fc + 1]), [f"psf{bi}", "convc"], [ykey])
                                if H == 0:
                                    S.op("pool", lambda e: e.tensor_copy(halo[:, fc, :], usb[:, 1024:1026]), [ukey], [f"halo{fc}"])
                                S.op("dve", lambda e: e.scalar_tensor_tensor(
                                    out=ysb[:], in0=usb[:, 1:1025], scalar=convc[:, 1, fc:fc + 1], in1=ysb[:], op0=ALU.mult, op1=ALU.add),
                                    [ukey, ykey, "convc"], [ykey])
                                S.op("dve", lambda e: e.scalar_tensor_tensor(
                                    out=ysb[:], in0=usb[:, 0:1024], scalar=convc[:, 0, fc:fc + 1], in1=ysb[:], op0=ALU.mult, op1=ALU.add),
                                    [ukey, ykey, "convc"], [ykey])
                            S.op("act", lambda e: e.activation(out=sg[:], in_=yg[:], func=AF.Silu), ["yg"], ["sg"])
                            S.op("dve", lambda e: e.tensor_tensor(aT[:, jj, :], sg[:], yv[:], ALU.mult), ["sg", "yv"], ["aT"])
                        for tt in range(8):
                            t = H * 8 + tt
                            for nh in range(2):
                                pp = psf[4 + ((2 * tt + nh) % 2)]
                                pkey = f"psf{4 + ((2 * tt + nh) % 2)}"
                                for jj in range(11):
                                    S.op("pe", lambda e, pp=pp, jj=jj, tt=tt, nh=nh, JG=JG: e.matmul(
                                        pp[:, :], aT[:, jj, tt * 128:(tt + 1) * 128], wdn[:, jj, nh * 512:(nh + 1) * 512],
                                        start=(jj == 0), stop=(jj == 10)), ["aT", "wdn"], [pkey])
                                S.op("dve", lambda e, pp=pp, nh=nh: e.tensor_tensor(tmpf[:], pp[:, :], g12[:, 1024 + nh * 512:1024 + (nh + 1) * 512],
                                                                                  ALU.mult), [pkey, "g12"], ["tmpf"])
                                S.op("pool", lambda e, t=t, nh=nh: e.tensor_tensor(x1[:, t, nh * 512:(nh + 1) * 512], tmpf[:],
                                                                                 x1[:, t, nh * 512:(nh + 1) * 512], ALU.add),
                                     ["tmpf", f"x1_{t}"], [f"x1_{t}"])
                            if JG == 1:
                                S.dma("sp", out_d[t * 128:(t + 1) * 128, :], x1[:, t, :], [f"x1_{t}"], [], "outd")
                            if H == 0 and JG == 1:
                                h2_norm(8 + tt, junk_f, xn_f[tt % 2], f"xnf{tt % 2}", st_ssq, st_rs)
                                if tt >= 1:
                                    h2_trans(8 + tt - 1, xn_f[(tt - 1) % 2], f"xnf{(tt - 1) % 2}")
                        if H == 0 and JG == 1:
                            h2_trans(15, xn_f[1], "xnf1")
                        if not (H == 1 and JG == 1):
                            nJG = 1 - JG
                            S.dma("pool", wdn[:], wdn_d[:, nJG * 11:(nJG + 1) * 11, :], ["dummy"], ["wdn"], "wdn")
        else:
            with ExitStack() as L2:
                z = sb(L2, "zout", [128, D], F32)
                S.op("dve", lambda e: e.memset(z[:], 0.0), [], ["z"])
                for t in range(NT):
                    S.dma("sp", out_d[t * 128:(t + 1) * 128, :], z[:], ["z"], [], "outd")
        S.finish()
    return nc, list(tap_d.keys())


def _host_constants():
    ki = np.arange(128)[:, None]
    col = np.arange(16 * 128)[None, :]
    dist = col - ki
    cnt = ((dist >= 0) & (dist <= 128)).astype(np.float32)
    cnt += ((dist >= 0) & (dist <= 512) & (dist % 4 == 0)).astype(np.float32)
    cnt += ((dist >= 0) & (dist <= 2048) & (dist % 16 == 0)).astype(np.float32)
    caus = (np.arange(128)[None, :] >= ki).astype(np.float32)
    masks = np.concatenate([cnt, caus], axis=1).astype(np.float32)
    inv_m = np.power(np.float32(10000.0), (-2.0 * np.arange(16, dtype=np.float32) / np.float32(32))).astype(np.float32)
    inv_d = np.power(np.float32(10000.0), (-2.0 * np.arange(32, dtype=np.float32) / np.float32(64))).astype(np.float32)
    invf = np.concatenate([inv_m, inv_d])[None, :].astype(np.float32)
    return masks, invf


def _prep_inputs(inp):
    f = lambda a: np.ascontiguousarray(np.asarray(a))
    masks, invf = _host_constants()
    w_ada = f(inp["w_ada"])[0]
    b_ada = f(inp["b_ada"])[0]
    w_in = f(inp["w_in"])[0]
    shared = {
        "w_ada": f(w_ada.reshape(8, 128, 6 * D).transpose(1, 0, 2)),
        "b_ada_col": f(b_ada.reshape(48, 128).T),
        "b_ada_g": f(np.concatenate([b_ada[2048:3072], b_ada[5120:6144]])[None, :]),
        "gcols": f(np.concatenate([f(inp["g_mix_norm"])[0].reshape(8, 128).T, f(inp["g_ffn_norm"])[0].reshape(8, 128).T], axis=1)),
        "gains": f(np.concatenate([f(inp["g_q_lat"])[0], f(inp["g_kv_lat"])[0], f(inp["g_mla_q_nope"])[0], f(inp["g_mla_q_pe"])[0],
                                   f(inp["g_mla_k_nope"])[0], f(inp["g_mla_k_pe"])[0], f(inp["g_dil_q"])[0], f(inp["g_dil_k"])[0]])[None, :]),
        "invf": invf,
        "w_inM": f(w_in[:, 0:800].reshape(8, 128, 800).transpose(1, 0, 2)),
        "w_inD": f(w_in[:, 800:2336].reshape(8, 128, 1536).transpose(1, 0, 2)),
        "w_qb": f(f(inp["w_q_b"])[0].reshape(4, 128, 768).transpose(1, 0, 2)),
        "w_kvb": f(f(inp["w_kv_b"])[0].reshape(2, 128, 1024).transpose(1, 0, 2)),
        "w_o": f(f(inp["w_o"])[0].reshape(8, 128, 1024).transpose(1, 0, 2)),
        "w_up": f(f(inp["w_up"])[0].reshape(8, 128, 2, NJ, 128).transpose(3, 1, 0, 2, 4)),
        "w_down": f(f(inp["w_down"])[0].reshape(NJ, 128, 1024).transpose(1, 0, 2)),
        "convcol": f(np.concatenate([f(inp["w_conv"])[0], f(inp["b_conv"])], axis=0).reshape(4, 2 * NJ, 128).transpose(2, 0, 1)),
        "masks": masks,
    }
    shared = {k: v.astype(np.float32) for k, v in shared.items()}
    x = f(inp["x"]); c = f(inp["c"]); pos = f(inp["positions"])
    maps = []
    for b in range(8):
        m = dict(shared)
        m["x"] = f(x[b]).astype(np.float32)
        m["ccol"] = f(c[b].reshape(8, 128).T).astype(np.float32)
        m["pos"] = f(pos[b].reshape(NT, 128).T).astype(np.int32)
        maps.append(m)
    return maps


_CACHE = {}


def kernel(**inputs):
    maps = _prep_inputs(inputs)
    if "nc" not in _CACHE:
        _CACHE["nc"] = build_program("F")[0]
    res = run_bass_kernel_spmd(_CACHE["nc"], maps, core_ids=list(range(8)))
    out = np.stack([np.asarray(r["out"]).reshape(S_LEN, D) for r in res.results], axis=0)
    return out.astype(np.float32)
```
